# Optimizing a Trainium2 kernel written in Bass

```python
import math
import jax, jax.numpy as jnp
from jax import lax
import numpy as np

D_MODEL = 2048
BATCH = 2
SEQ = 16384
DEPTH = 4

N_MEM = 256
N_MIXERS = 2
N_DIFF_LAYERS = (DEPTH + 1) // 2
N_MLA_LAYERS = DEPTH // 2
ROPE_THETA = 500000.0
Q_BLOCK = 128
EPS = 1e-6
DA_HEAD_DIM = 128
DA_HEADS = D_MODEL // (2 * DA_HEAD_DIM)
DA_ROT = DA_HEAD_DIM // 4
MLA_NOPE = 128
MLA_ROPE = 64
MLA_V = 128
MLA_HEADS = D_MODEL // MLA_V
MLA_Q_LORA = D_MODEL // 4
MLA_KV_LORA = D_MODEL // 4
CA_HEADS = 4
CA_HEAD_DIM = 128
D_FF = 4 * D_MODEL

kernel_name = "hybrid_diffattn_mla_memxattn_sqrelu"


def rms_norm(x, g):
    xf = x.astype(jnp.float32)
    y = xf * lax.rsqrt(jnp.mean(xf * xf, axis=-1, keepdims=True) + EPS)
    return (y * g.astype(jnp.float32)).astype(x.dtype)


def rope_angles(positions, rot_dim):
    inv_freq = ROPE_THETA ** (-jnp.arange(0, rot_dim, 2, dtype=jnp.float32) / rot_dim)
    ang = positions.astype(jnp.float32)[..., None] * inv_freq
    return jnp.cos(ang)[:, :, None, :], jnp.sin(ang)[:, :, None, :]


def apply_rope(x, cos, sin):
    xf = x.astype(jnp.float32)
    x1, x2 = jnp.split(xf, 2, axis=-1)
    return jnp.concatenate([x1 * cos - x2 * sin, x2 * cos + x1 * sin], axis=-1).astype(x.dtype)


def partial_rope(x, cos, sin, rot):
    return jnp.concatenate([apply_rope(x[..., :rot], cos, sin), x[..., rot:]], axis=-1)


def causal_mask(q_start, s_len):
    qpos = q_start + jnp.arange(Q_BLOCK)
    kpos = jnp.arange(s_len)
    return kpos[None, :] <= qpos[:, None]


def causal_block_sweep(block_fn, q):
    B, S = q.shape[:2]
    nb = S // Q_BLOCK
    qb = jnp.moveaxis(q.reshape((B, nb, Q_BLOCK) + q.shape[2:]), 1, 0)
    out = lax.map(lambda a: block_fn(a[0], a[1]), (qb, jnp.arange(nb) * Q_BLOCK))
    return jnp.moveaxis(out, 0, 1).reshape((B, S) + out.shape[3:])


def diff_attention(h, wqkv, lam_vecs, subln, wo, cos, sin, lambda_init):
    B, S, _ = h.shape
    q, k, v = jnp.split(h @ wqkv, 3, axis=-1)
    q = partial_rope(q.reshape(B, S, 2 * DA_HEADS, DA_HEAD_DIM), cos, sin, DA_ROT)
    k = partial_rope(k.reshape(B, S, 2 * DA_HEADS, DA_HEAD_DIM), cos, sin, DA_ROT)
    q = q.reshape(B, S, DA_HEADS, 2, DA_HEAD_DIM)
    k = k.reshape(B, S, DA_HEADS, 2, DA_HEAD_DIM)
    v = v.reshape(B, S, DA_HEADS, 2 * DA_HEAD_DIM)
    lv = lam_vecs.astype(jnp.float32)
    lam = jnp.exp(jnp.sum(lv[0] * lv[1])) - jnp.exp(jnp.sum(lv[2] * lv[3])) + lambda_init
    scale = 1.0 / math.sqrt(DA_HEAD_DIM)

    def block(qi, q0):
        s = jnp.einsum('bqhcd,bkhcd->bhcqk', qi, k).astype(jnp.float32) * scale
        s = jnp.where(causal_mask(q0, S), s, -jnp.inf)
        p = jax.nn.softmax(s, axis=-1)
        a = (p[:, :, 0] - lam * p[:, :, 1]).astype(v.dtype)
        return jnp.einsum('bhqk,bkhd->bqhd', a, v)

    o = causal_block_sweep(block, q)
    o = rms_norm(o, subln) * (1.0 - lambda_init)
    return o.reshape(B, S, D_MODEL) @ wo


def mla_attention(h, wdown, q_norm, kv_norm, wuq, wukv, wo, cos, sin):
    B, S, _ = h.shape
    d = h @ wdown
    c_q = rms_norm(d[..., :MLA_Q_LORA], q_norm)
    c_kv = rms_norm(d[..., MLA_Q_LORA:MLA_Q_LORA + MLA_KV_LORA], kv_norm)
    k_rope = apply_rope(d[..., MLA_Q_LORA + MLA_KV_LORA:][:, :, None, :], cos, sin)
    q = (c_q @ wuq).reshape(B, S, MLA_HEADS, MLA_NOPE + MLA_ROPE)
    q = jnp.concatenate([q[..., :MLA_NOPE], apply_rope(q[..., MLA_NOPE:], cos, sin)], axis=-1)
    kv = (c_kv @ wukv).reshape(B, S, MLA_HEADS, MLA_NOPE + MLA_V)
    k = jnp.concatenate([kv[..., :MLA_NOPE],
                         jnp.broadcast_to(k_rope, (B, S, MLA_HEADS, MLA_ROPE))], axis=-1)
    v = kv[..., MLA_NOPE:]
    scale = 1.0 / math.sqrt(MLA_NOPE + MLA_ROPE)

    def block(qi, q0):
        s = jnp.einsum('bqhd,bkhd->bhqk', qi, k).astype(jnp.float32) * scale
        s = jnp.where(causal_mask(q0, S), s, -jnp.inf)
        p = jax.nn.softmax(s, axis=-1).astype(v.dtype)
        return jnp.einsum('bhqk,bkhd->bqhd', p, v)

    o = causal_block_sweep(block, q)
    return o.reshape(B, S, MLA_HEADS * MLA_V) @ wo


def memory_cross_attention(h, mem_h, wq, wkv, wo):
    B, S, _ = h.shape
    M = mem_h.shape[1]
    q = (h @ wq).reshape(B, S, CA_HEADS, CA_HEAD_DIM)
    k, v = jnp.split(mem_h @ wkv, 2, axis=-1)
    k = k.reshape(B, M, CA_HEADS, CA_HEAD_DIM)
    v = v.reshape(B, M, CA_HEADS, CA_HEAD_DIM)
    s = jnp.einsum('bqhd,bmhd->bhqm', q, k).astype(jnp.float32) * (1.0 / math.sqrt(CA_HEAD_DIM))
    p = jax.nn.softmax(s, axis=-1).astype(v.dtype)
    o = jnp.einsum('bhqm,bmhd->bqhd', p, v)
    return o.reshape(B, S, CA_HEADS * CA_HEAD_DIM) @ wo


def sq_relu_mlp(h, wup, wdown):
    return jnp.square(jax.nn.relu(h @ wup)) @ wdown


def _normal(k, shape, scale):
    return jax.random.normal(k, shape, jnp.float32) * scale


def _gain(k, shape):
    return 1.0 + 0.02 * jax.random.normal(k, shape, jnp.float32)


def setup_inputs(seed: int = 0) -> dict:
    key = jax.random.key(seed)
    ks = jax.random.split(key, 22)
    D = D_MODEL
    return {
        "x": _normal(ks[0], (BATCH, SEQ, D), 1.0),
        "mem": _normal(ks[1], (BATCH, N_MEM, D), 1.0),
        "positions": jnp.broadcast_to(jnp.arange(SEQ, dtype=jnp.int32), (BATCH, SEQ)),
        "attn_norm": _gain(ks[2], (DEPTH, D)),
        "cross_norm": _gain(ks[3], (DEPTH, D)),
        "mlp_norm": _gain(ks[4], (DEPTH, D)),
        "mem_norm": _gain(ks[5], (D,)),
        "final_norm": _gain(ks[6], (D,)),
        "da_wqkv": _normal(ks[7], (N_DIFF_LAYERS, D, 3 * D), D ** -0.5),
        "da_lambda": _normal(ks[8], (N_DIFF_LAYERS, 4, DA_HEAD_DIM), 0.1),
        "da_subln": _gain(ks[9], (N_DIFF_LAYERS, 2 * DA_HEAD_DIM)),
        "da_wo": _normal(ks[10], (N_DIFF_LAYERS, D, D), D ** -0.5),
        "mla_wdown": _normal(ks[11], (N_MLA_LAYERS, D, MLA_Q_LORA + MLA_KV_LORA + MLA_ROPE), D ** -0.5),
        "mla_q_norm": _gain(ks[12], (N_MLA_LAYERS, MLA_Q_LORA)),
        "mla_kv_norm": _gain(ks[13], (N_MLA_LAYERS, MLA_KV_LORA)),
        "mla_wuq": _normal(ks[14], (N_MLA_LAYERS, MLA_Q_LORA, MLA_HEADS * (MLA_NOPE + MLA_ROPE)), MLA_Q_LORA ** -0.5),
        "mla_wukv": _normal(ks[15], (N_MLA_LAYERS, MLA_KV_LORA, MLA_HEADS * (MLA_NOPE + MLA_V)), MLA_KV_LORA ** -0.5),
        "mla_wo": _normal(ks[16], (N_MLA_LAYERS, MLA_HEADS * MLA_V, D), (MLA_HEADS * MLA_V) ** -0.5),
        "ca_wq": _normal(ks[17], (DEPTH, D, CA_HEADS * CA_HEAD_DIM), D ** -0.5),
        "ca_wkv": _normal(ks[18], (DEPTH, D, 2 * CA_HEADS * CA_HEAD_DIM), D ** -0.5),
        "ca_wo": _normal(ks[19], (DEPTH, CA_HEADS * CA_HEAD_DIM, D), (CA_HEADS * CA_HEAD_DIM) ** -0.5),
        "mlp_wup": _normal(ks[20], (DEPTH, D, D_FF), D ** -0.5),
        "mlp_wdown": _normal(ks[21], (DEPTH, D_FF, D), D_FF ** -0.5),
    }


def reference(x, mem, positions, attn_norm, cross_norm, mlp_norm, mem_norm, final_norm,
              da_wqkv, da_lambda, da_subln, da_wo,
              mla_wdown, mla_q_norm, mla_kv_norm, mla_wuq, mla_wukv, mla_wo,
              ca_wq, ca_wkv, ca_wo, mlp_wup, mlp_wdown):
    cos_p, sin_p = rope_angles(positions, DA_ROT)
    cos_m, sin_m = rope_angles(positions, MLA_ROPE)
    mem_h = rms_norm(mem, mem_norm)
    for i in range(DEPTH):
        j = i // N_MIXERS
        h = rms_norm(x, attn_norm[i])
        if i % N_MIXERS == 0:
            lambda_init = 0.8 - 0.6 * math.exp(-0.3 * i)
            x = x + diff_attention(h, da_wqkv[j], da_lambda[j], da_subln[j], da_wo[j],
                                   cos_p, sin_p, lambda_init)
        else:
            x = x + mla_attention(h, mla_wdown[j], mla_q_norm[j], mla_kv_norm[j],
                                  mla_wuq[j], mla_wukv[j], mla_wo[j], cos_m, sin_m)
        x = x + memory_cross_attention(rms_norm(x, cross_norm[i]), mem_h,
                                       ca_wq[i], ca_wkv[i], ca_wo[i])
        x = x + sq_relu_mlp(rms_norm(x, mlp_norm[i]), mlp_wup[i], mlp_wdown[i])
    return rms_norm(x, final_norm)
```

```python
import math
from contextlib import ExitStack

import numpy as np
import concourse.bass as bass
import concourse.mybir as mybir
from concourse.bass_utils import run_bass_kernel_spmd

F32 = mybir.dt.float32
BF16 = mybir.dt.bfloat16
I32 = mybir.dt.int32
AF = mybir.ActivationFunctionType
ALU = mybir.AluOpType
AX = mybir.AxisListType

D = 2048
DEPTH = 4
NMEM = 256
EPS = 1e-6
THETA = 500000.0
TT = 512
ARENA_BYTES = 184 * 1024
N_DMA_SEMS = 32
N_SW_SEMS = 12
N_CC_SEMS = 36
PI = math.pi


class Op:
    __slots__ = ("eng", "fn", "deps", "kind", "needs_inc", "sem", "val", "idx")

    def __init__(self, eng, fn, kind):
        self.eng = eng
        self.fn = fn
        self.kind = kind
        self.deps = []
        self.needs_inc = False
        self.sem = None
        self.val = None


class Sched:
    ENGS = ("pe", "act", "dve", "pool", "sp")

    def __init__(self):
        self.ops = {e: [] for e in self.ENGS}
        self.last_w = {}
        self.readers = {}
        self.dma_rr = 0
        self.swdma_rr = 0
        self.cc_rr = 0
        self.dma_last = {}
        self.cc_last = {}
        self.all_async = []
        self.barrier_deps = {e: [] for e in self.ENGS}

    def _emit(self, eng, fn, reads, writes, kind, sbuf=True):
        op = Op(eng, fn, kind)
        deps = []
        for r in reads:
            w = self.last_w.get(r)
            if w is not None:
                deps.append(w)
        for r in writes:
            w = self.last_w.get(r)
            if w is not None:
                deps.append(w)
            deps.extend(self.readers.get(r, ()))
        if self.barrier_deps[eng]:
            deps.extend(self.barrier_deps[eng])
            self.barrier_deps[eng] = []
        if kind == "d":
            if eng == "pool":
                s = N_DMA_SEMS + (self.swdma_rr % N_SW_SEMS)
                self.swdma_rr += 1
            else:
                s = self.dma_rr % N_DMA_SEMS
                self.dma_rr += 1
            prev = self.dma_last.get(s)
            if prev is not None:
                deps.append(prev)
            self.dma_last[s] = op
            op.sem = ("dma", s)
            op.needs_inc = True
            if sbuf:
                self.all_async.append(op)
        elif kind == "x":
            s = self.cc_rr % N_CC_SEMS
            self.cc_rr += 1
            prev = self.cc_last.get(s)
            if prev is not None:
                deps.append(prev)
            self.cc_last[s] = op
            op.sem = ("cc", s)
            op.needs_inc = True
        else:
            op.sem = ("eng", eng)
        seen = set()
        for d in deps:
            if d is op or id(d) in seen:
                continue
            seen.add(id(d))
            if d.kind == "c" and d.eng == eng and eng == "pe":
                continue
            op.deps.append(d)
            d.needs_inc = True
        for r in reads:
            self.readers.setdefault(r, []).append(op)
        for r in writes:
            self.last_w[r] = op
            self.readers[r] = []
        self.ops[eng].append(op)
        return op

    def pe(self, fn, r=(), w=()):
        return self._emit("pe", fn, r, w, "c")

    def act(self, fn, r=(), w=()):
        return self._emit("act", fn, r, w, "c")

    def dve(self, fn, r=(), w=()):
        return self._emit("dve", fn, r, w, "c")

    def pool(self, fn, r=(), w=()):
        return self._emit("pool", fn, r, w, "c")

    def dma(self, q, out, in_, r=(), w=(), sbuf=True, **kw):
        return self._emit(q, lambda e: e.dma_start(out=out, in_=in_, **kw), r, w, "d", sbuf=sbuf)

    def cc(self, fn, r=(), w=()):
        return self._emit("pool", fn, r, w, "x")

    def barrier(self):
        deps = []
        for e in self.ENGS:
            if self.ops[e]:
                deps.append(self.ops[e][-1])
        deps.extend(self.all_async)
        self.all_async = []
        for e in self.ENGS:
            self.barrier_deps[e] = list(deps)

    def finalize(self, nc, block, sems):
        cnt = {}
        for e in self.ENGS:
            for op in self.ops[e]:
                if op.needs_inc:
                    step = 16 if op.kind == "d" else 1
                    cnt[op.sem] = cnt.get(op.sem, 0) + step
                    op.val = cnt[op.sem]
        final_vals = dict(cnt)

        def run(engname, eng):
            waited = {}
            for op in self.ops[engname]:
                need = {}
                for d in op.deps:
                    if need.get(d.sem, 0) < d.val:
                        need[d.sem] = d.val
                for s, v in need.items():
                    if waited.get(s, 0) >= v:
                        continue
                    eng.wait_ge(sems[s], v)
                    waited[s] = v
                ins = op.fn(eng)
                if op.needs_inc:
                    ins.then_inc(sems[op.sem], 16 if op.kind == "d" else 1)
            if engname == "sp":
                for s, v in final_vals.items():
                    if waited.get(s, 0) < v:
                        eng.wait_ge(sems[s], v)

        block.tensor(lambda e: run("pe", e))
        block.scalar(lambda e: run("act", e))
        block.vector(lambda e: run("dve", e))
        block.gpsimd(lambda e: run("pool", e))
        block.sync(lambda e: run("sp", e))


def _rl(x):
    return list(x) if isinstance(x, list) else [x]


def lambda_init(i):
    return 0.8 - 0.6 * math.exp(-0.3 * i)


class Builder:
    def __init__(self, S, depth=DEPTH, stage=99):
        self.stage = stage
        self.S = S
        self.depth = depth
        self.NT = S // 4
        self.NB = self.NT // 128
        self.NTT = self.NT // TT
        assert self.NT % TT == 0
        self.nc = bass.Bass("TRN2", target_bir_lowering=False)
        self.s = Sched()
        self.bank_rr = 0

    def declare(self):
        nc, NT, NB = self.nc, self.NT, self.NB
        nd = (self.depth + 1) // 2
        nm = self.depth // 2
        self.nd, self.nm = nd, nm
        ei = lambda n, shp, dt=F32: nc.dram_tensor(n, shp, dt, kind="ExternalInput").ap()
        self.x_in = ei("x", [NT, D])
        self.pos_in = ei("pos", [128, NB], I32)
        self.mem_in = ei("mem", [NMEM, D])
        self.ncol = 14 * 16 + nd * 2 + max(nm, 1) * 8
        self.colp_in = ei("colp", [128, self.ncol])
        self.ident_in = ei("ident", [128, 128])
        self.masks_in = ei("masks", [128, 4, 128])
        self.invf_in = ei("invf", [128, 48])
        self.fin_in = ei("final_norm", [D])
        self.lam_in = ei("da_lambda", [max(nd, 1), 512])
        self.w_in = {
            "da_wqkv": ei("da_wqkv", [nd, D, 3 * D]),
            "da_wo": ei("da_wo", [nd, D, D]),
            "ca_wq": ei("ca_wq", [self.depth, D, 512]),
            "ca_wkv": ei("ca_wkv", [self.depth, D, 1024]),
            "ca_wo": ei("ca_wo", [self.depth, 512, D]),
            "mlp_wup": ei("mlp_wup", [self.depth, D, 4 * D]),
            "mlp_wdown": ei("mlp_wdown", [self.depth, 4 * D, D]),
        }
        if nm:
            self.w_in.update({
                "mla_wdown": ei("mla_wdown", [nm, D, 1088]),
                "mla_wuq": ei("mla_wuq", [nm, 512, 3072]),
                "mla_wukv": ei("mla_wukv", [nm, 512, 4096]),
                "mla_wo": ei("mla_wo", [nm, D, D]),
            })
        self.out = nc.dram_tensor("out", [NT, D], F32, kind="ExternalOutput").ap()

        it = lambda n, shp, dt=BF16: nc.dram_tensor(n, shp, dt).ap()
        self.pan = {
            "da_wqkv": it("p_da_wqkv", [nd, 12, 128, 16 * 512]),
            "da_wo": it("p_da_wo", [nd, 4, 128, 16 * 512]),
            "ca_wq": it("p_ca_wq", [self.depth, 1, 128, 16 * 512]),
            "ca_wkv": it("p_ca_wkv", [self.depth, 2, 128, 16 * 512]),
            "ca_wo": it("p_ca_wo", [self.depth, 4, 128, 4 * 512]),
            "mlp_wup": it("p_mlp_wup", [self.depth, 16, 128, 16 * 512]),
            "mlp_wdown": it("p_mlp_wdown", [self.depth, 16, 128, 16 * 512]),
        }
        if nm:
            self.pan.update({
                "mla_wdown": it("p_mla_wdown", [nm, 3, 128, 16 * 512]),
                "mla_wuq": it("p_mla_wuq", [nm, 6, 128, 4 * 512]),
                "mla_wukv": it("p_mla_wukv", [nm, 8, 128, 4 * 512]),
                "mla_wo": it("p_mla_wo", [nm, 4, 128, 16 * 512]),
            })
        self.xs = it("xs", [NT, D], F32)
        self.ao = it("ao", [NT, D], F32)
        self.ropetab = it("ropetab", [128, NB * 96], F32)
        self.memhT_d = it("memhT", [128, 16 * 256])
        self.q_loc = it("q_loc", [16, 128, NT])
        self.qr_loc = it("qr_loc", [8, 128, NT])
        self.kt_loc = it("kt_loc", [16, 128, NT])
        self.kt_all = it("kt_all", [16, 4 * 128, NT])
        self.kr_loc = it("kr_loc", [128, NT])
        self.kr_all = it("kr_all", [4 * 128, NT])
        self.vch_da = min(NT, 2048)
        self.vch_mla = min(NT, 4096)
        self.v_loc = it("v_loc", [NT * D])
        self.v_all = it("v_all", [4 * NT * D])

    def carve(self, off, shape, dt):
        n = 1
        for d_ in shape[1:]:
            n *= d_
        nb = n * (2 if dt == BF16 else 4)
        assert off % 4 == 0 and off + nb <= ARENA_BYTES, (off, shape)
        v = self.arena[:, off // 2:(off + nb) // 2]
        if dt != BF16:
            v = v.bitcast(dt)
        if len(shape) == 3:
            v = v.rearrange("p (a b) -> p a b", b=shape[2])
        elif len(shape) == 4:
            v = v.rearrange("p (a b c) -> p a b c", b=shape[2], c=shape[3])
        return v

    def alloc(self, st):
        nc, NT, NB = self.nc, self.NT, self.NB
        sb = lambda n, shp, dt: st.enter_context(nc.sbuf_tensor("sb_" + n, shp, dt))
        self.arena = sb("arena", [128, ARENA_BYTES // 2], BF16)
        K = 1024
        self.xt = self.carve(0, [128, 4, D], F32)
        self.hT = self.carve(32 * K, [128, 16, 512], BF16)
        self.hT_f32 = self.carve(32 * K, [128, D], F32)
        self.WP = [self.carve((48 + 16 * i) * K, [128, 16, 512], BF16) for i in range(3)]
        self.actb = self.carve(96 * K, [128, 64, 512], BF16)
        self.aot = self.carve(96 * K, [128, 4, D], F32)
        self.xn = [self.carve((160 + 4 * i) * K, [128, D], BF16) for i in range(2)]
        self.stg = [self.carve((168 + 2 * i) * K, [128, 512], F32) for i in range(2)]
        self.qk_tok = self.carve(172 * K, [128, 4, 512], BF16)
        self.qkT = self.carve(176 * K, [128, 4, 512], BF16)
        self.vst = self.carve(180 * K, [128, 4, 512], BF16)
        self.KT = self.carve(0, [128, 4, NT], BF16)
        self.Vda = self.carve(32 * K, [128, 4, NB, 257], BF16)
        self.Vml = self.carve(32 * K, [128, 4, NB, 129], BF16)
        self.QT = self.carve(97 * K, [128, NT], BF16)
        self.O1n = self.carve(105 * K, [128, NB, 256], F32)
        self.KR = self.carve(105 * K, [128, 4, NT], BF16)
        self.QR = self.carve(137 * K, [128, NT], BF16)
        self.PT = [self.carve((145 + i) * K, [128, 512], BF16) for i in range(4)]
        self.osb = [self.carve((149 + 4 * i) * K, [128, 4, 256], F32) for i in range(2)]
        self.tab = self.carve(0, [128, NB, 96], F32)
        self.ident = sb("ident", [128, 128], BF16)
        self.ident_f = sb("ident_f", [128, 128], F32)
        self.masks = sb("masks", [128, 4, 128], BF16)
        self.masks_f = sb("masks_f", [128, 4, 128], F32)
        self.colp = sb("colp", [128, self.ncol], F32)
        self.invf = sb("invf", [128, 48], F32)
        self.pos_i = sb("pos_i", [128, NB], I32)
        self.pos_f = sb("pos_f", [128, NB], F32)
        self.kmT = sb("kmT", [128, 4, 256], BF16)
        self.vm = sb("vm", [128, 2, 512], BF16)
        self.ones_bf = sb("ones_bf", [128, 128], BF16)
        self.rope = [sb("rope%d" % i, [128, 4, 96], F32) for i in range(2)]
        self.ss = sb("ss", [128, 16], F32)
        self.rstd = sb("rstd", [128, 16], F32)
        self.tmp = sb("tmp", [128, 4, 256], F32)
        self.lamt = sb("lamt", [128, 512], F32)
        self.lamw = sb("lamw", [128, 256], F32)
        self.lam = sb("lam", [128, 8], F32)
        self.rz = sb("rz", [128, 8], F32)
        self.junk = sb("junk", [128, 512], BF16)
        self.ps = [st.enter_context(nc.psum_tensor("ps%d" % i, [128, 512], F32)) for i in range(8)]

    def next_bank(self):
        b = self.bank_rr % 8
        self.bank_rr += 1
        return b

    def cast_panels(self, name, li, lo=0, hi=None):
        s = self.s
        w = self.w_in[name][li]
        pan = self.pan[name][li]
        npan = pan.shape[0]
        hi = npan if hi is None else hi
        for c in range(lo, hi):
            if name in ("da_wqkv", "da_wo", "mla_wo", "ca_wq", "ca_wkv", "mlp_wup"):
                src = w.rearrange("(kc p) n -> p kc n", p=128)[:, :, c * 512:(c + 1) * 512]
                dst = pan[c].rearrange("p (kc n) -> p kc n", n=512)
            elif name == "mlp_wdown":
                cc, g = c // 4, c % 4
                src = w[g * 2048:(g + 1) * 2048].rearrange("(kc p) n -> p kc n", p=128)[:, :, cc * 512:(cc + 1) * 512]
                dst = pan[c].rearrange("p (kc n) -> p kc n", n=512)
            elif name == "ca_wo":
                src = w.rearrange("(kc p) n -> p kc n", p=128)[:, :, c * 512:(c + 1) * 512]
                dst = pan[c].rearrange("p (kc n) -> p kc n", n=512)
            elif name == "mla_wdown":
                wdt = 512 if c < 2 else 64
                src = w.rearrange("(kc p) n -> p kc n", p=128)[:, :, c * 512:c * 512 + wdt]
                dst = pan[c].rearrange("p (kc n) -> p kc n", n=512)[:, :, 0:wdt]
            elif name == "mla_wuq":
                wv = w.rearrange("(kc p) (h e) -> p kc h e", p=128, e=192)
                if c < 4:
                    src = wv[:, :, 4 * c:4 * c + 4, 0:128]
                    dst = pan[c].rearrange("p (kc h e) -> p kc h e", h=4, e=128)
                else:
                    src = wv[:, :, 8 * (c - 4):8 * (c - 4) + 8, 128:192]
                    dst = pan[c].rearrange("p (kc h e) -> p kc h e", h=8, e=64)
            elif name == "mla_wukv":
                wv = w.rearrange("(kc p) (h e) -> p kc h e", p=128, e=256)
                if c < 4:
                    src = wv[:, :, 4 * c:4 * c + 4, 0:128]
                else:
                    src = wv[:, :, 4 * (c - 4):4 * (c - 4) + 4, 128:256]
                dst = pan[c].rearrange("p (kc h e) -> p kc h e", h=4, e=128)
            else:
                raise KeyError(name)
            if len(src.shape) == 4:
                for kc in range(src.shape[1]):
                    s.dma("pool", dst[:, kc], src[:, kc], r=(), w=[("pan", name, li, c), "castq"], sbuf=False)
                continue
            s.dma("pool", dst, src, r=(), w=[("pan", name, li, c), "castq"], sbuf=False)

    def load_panel(self, name, li, c, slot, kc=16, width=512):
        src = self.pan[name][li][c][:, 0:kc * 512].rearrange("p (kc n) -> p kc n", n=512)[:, :, 0:width]
        self.s.dma("sp", self.WP[slot][:, 0:kc, 0:width], src, r=[("pan", name, li, c)], w=[("WP", slot)])

    def transpose_to(self, bank, col0, src_ap, rd, ident=None, ncols=128):
        ps = self.ps[bank]
        idn = self.ident if ident is None else ident
        self.s.pe(lambda e: e.matmul(ps[:, col0:col0 + ncols], lhsT=src_ap, rhs=idn[:, 0:ncols],
                                     start=True, stop=True),
                  r=list(rd) + ["ident"], w=[("ps", bank)])

    def rms_rstd(self, src_ap, n, col, rd, junk_ap=None, junk_res="junk"):
        s = self.s
        ss, rstd = self.ss, self.rstd
        jk = self.junk[:, 0:n] if junk_ap is None else junk_ap
        s.act(lambda e: e.activation(jk, src_ap, AF.Square, accum_out=ss[:, col:col + 1]),
              r=list(rd), w=[("ss", col), junk_res])
        s.act(lambda e: e.activation(rstd[:, col:col + 1], ss[:, col:col + 1], AF.Ln, bias=self.epsb[:, 0:1], scale=1.0 / n),
              r=[("ss", col), "epsb"], w=[("rstd", col)])
        s.act(lambda e: e.activation(rstd[:, col:col + 1], rstd[:, col:col + 1], AF.Exp, scale=-0.5),
              r=[("rstd", col)], w=[("rstd", col)])

    def norm_to_hT(self, src, nblk, gcol0, src_res, dst=None, dst_res=None, width=512):
        s = self.s
        dst = self.hT if dst is None else dst
        dst_res = "hT" if dst_res is None else dst_res
        for b in range(nblk):
            xn = self.xn[b % 2]
            xr = ("xn", b % 2)
            sap = src[:, b, :]
            self.rms_rstd(sap, D, b, rd=[src_res(b)], junk_ap=xn[:, :], junk_res=xr)
            s.dve(lambda e, xn=xn, sap=sap, b=b: e.tensor_scalar(
                xn[:, :], sap, self.rstd[:, b:b + 1], None, op0=ALU.mult),
                r=[src_res(b), ("rstd", b)], w=[xr])
            for g in range(4):
                bank = self.next_bank()
                for k in range(4):
                    kc = 4 * g + k
                    self.transpose_to(bank, k * 128, xn[:, kc * 128:(kc + 1) * 128], rd=[xr])
                ps = self.ps[bank]
                gc = self.colp[:, gcol0 + 4 * g:gcol0 + 4 * g + 4].unsqueeze(2).to_broadcast([128, 4, 128])
                s.dve(lambda e, ps=ps, g=g, b=b, gc=gc: e.tensor_tensor(
                    dst[:, 4 * g:4 * g + 4, b * 128:(b + 1) * 128],
                    ps[:, :].rearrange("p (a t) -> p a t", t=128), gc, op=ALU.mult),
                    r=[("ps", bank), "colp"], w=[dst_res])

    def proj_tok(self, lhs, lhs_res, nblk, kcn, slot, width, consume):
        s = self.s
        wp = self.WP[slot]
        for b in range(nblk):
            bank = self.next_bank()
            ps = self.ps[bank]
            for kc in range(kcn):
                s.pe(lambda e, kc=kc, b=b, ps=ps: e.matmul(
                    ps[:, 0:width], lhsT=lhs[:, kc, b * 128:(b + 1) * 128], rhs=wp[:, kc, 0:width],
                    start=(kc == 0), stop=(kc == kcn - 1)),
                    r=_rl(lhs_res) + [("WP", slot)], w=[("ps", bank)])
            consume(b, bank)

    def proj_feat(self, rhs, rhs_res, ntok, kcn, slot, m0, consume_bank):
        s = self.s
        wp = self.WP[slot]
        bank = self.next_bank()
        ps = self.ps[bank]
        for kc in range(kcn):
            s.pe(lambda e, kc=kc: e.matmul(
                ps[:, 0:ntok], lhsT=wp[:, kc, m0:m0 + 128], rhs=rhs[:, kc, 0:ntok],
                start=(kc == 0), stop=(kc == kcn - 1)),
                r=_rl(rhs_res) + [("WP", slot)], w=[("ps", bank)])
        consume_bank(bank)

    def setup(self):
        s, NB = self.s, self.NB
        s.dma("sp", self.ident_f[:, :], self.ident_in, w=["ident_f"])
        s.dma("sp", self.masks_f[:, :, :], self.masks_in, w=["masks_f"])
        s.dma("sp", self.colp[:, :], self.colp_in, w=["colp"])
        s.dma("sp", self.invf[:, :], self.invf_in, w=["invf"])
        s.dma("sp", self.pos_i[:, :], self.pos_in, w=["pos_i"])
        s.dve(lambda e: e.tensor_copy(self.ident[:, :], self.ident_f[:, :]), r=["ident_f"], w=["ident"])
        s.dve(lambda e: e.tensor_copy(self.masks[:, :, :], self.masks_f[:, :, :]), r=["masks_f"], w=["masks"])
        s.dve(lambda e: e.tensor_copy(self.pos_f[:, :], self.pos_i[:, :]), r=["pos_i"], w=["pos_f"])
        s.dve(lambda e: e.memset(self.ones_bf[:, :], 1.0), w=["ones_bf"])
        tab = self.tab
        K_ = 1024
        C1 = 6.28125
        C2 = 2 * PI - C1
        for (f0, nf, c0, s0) in ((0, 16, 0, 16), (16, 32, 32, 64)):
            tf = [self.carve((48 + 4 * i) * K_, [128, NB, 32], F32)[:, :, 0:nf] for i in range(6)]
            ti = self.carve((48 + 4 * 6) * K_, [128, NB, 32], I32)[:, :, 0:nf]
            a, q, kf, r_, y, w1 = tf
            pb = self.pos_f[:, :].unsqueeze(2).to_broadcast([128, NB, nf])
            fb = self.invf[:, f0:f0 + nf].unsqueeze(1).to_broadcast([128, NB, nf])
            s.dve(lambda e, a=a, pb=pb, fb=fb: e.tensor_tensor(a, pb, fb, op=ALU.mult), r=["pos_f", "invf", "tab"], w=["ta"])
            s.dve(lambda e, a=a, q=q: e.tensor_scalar(q, a, 1.0 / (2 * PI), None, op0=ALU.mult), r=["ta"], w=["tq"])
            s.dve(lambda e, ti=ti, q=q: e.tensor_copy(ti, q), r=["tq"], w=["ti"])
            s.dve(lambda e, ti=ti, kf=kf: e.tensor_copy(kf, ti), r=["ti"], w=["tk"])
            s.dve(lambda e, kf=kf, a=a, r_=r_: e.scalar_tensor_tensor(r_, kf, -C1, a, op0=ALU.mult, op1=ALU.add),
                  r=["tk", "ta"], w=["tr"])
            s.dve(lambda e, kf=kf, r_=r_: e.scalar_tensor_tensor(r_, kf, -C2, r_, op0=ALU.mult, op1=ALU.add),
                  r=["tk", "tr"], w=["tr"])
            for (shift, col) in ((PI / 2, c0), (0.0, s0)):
                s.dve(lambda e, y=y, r_=r_, shift=shift: e.tensor_scalar(y, r_, shift, None, op0=ALU.add),
                      r=["tr", "tab"], w=["ty"])
                s.dve(lambda e, y=y, w1=w1: e.tensor_scalar(w1, y, PI, -2 * PI, op0=ALU.is_gt, op1=ALU.mult),
                      r=["ty"], w=["tw"])
                s.dve(lambda e, y=y, w1=w1: e.tensor_tensor(y, y, w1, op=ALU.add), r=["ty", "tw"], w=["ty"])
                s.dve(lambda e, y=y, w1=w1: e.tensor_scalar(w1, y, -PI, 2 * PI, op0=ALU.is_lt, op1=ALU.mult),
                      r=["ty"], w=["tw"])
                s.dve(lambda e, y=y, w1=w1: e.tensor_tensor(y, y, w1, op=ALU.add), r=["ty", "tw"], w=["ty"])
                s.act(lambda e, y=y, col=col, nf=nf: e.activation(tab[:, :, col:col + nf], y, AF.Sin),
                      r=["ty"], w=["tab"])
        s.dma("sp", self.ropetab.rearrange("p (m f) -> p m f", f=96), tab[:, :, :], r=["tab"], w=["ropetab"])
        s.barrier()
        s.dma("sp", self.xt[:, 0:2, :], self.mem_in.rearrange("(b p) d -> p b d", p=128), w=[("xt", 0), ("xt", 1)])
        mh = self.hT[:, :, 0:256]
        self.norm_to_hT(self.xt, 2, 12 * 16, lambda b: ("xt", b))
        s.dma("sp", self.memhT_d.rearrange("p (kc t) -> p kc t", t=256), mh, r=["hT"], w=["memhT_d"])

    def mem_kv(self, L):
        s = self.s
        mh = self.actb[:, 0:8, :].rearrange("p a b -> p (a b)").rearrange("p (kc t) -> p kc t", t=256)
        s.dma("sp", mh, self.memhT_d.rearrange("p (kc t) -> p kc t", t=256), r=["memhT_d"],
              w=[("act", i) for i in range(8)])
        mres = ("act", 0)
        self.load_panel("ca_wkv", L, 0, 0)
        self.load_panel("ca_wkv", L, 1, 1)
        kmT, vm = self.kmT, self.vm
        for h in range(4):
            def cons(bank, h=h):
                ps = self.ps[bank]
                s.act(lambda e: e.copy(kmT[:, h, :], ps[:, 0:256]), r=[("ps", bank)], w=["kmT"])
            self.proj_feat(mh, mres, 256, 16, 0, h * 128, cons)

        def consv(b, bank):
            ps = self.ps[bank]
            s.act(lambda e: e.copy(vm[:, b, :], ps[:, :]), r=[("ps", bank)], w=["vm"])
        self.proj_tok(mh, mres, 2, 16, 1, 512, consv)

    def load_rope(self, t):
        rp = self.rope[t % 2]
        self.s.dma("sp", rp[:, :, :],
                   self.ropetab.rearrange("p (m f) -> p m f", f=96)[:, 4 * t:4 * t + 4, :],
                   r=["ropetab"], w=[("rope", t % 2)])
        return rp, ("rope", t % 2)

    def rope_ops(self, dst, src, cos, sin, nh, half, rd, wr):
        s = self.s
        tmp = self.tmp
        cb = cos.unsqueeze(1).to_broadcast([128, nh, half])
        sn = sin.unsqueeze(1).to_broadcast([128, nh, half])
        x1, x2 = src[:, :, 0:half], src[:, :, half:2 * half]
        t = [tmp[:, i, 0:nh * half].rearrange("p (h f) -> p h f", f=half) for i in range(4)]
        s.dve(lambda e: e.tensor_tensor(t[0], x1, cb, op=ALU.mult), r=rd, w=[("tmp", 0)])
        s.dve(lambda e: e.tensor_tensor(t[1], x2, sn, op=ALU.mult), r=rd, w=[("tmp", 1)])
        s.dve(lambda e: e.tensor_tensor(t[2], x2, cb, op=ALU.mult), r=rd, w=[("tmp", 2)])
        s.dve(lambda e: e.tensor_tensor(t[3], x1, sn, op=ALU.mult), r=rd, w=[("tmp", 3)])
        s.dve(lambda e: e.tensor_tensor(dst[:, :, 0:half], t[0], t[1], op=ALU.subtract),
              r=[("tmp", 0), ("tmp", 1)], w=wr)
        s.dve(lambda e: e.tensor_tensor(dst[:, :, half:2 * half], t[2], t[3], op=ALU.add),
              r=[("tmp", 2), ("tmp", 3)], w=wr)

    def p_phase_da(self, L, t):
        s, NT = self.s, self.NT
        j = L // 2
        rp, rres = self.load_rope(t)
        self.norm_to_hT(self.xt, 4, (0 * 4 + L) * 16, lambda b: ("xt", b))
        for c in range(12):
            slot = self.wp_rr % 3
            self.wp_rr += 1
            self.load_panel("da_wqkv", j, c, slot)
            if c < 8:
                def cons(b, bank, c=c):
                    ps = self.ps[bank]
                    stg = self.stg[b % 2]
                    sr = ("stg", b % 2)
                    s.act(lambda e: e.copy(stg[:, :], ps[:, :]), r=[("ps", bank)], w=[sr])
                    sv = stg[:, :].rearrange("p (h e) -> p h e", e=128)
                    dv = self.qk_tok[:, b, :].rearrange("p (h e) -> p h e", e=128)
                    self.rope_ops(dv[:, :, 0:32], sv[:, :, 0:32], rp[:, b, 0:16], rp[:, b, 16:32], 4, 16,
                                  rd=[sr, rres], wr=[("qk_tok", b)])
                    s.pool(lambda e: e.tensor_copy(dv[:, :, 32:128], sv[:, :, 32:128]), r=[sr], w=[("qk_tok", b)])
                self.proj_tok(self.hT, "hT", 4, 16, slot, 512, cons)
                for hh in range(4):
                    bank = self.next_bank()
                    for b in range(4):
                        self.transpose_to(bank, b * 128, self.qk_tok[:, b, hh * 128:(hh + 1) * 128], rd=[("qk_tok", b)])
                    ps = self.ps[bank]
                    s.act(lambda e, ps=ps, hh=hh: e.copy(self.qkT[:, hh, :], ps[:, :]), r=[("ps", bank)], w=["qkT"])
                u0 = (c % 4) * 4
                dstT = self.q_loc if c < 4 else self.kt_loc
                nm = "q_loc" if c < 4 else "kt_loc"
                s.dma("sp", dstT[u0:u0 + 4, :, t * TT:(t + 1) * TT].rearrange("u p t -> p u t"), self.qkT[:, :, :],
                      r=["qkT"], w=[(nm, u0 + i) for i in range(4)])
            else:
                def consv(b, bank):
                    ps = self.ps[bank]
                    s.act(lambda e: e.copy(self.vst[:, b, :], ps[:, :]), r=[("ps", bank)], w=[("vst", b)])
                self.proj_tok(self.hT, "hT", 4, 16, slot, 512, consv)
                vl = self.v_loc.rearrange("(h n d) -> h n d", n=NT, d=256)
                h0 = 2 * (c - 8)
                for hd in range(2):
                    s.dma("sp", vl[h0 + hd, t * TT:(t + 1) * TT, :].rearrange("(b p) d -> p b d", p=128),
                          self.vst[:, :, hd * 256:(hd + 1) * 256],
                          r=[("vst", b) for b in range(4)], w=[("v_loc", h0 + hd)])

    def p_phase_mla(self, L, t):
        s, NT = self.s, self.NT
        j = L // 2
        cb = 14 * 16 + self.nd * 2 + j * 8
        rp, rres = self.load_rope(t)
        self.norm_to_hT(self.xt, 4, (0 * 4 + L) * 16, lambda b: ("xt", b))
        cqT = self.actb[:, 0:4, :]
        ckvT = self.actb[:, 4:8, :]
        for c in range(3):
            slot = self.wp_rr % 3
            self.wp_rr += 1
            wdt = 512 if c < 2 else 64
            self.load_panel("mla_wdown", j, c, slot, width=wdt)
            if c < 2:
                dstT = cqT if c == 0 else ckvT
                dres = ("act", 0) if c == 0 else ("act", 4)

                def cons(b, bank, c=c, dstT=dstT, dres=dres):
                    ps = self.ps[bank]
                    col = 4 + b
                    self.rms_rstd(ps[:, :], 512, col, rd=[("ps", bank)])
                    xn = self.xn[b % 2]
                    xr = ("xn", b % 2)
                    s.dve(lambda e: e.tensor_scalar(xn[:, 0:512], ps[:, :], self.rstd[:, col:col + 1], None,
                                                    op0=ALU.mult),
                          r=[("ps", bank), ("rstd", col)], w=[xr])
                    bank2 = self.next_bank()
                    for k in range(4):
                        self.transpose_to(bank2, k * 128, xn[:, k * 128:(k + 1) * 128], rd=[xr])
                    ps2 = self.ps[bank2]
                    gc = self.colp[:, cb + 4 * c:cb + 4 * c + 4].unsqueeze(2).to_broadcast([128, 4, 128])
                    s.dve(lambda e: e.tensor_tensor(dstT[:, :, b * 128:(b + 1) * 128],
                                                    ps2[:, :].rearrange("p (a t) -> p a t", t=128), gc, op=ALU.mult),
                          r=[("ps", bank2), "colp"], w=[dres])
                self.proj_tok(self.hT, "hT", 4, 16, slot, 512, cons)
            else:
                def consr(b, bank):
                    ps = self.ps[bank]
                    stg = self.stg[b % 2]
                    sr = ("stg", b % 2)
                    s.act(lambda e: e.copy(stg[:, 0:64], ps[:, 0:64]), r=[("ps", bank)], w=[sr])
                    sv = stg[:, 0:64].rearrange("p (h e) -> p h e", e=64)
                    dv = self.qk_tok[:, b, 0:64].rearrange("p (h e) -> p h e", e=64)
                    self.rope_ops(dv, sv, rp[:, b, 32:64], rp[:, b, 64:96], 1, 32, rd=[sr, rres], wr=[("qk_tok", b)])
                    s.pool(lambda e: e.tensor_copy(self.qk_tok[:, b, 64:128], self.qk_tok[:, b, 0:64]),
                           r=[("qk_tok", b)], w=[("qk_tok", b)])
                self.proj_tok(self.hT, "hT", 4, 16, slot, 64, consr)
                bank = self.next_bank()
                for b in range(4):
                    self.transpose_to(bank, b * 128, self.qk_tok[:, b, 0:128], rd=[("qk_tok", b)])
                ps = self.ps[bank]
                s.act(lambda e, ps=ps: e.copy(self.qkT[:, 0, :], ps[:, :]), r=[("ps", bank)], w=["qkT"])
                s.dma("sp", self.kr_loc[:, t * TT:(t + 1) * TT], self.qkT[:, 0, :], r=["qkT"], w=["kr_loc"])
        for c in range(4):
            slot = self.wp_rr % 3
            self.wp_rr += 1
            self.load_panel("mla_wuq", j, c, slot, kc=4)
            for hh in range(4):
                def cons(bank, hh=hh):
                    ps = self.ps[bank]
                    s.act(lambda e: e.copy(self.qkT[:, hh, :], ps[:, :]), r=[("ps", bank)], w=["qkT"])
                self.proj_feat(cqT, ("act", 0), 512, 4, slot, hh * 128, cons)
            s.dma("sp", self.q_loc[4 * c:4 * c + 4, :, t * TT:(t + 1) * TT].rearrange("u p t -> p u t"), self.qkT[:, :, :],
                  r=["qkT"], w=[("q_loc", 4 * c + i) for i in range(4)])
        for c in range(2):
            slot = self.wp_rr % 3
            self.wp_rr += 1
            self.load_panel("mla_wuq", j, 4 + c, slot, kc=4)

            def consq(b, bank):
                ps = self.ps[bank]
                stg = self.stg[b % 2]
                sr = ("stg", b % 2)
                s.act(lambda e: e.copy(stg[:, :], ps[:, :]), r=[("ps", bank)], w=[sr])
                sv = stg[:, :].rearrange("p (h e) -> p h e", e=64)
                dv = self.qk_tok[:, b, :].rearrange("p (h e) -> p h e", e=64)
                self.rope_ops(dv, sv, rp[:, b, 32:64], rp[:, b, 64:96], 8, 32, rd=[sr, rres], wr=[("qk_tok", b)])
            self.proj_tok(cqT, ("act", 0), 4, 4, slot, 512, consq)
            for pr in range(4):
                bank = self.next_bank()
                for b in range(4):
                    self.transpose_to(bank, b * 128, self.qk_tok[:, b, pr * 128:(pr + 1) * 128], rd=[("qk_tok", b)])
                ps = self.ps[bank]
                s.act(lambda e, ps=ps, pr=pr: e.copy(self.qkT[:, pr, :], ps[:, :]), r=[("ps", bank)], w=["qkT"])
            s.dma("sp", self.qr_loc[4 * c:4 * c + 4, :, t * TT:(t + 1) * TT].rearrange("u p t -> p u t"), self.qkT[:, :, :],
                  r=["qkT"], w=[("qr_loc", 4 * c + i) for i in range(4)])
        for c in range(4):
            slot = self.wp_rr % 3
            self.wp_rr += 1
            self.load_panel("mla_wukv", j, c, slot, kc=4)
            for hh in range(4):
                def cons(bank, hh=hh):
                    ps = self.ps[bank]
                    s.act(lambda e: e.copy(self.qkT[:, hh, :], ps[:, :]), r=[("ps", bank)], w=["qkT"])
                self.proj_feat(ckvT, ("act", 4), 512, 4, slot, hh * 128, cons)
            s.dma("sp", self.kt_loc[4 * c:4 * c + 4, :, t * TT:(t + 1) * TT].rearrange("u p t -> p u t"), self.qkT[:, :, :],
                  r=["qkT"], w=[("kt_loc", 4 * c + i) for i in range(4)])
        vl = self.v_loc.rearrange("(h n d) -> h n d", n=NT, d=128)
        for c in range(4):
            slot = self.wp_rr % 3
            self.wp_rr += 1
            self.load_panel("mla_wukv", j, 4 + c, slot, kc=4)

            def consv(b, bank):
                ps = self.ps[bank]
                s.act(lambda e: e.copy(self.vst[:, b, :], ps[:, :]), r=[("ps", bank)], w=[("vst", b)])
            self.proj_tok(ckvT, ("act", 4), 4, 4, slot, 512, consv)
            for hd in range(4):
                s.dma("sp", vl[4 * c + hd, t * TT:(t + 1) * TT, :].rearrange("(b p) d -> p b d", p=128),
                      self.vst[:, :, hd * 128:(hd + 1) * 128],
                      r=[("vst", b) for b in range(4)], w=[("v_loc", 4 * c + hd)])

    def exchange(self, L):
        s, NT = self.s, self.NT
        rg = [[0, 1, 2, 3], [4, 5, 6, 7]]
        mla = (L % 2 == 1)

        def ag(src, dst, rd, wr):
            s.cc(lambda e: e.collective_compute("AllGather", ALU.bypass, replica_groups=rg,
                                                ins=[src.opt()], outs=[dst.opt()]), r=rd, w=wr)
        if mla:
            ag(self.kr_loc, self.kr_all, ["kr_loc"], ["kr_all"])
        for u in range(16):
            ag(self.kt_loc[u], self.kt_all[u], [("kt_loc", u)], [("kt_all", u)])
            if not mla and u % 2 == 0:
                h = u // 2
                ch = self.vch_da
                vl = self.v_loc.rearrange("(h n d) -> h n d", n=NT, d=256)
                va = self.v_all.rearrange("(h c r n d) -> h c (r n) d", c=NT // ch, r=4, n=ch, d=256)
                for c2 in range(NT // ch):
                    ag(vl[h, c2 * ch:(c2 + 1) * ch, :], va[h, c2], [("v_loc", h)], [("v_all", h)])
            if mla:
                ch = self.vch_mla
                vl = self.v_loc.rearrange("(h n d) -> h n d", n=NT, d=128)
                va = self.v_all.rearrange("(h c r n d) -> h c (r n) d", c=NT // ch, r=4, n=ch, d=128)
                for c2 in range(NT // ch):
                    ag(vl[u, c2 * ch:(c2 + 1) * ch, :], va[u, c2], [("v_loc", u)], [("v_all", u)])

    def attention(self, L):
        s, NT, NB = self.s, self.NT, self.NB
        mla = (L % 2 == 1)
        j = L // 2
        NQT = NT // 512
        dv = 128 if mla else 256
        V = self.Vml if mla else self.Vda
        sc = 1.0 / math.sqrt(192.0 if mla else 128.0)
        KT, QT, KR, QR, PT = self.KT, self.QT, self.KR, self.QR, self.PT
        s.pool(lambda e: e.memset(V[:, :, :, dv:dv + 1], 1.0), w=["Vones"])
        if mla:
            s.dma("sp", KR[:, :, :], self.kr_all.rearrange("(r p) t -> p r t", p=128), r=["kr_all"], w=["KR"])
        else:
            li = lambda_init(L)
            lamt, lamw, lam = self.lamt, self.lamw, self.lam
            s.dma("sp", lamt[:, :], self.lam_in[j].partition_broadcast(128), w=["lamt"])
            lv = lamt[:, :].rearrange("p (a b d) -> p a b d", a=2, b=2)
            s.dve(lambda e: e.tensor_tensor(lamw[:, :].rearrange("p (a d) -> p a d", a=2), lv[:, :, 0, :], lv[:, :, 1, :],
                                            op=ALU.mult), r=["lamt"], w=["lamw"])
            s.dve(lambda e: e.reduce_sum(lam[:, 0:2], lamw[:, :].rearrange("p (a d) -> p a d", a=2), axis=AX.X),
                  r=["lamw"], w=[("lam", 0)])
            s.act(lambda e: e.activation(lam[:, 2:4], lam[:, 0:2], AF.Exp), r=[("lam", 0)], w=[("lam", 1)])
            s.dve(lambda e: e.tensor_tensor(lam[:, 4:5], lam[:, 2:3], lam[:, 3:4], op=ALU.subtract),
                  r=[("lam", 1)], w=[("lam", 2)])
            s.dve(lambda e: e.tensor_scalar(lam[:, 5:6], lam[:, 4:5], li, -1.0, op0=ALU.add, op1=ALU.mult),
                  r=[("lam", 2)], w=[("lam", 3)])
        kb_global = [0]
        osb_rr = [0]
        for u in range(16):
            hb = (u % 2) * 64
            s.dma("sp", KT[:, :, :], self.kt_all[u].rearrange("(r p) t -> p r t", p=128), r=[("kt_all", u)], w=["KT"])
            s.dma("sp", QT[:, :], self.q_loc[u], r=[("q_loc", u)], w=["QT"])
            if mla:
                if u % 2 == 0:
                    s.dma("sp", QR[:, :], self.qr_loc[u // 2], r=[("qr_loc", u // 2)], w=["QR"])
                ch = self.vch_mla
                va = self.v_all.rearrange("(h c r n d) -> h c r n d", c=NT // ch, r=4, n=ch, d=128)
                mb = ch // 128
                for c2 in range(NT // ch):
                    for r in range(4):
                        s.dma("sp", V[:, r, c2 * mb:(c2 + 1) * mb, 0:128],
                              va[u, c2, r].rearrange("(m p) d -> p m d", p=128), r=[("v_all", u)], w=["V"])
            elif u % 2 == 0:
                ch = self.vch_da
                va = self.v_all.rearrange("(h c r n d) -> h c r n d", c=NT // ch, r=4, n=ch, d=256)
                mb = ch // 128
                for c2 in range(NT // ch):
                    for r in range(4):
                        s.dma("sp", V[:, r, c2 * mb:(c2 + 1) * mb, 0:256],
                              va[u // 2, c2, r].rearrange("(m p) d -> p m d", p=128), r=[("v_all", u // 2)], w=["V"])
            items = []
            for t in range(NQT):
                for r in range(4):
                    for m in range(4 * t + 4):
                        items.append((t, r, m))
            n = len(items)
            LA = 2

            def qk(idx):
                t, r, m = items[idx]
                i0 = max(0, m - 4 * t)
                ncols = 512 - 128 * i0
                q0 = t * 512 + 128 * i0
                kb = kb_global[0] + idx
                bank = kb % 4
                ps = self.ps[bank]
                pt = PT[kb % 4]
                s.pe(lambda e: e.matmul(ps[:, 0:ncols], lhsT=KT[:, r, m * 128:(m + 1) * 128], rhs=QT[:, q0:q0 + ncols],
                                        start=True, stop=(not mla)), r=["KT", "QT"], w=[("ps", bank)])
                if mla:
                    s.pe(lambda e, hb=hb: e.matmul(ps[:, 0:ncols], lhsT=KR[hb:hb + 64, r, m * 128:(m + 1) * 128],
                                                   rhs=QR[hb:hb + 64, q0:q0 + ncols], start=False, stop=True),
                         r=["KR", "QR"], w=[("ps", bank)])
                s.act(lambda e: e.activation(pt[:, 0:ncols], ps[:, 0:ncols], AF.Exp, scale=sc),
                      r=[("ps", bank)], w=[("PT", kb % 4)])
                if m >= 4 * t:
                    s.dve(lambda e: e.tensor_tensor(pt[:, 0:128], pt[:, 0:128], self.masks[:, r, :], op=ALU.mult),
                          r=[("PT", kb % 4), "masks"], w=[("PT", kb % 4)])

            def pv(idx):
                t, r, m = items[idx]
                i0 = max(0, m - 4 * t)
                kb = kb_global[0] + idx
                pt = PT[kb % 4]
                for i in range(i0, 4):
                    first = (r == 0 and m == 0)
                    last = (r == 3 and m == 4 * t + i)
                    ob = 4 + i
                    po = self.ps[ob]
                    s.pe(lambda e, i=i, po=po, first=first, last=last: e.matmul(
                        po[:, 0:dv + 1], lhsT=pt[:, (i - i0) * 128:(i - i0 + 1) * 128],
                        rhs=V[:, r, m, 0:dv + 1], start=first, stop=last),
                         r=[("PT", kb % 4), "V", "Vones"], w=[("ps", ob)])
                if r == 3 and m == 4 * t + 3:
                    finish(t)

            def finish(t):
                ob_i = osb_rr[0] % 2
                osb_rr[0] += 1
                osb = self.osb[ob_i]
                rz = self.rz
                for i in range(4):
                    po = self.ps[4 + i]
                    blk = 4 * t + i
                    s.dve(lambda e, po=po, i=i: e.reciprocal(rz[:, i:i + 1], po[:, dv:dv + 1]),
                          r=[("ps", 4 + i)], w=[("rz", i)])
                    if mla:
                        s.dve(lambda e, po=po, i=i: e.tensor_scalar(osb[:, i, 0:128], po[:, 0:128], rz[:, i:i + 1], None,
                                                                    op0=ALU.mult),
                              r=[("ps", 4 + i), ("rz", i)], w=[("osb", ob_i)])
                    elif u % 2 == 0:
                        s.dve(lambda e, po=po, i=i, blk=blk: e.tensor_scalar(self.O1n[:, blk, :], po[:, 0:256], rz[:, i:i + 1],
                                                                             None, op0=ALU.mult),
                              r=[("ps", 4 + i), ("rz", i)], w=[("O1n", blk)])
                    else:
                        s.dve(lambda e, i=i: e.tensor_tensor(rz[:, 4 + i:5 + i], rz[:, i:i + 1], self.lam[:, 5:6], op=ALU.mult),
                              r=[("rz", i), ("lam", 3)], w=[("rz", 4 + i)])
                        s.dve(lambda e, po=po, i=i, blk=blk: e.scalar_tensor_tensor(
                            osb[:, i, :], po[:, 0:256], rz[:, 4 + i:5 + i], self.O1n[:, blk, :],
                            op0=ALU.mult, op1=ALU.add),
                            r=[("ps", 4 + i), ("rz", 4 + i), ("O1n", blk)], w=[("osb", ob_i)])
                if mla:
                    s.dma("sp", self.ao[t * 512:(t + 1) * 512, u * 128:(u + 1) * 128].rearrange("(b p) d -> p b d", p=128),
                          osb[:, :, 0:128], r=[("osb", ob_i)], w=[("ao", t)])
                elif u % 2 == 1:
                    h = u // 2
                    s.dma("sp", self.ao[t * 512:(t + 1) * 512, h * 256:(h + 1) * 256].rearrange("(b p) d -> p b d", p=128),
                          osb[:, :, :], r=[("osb", ob_i)], w=[("ao", t)])

            for idx in range(n + LA):
                if idx < n:
                    qk(idx)
                if idx - LA >= 0:
                    pv(idx - LA)
            kb_global[0] += n

    def c_phase(self, L, t):
        s, NT = self.s, self.NT
        mla = (L % 2 == 1)
        j = L // 2
        xsrc = self.x_in if L == 0 else self.xs
        xres = "x_in" if L == 0 else ("xs", t)
        xt = self.xt
        s.dma("sp", xt[:, :, :], xsrc[t * TT:(t + 1) * TT, :].rearrange("(b p) d -> p b d", p=128),
              r=[xres], w=[("xt", b) for b in range(4)])
        aot = self.aot
        for b in range(4):
            s.dma("sp", aot[:, b, :], self.ao[t * TT + b * 128:t * TT + (b + 1) * 128, :],
                  r=[("ao", t)], w=[("act", 8 * b + i) for i in range(8)])
        hT = self.hT
        for b in range(4):
            ares = ("act", 8 * b)
            xn = self.xn[b % 2]
            xr = ("xn", b % 2)
            if not mla:
                av = aot[:, b, :].rearrange("p (h e) -> p h e", e=256)
                for h in range(8):
                    s.act(lambda e, h=h, av=av: e.activation(self.junk[:, 0:256], av[:, h, :], AF.Square,
                                                             accum_out=self.ss[:, 8 + h:9 + h]),
                          r=[ares], w=[("ss", 8 + h), "junk"])
                s.act(lambda e: e.activation(self.rstd[:, 8:16], self.ss[:, 8:16], AF.Ln, bias=self.epsb[:, 0:1], scale=1.0 / 256),
                      r=[("ss", 8 + h) for h in range(8)] + ["epsb"], w=[("rstd", 8)])
                s.act(lambda e: e.activation(self.rstd[:, 8:16], self.rstd[:, 8:16], AF.Exp, scale=-0.5),
                      r=[("rstd", 8)], w=[("rstd", 8)])
                rb = self.rstd[:, 8:16].unsqueeze(2).to_broadcast([128, 8, 256])
                s.dve(lambda e, av=av, xn=xn, rb=rb: e.tensor_tensor(xn[:, :].rearrange("p (h e) -> p h e", e=256), av, rb,
                                                                      op=ALU.mult),
                      r=[ares, ("rstd", 8)], w=[xr])
            else:
                s.dve(lambda e, xn=xn, b=b: e.tensor_copy(xn[:, :], aot[:, b, :]), r=[ares], w=[xr])
            for g in range(4):
                bank = self.next_bank()
                for k in range(4):
                    kc = 4 * g + k
                    self.transpose_to(bank, k * 128, xn[:, kc * 128:(kc + 1) * 128], rd=[xr])
                ps = self.ps[bank]
                dstv = hT[:, 4 * g:4 * g + 4, b * 128:(b + 1) * 128]
                psv = ps[:, :].rearrange("p (a t) -> p a t", t=128)
                if not mla:
                    c0 = 14 * 16 + j * 2
                    for k in range(4):
                        kc = 4 * g + k
                        gcol = self.colp[:, c0 + (kc % 2):c0 + (kc % 2) + 1]
                        s.dve(lambda e, k=k, gcol=gcol, dstv=dstv, psv=psv: e.tensor_scalar(
                            dstv[:, k, :], psv[:, k, :], gcol, 1.0 - lambda_init(L), op0=ALU.mult, op1=ALU.mult),
                            r=[("ps", bank), "colp"], w=["hT"])
                else:
                    s.act(lambda e, dstv=dstv, psv=psv: e.copy(dstv, psv), r=[("ps", bank)], w=["hT"])
        wname = "mla_wo" if mla else "da_wo"
        self.linear_residual(wname, j, hT, "hT", 16)
        self.norm_to_hT(xt, 4, (1 * 4 + L) * 16, lambda b: ("xt", b))
        slot = self.wp_rr % 3
        self.wp_rr += 1
        self.load_panel("ca_wq", L, 0, slot)
        qcT = self.qkT
        for h in range(4):
            def cons(bank, h=h):
                ps = self.ps[bank]
                s.act(lambda e: e.copy(qcT[:, h, :], ps[:, :]), r=[("ps", bank)], w=["qkT"])
            self.proj_feat(hT, "hT", 512, 16, slot, h * 128, cons)
        ocT = self.qk_tok
        ptc = self.vst
        scc = 1.0 / math.sqrt(128.0)
        for h in range(4):
            for mb in range(2):
                bank = self.next_bank()
                ps = self.ps[bank]
                s.pe(lambda e, ps=ps, mb=mb, h=h: e.matmul(ps[:, :], lhsT=self.kmT[:, h, mb * 128:(mb + 1) * 128],
                                                            rhs=qcT[:, h, :], start=True, stop=True),
                     r=["kmT", "qkT"], w=[("ps", bank)])
                s.act(lambda e, ps=ps, mb=mb: e.activation(ptc[:, mb, :], ps[:, :], AF.Exp, scale=scc),
                      r=[("ps", bank)], w=[("vst", mb)])
            bo = self.next_bank()
            bz = self.next_bank()
            po, pz = self.ps[bo], self.ps[bz]
            for mb in range(2):
                s.pe(lambda e, mb=mb, h=h, po=po: e.matmul(po[:, :], lhsT=self.vm[:, mb, h * 128:(h + 1) * 128],
                                                           rhs=ptc[:, mb, :], start=(mb == 0), stop=(mb == 1)),
                     r=["vm", ("vst", mb)], w=[("ps", bo)])
            for mb in range(2):
                s.pe(lambda e, mb=mb, pz=pz: e.matmul(pz[:, :], lhsT=self.ones_bf[:, :], rhs=ptc[:, mb, :],
                                                      start=(mb == 0), stop=(mb == 1)),
                     r=["ones_bf", ("vst", mb)], w=[("ps", bz)])
            stg = self.stg[h % 2]
            sr = ("stg", h % 2)
            s.dve(lambda e, stg=stg, pz=pz: e.reciprocal(stg[:, :], pz[:, :]), r=[("ps", bz)], w=[sr])
            s.dve(lambda e, stg=stg, po=po, h=h: e.tensor_tensor(ocT[:, h, :], po[:, :], stg[:, :], op=ALU.mult),
                  r=[("ps", bo), sr], w=[("qk_tok", h)])
        self.linear_residual("ca_wo", L, ocT, [("qk_tok", h) for h in range(4)], 4)
        self.norm_to_hT(xt, 4, (2 * 4 + L) * 16, lambda b: ("xt", b))
        actb = self.actb
        for fp in range(16):
            slot = self.wp_rr % 3
            self.wp_rr += 1
            self.load_panel("mlp_wup", L, fp, slot)
            for n_ in range(4):
                fc = fp * 4 + n_

                def cons(bank, fc=fc):
                    ps = self.ps[bank]
                    stg = self.stg[fc % 2]
                    sr = ("stg", fc % 2)
                    s.act(lambda e: e.activation(stg[:, :], ps[:, :], AF.Relu), r=[("ps", bank)], w=[sr])
                    s.pool(lambda e: e.tensor_tensor(actb[:, fc, :], stg[:, :], stg[:, :], op=ALU.mult),
                           r=[sr], w=[("act", fc)])
                self.proj_feat(hT, "hT", 512, 16, slot, n_ * 128, cons)
        for c in range(4):
            banks = [self.next_bank() for _ in range(4)]
            for g in range(4):
                slot = self.wp_rr % 3
                self.wp_rr += 1
                self.load_panel("mlp_wdown", L, c * 4 + g, slot)
                wp = self.WP[slot]
                for b in range(4):
                    ps = self.ps[banks[b]]
                    for k in range(16):
                        fc = g * 16 + k
                        s.pe(lambda e, ps=ps, fc=fc, b=b, k=k, wp=wp, g=g: e.matmul(
                            ps[:, :], lhsT=actb[:, fc, b * 128:(b + 1) * 128], rhs=wp[:, k, :],
                            start=(g == 0 and k == 0), stop=(g == 3 and k == 15)),
                            r=[("act", fc), ("WP", slot)], w=[("ps", banks[b])])
            for b in range(4):
                ps = self.ps[banks[b]]
                s.dve(lambda e, ps=ps, b=b, c=c: e.tensor_tensor(xt[:, b, c * 512:(c + 1) * 512], xt[:, b, c * 512:(c + 1) * 512],
                                                                 ps[:, :], op=ALU.add),
                      r=[("ps", banks[b]), ("xt", b)], w=[("xt", b)])

    def linear_residual(self, wname, li, lhs, lhs_res, kcn):
        s = self.s
        xt = self.xt
        for c in range(4):
            slot = self.wp_rr % 3
            self.wp_rr += 1
            self.load_panel(wname, li, c, slot, kc=kcn)

            def cons(b, bank, c=c):
                ps = self.ps[bank]
                s.dve(lambda e: e.tensor_tensor(xt[:, b, c * 512:(c + 1) * 512], xt[:, b, c * 512:(c + 1) * 512], ps[:, :],
                                                op=ALU.add),
                      r=[("ps", bank), ("xt", b)], w=[("xt", b)])
            self.proj_tok(lhs, lhs_res, 4, kcn, slot, 512, cons)

    def final_norm(self, t):
        s = self.s
        xt = self.xt
        g = self.hT_f32
        s.dma("sp", g[:, :], self.fin_in.partition_broadcast(128), r=["hT"], w=["hT"])
        for b in range(4):
            self.rms_rstd(xt[:, b, :], D, b, rd=[("xt", b)], junk_ap=self.xn[b % 2][:, :], junk_res=("xn", b % 2))
            s.dve(lambda e, b=b: e.scalar_tensor_tensor(xt[:, b, :], xt[:, b, :], self.rstd[:, b:b + 1], g[:, :],
                                                        op0=ALU.mult, op1=ALU.mult),
                  r=[("xt", b), ("rstd", b), "hT"], w=[("xt", b)])
        s.dma("sp", self.out[t * TT:(t + 1) * TT, :].rearrange("(b p) d -> p b d", p=128), xt[:, :, :],
              r=[("xt", b) for b in range(4)], w=[("out", t)])

    def cast_layer_c(self, L):
        import os
        j = L // 2
        sel = os.environ.get("CASTS", "wo,ca_wkv,ca_wq,ca_wo,mlp_wup,mlp_wdown").split(",")
        if "wo" in sel:
            self.cast_panels("mla_wo" if L % 2 else "da_wo", j)
        for nme in ("ca_wkv", "ca_wq", "ca_wo", "mlp_wup", "mlp_wdown"):
            if nme in sel:
                self.cast_panels(nme, L)

    def cast_layer_p(self, L):
        j = L // 2
        if L % 2 == 0:
            self.cast_panels("da_wqkv", j)
        else:
            self.cast_panels("mla_wdown", j)
            self.cast_panels("mla_wuq", j)
            self.cast_panels("mla_wukv", j)

    def p_phase(self, L, t):
        if L % 2 == 0:
            self.p_phase_da(L, t)
        else:
            self.p_phase_mla(L, t)

    def build(self):
        nc, s = self.nc, self.s
        self.declare()
        self.wp_rr = 0
        with ExitStack() as st:
            self.alloc(st)
            self.pibias = st.enter_context(nc.sbuf_tensor("pibias", [128, 1], F32))
            s.dve(lambda e: e.memset(self.pibias[:, :], PI), w=["pibias"])
            self.epsb = st.enter_context(nc.sbuf_tensor("epsb", [128, 1], F32))
            s.dve(lambda e: e.memset(self.epsb[:, :], EPS), w=["epsb"])
            self.cast_layer_p(0)
            if self.stage >= 1:
                self.setup()
            for t in range(self.NTT):
                if self.stage < 2:
                    break
                s.dma("sp", self.xt[:, :, :], self.x_in[t * TT:(t + 1) * TT, :].rearrange("(b p) d -> p b d", p=128),
                      r=["x_in"], w=[("xt", b) for b in range(4)])
                self.p_phase(0, t)
            import os
            if self.stage >= 3 and not os.environ.get("NOEX"):
                self.exchange(0)
            for L in range(self.depth):
                if self.stage < 4:
                    break
                self.cast_layer_c(L)
                if L + 1 < self.depth:
                    self.cast_layer_p(L + 1)
                s.barrier()
                if self.stage < 5:
                    break
                self.attention(L)
                s.barrier()
                if self.stage < 6:
                    break
                self.mem_kv(L)
                for t in range(self.NTT):
                    self.c_phase(L, t)
                    if L + 1 < self.depth:
                        s.dma("sp", self.xs[t * TT:(t + 1) * TT, :].rearrange("(b p) d -> p b d", p=128), self.xt[:, :, :],
                              r=[("xt", b) for b in range(4)], w=[("xs", t)])
                        self.p_phase(L + 1, t)
                    else:
                        self.final_norm(t)
                if L + 1 < self.depth:
                    self.exchange(L + 1)
            sems = {}
            for e in ("pe", "act", "dve", "pool"):
                sems[("eng", e)] = st.enter_context(nc.semaphore("s_" + e))
            for i in range(N_DMA_SEMS + N_SW_SEMS):
                sems[("dma", i)] = st.enter_context(nc.semaphore("s_dma%d" % i))
            for i in range(N_CC_SEMS):
                sems[("cc", i)] = st.enter_context(nc.semaphore("s_cc%d" % i))
            block = st.enter_context(nc.Block())
            s.finalize(nc, block, sems)
        return nc


def make_core_inputs(inputs, S, depth, b, j):
    NT = S // 4
    NB = NT // 128
    nd = (depth + 1) // 2
    nm = depth // 2
    f = lambda a: np.ascontiguousarray(np.asarray(a), dtype=np.float32)
    x = np.asarray(inputs["x"])[b, :S].reshape(S // 512, 4, 128, D)[:, j].reshape(NT, D)
    pos = np.asarray(inputs["positions"])[b, :S].reshape(S // 512, 4, 128)[:, j]
    colv = []
    for nme in ("attn_norm", "cross_norm", "mlp_norm"):
        a = np.asarray(inputs[nme])
        for L in range(4):
            colv.append(a[min(L, a.shape[0] - 1)].reshape(16, 128).T)
    colv.append(np.asarray(inputs["mem_norm"]).reshape(16, 128).T)
    colv.append(np.asarray(inputs["final_norm"]).reshape(16, 128).T)
    for jj in range(nd):
        colv.append(np.asarray(inputs["da_subln"])[jj].reshape(2, 128).T)
    for jj in range(max(nm, 1)):
        if nm:
            colv.append(np.asarray(inputs["mla_q_norm"])[jj].reshape(4, 128).T)
            colv.append(np.asarray(inputs["mla_kv_norm"])[jj].reshape(4, 128).T)
        else:
            colv.append(np.zeros((128, 8), np.float32))
    colp = np.concatenate(colv, axis=1)
    kk = np.arange(128)[:, None]
    qq = np.arange(128)[None, :]
    masks = np.zeros((128, 4, 128), np.float32)
    for r in range(4):
        if r < j:
            masks[:, r, :] = 1.0
        elif r == j:
            masks[:, r, :] = (kk <= qq).astype(np.float32)
    invf = np.zeros((128, 48), np.float32)
    invf[:, 0:16] = (THETA ** (-np.arange(0, 32, 2, dtype=np.float32) / 32)).astype(np.float32)[None, :]
    invf[:, 16:48] = (THETA ** (-np.arange(0, 64, 2, dtype=np.float32) / 64)).astype(np.float32)[None, :]
    m = {
        "x": f(x),
        "pos": np.ascontiguousarray(pos.T.astype(np.int32)),
        "mem": f(np.asarray(inputs["mem"])[b]),
        "colp": f(colp),
        "ident": np.eye(128, dtype=np.float32),
        "masks": masks,
        "invf": invf,
        "final_norm": f(inputs["final_norm"]),
        "da_lambda": f(np.asarray(inputs["da_lambda"])[:nd].reshape(nd, 512)),
        "da_wqkv": f(np.asarray(inputs["da_wqkv"])[:nd]),
        "da_wo": f(np.asarray(inputs["da_wo"])[:nd]),
        "ca_wq": f(np.asarray(inputs["ca_wq"])[:depth]),
        "ca_wkv": f(np.asarray(inputs["ca_wkv"])[:depth]),
        "ca_wo": f(np.asarray(inputs["ca_wo"])[:depth]),
        "mlp_wup": f(np.asarray(inputs["mlp_wup"])[:depth]),
        "mlp_wdown": f(np.asarray(inputs["mlp_wdown"])[:depth]),
    }
    if nm:
        m.update({
            "mla_wdown": f(np.asarray(inputs["mla_wdown"])[:nm]),
            "mla_wuq": f(np.asarray(inputs["mla_wuq"])[:nm]),
            "mla_wukv": f(np.asarray(inputs["mla_wukv"])[:nm]),
            "mla_wo": f(np.asarray(inputs["mla_wo"])[:nm]),
        })
    return m


def run(inputs, S, depth, trace=False, stage=99):
    bld = Builder(S, depth, stage)
    nc = bld.build()
    in_maps = []
    shared = None
    for c in range(8):
        b, j = c // 4, c % 4
        m = make_core_inputs(inputs, S, depth, b, j)
        if shared is None:
            shared = {k: v for k, v in m.items() if k not in ("x", "pos", "mem", "masks")}
        else:
            for k in shared:
                m[k] = shared[k]
        in_maps.append(m)
    res = run_bass_kernel_spmd(nc, in_maps, core_ids=list(range(8)), trace=trace)
    NT = S // 4
    out = np.zeros((2, S, D), np.float32)
    for c in range(8):
        b, j = c // 4, c % 4
        o = np.asarray(res.results[c]["out"]).reshape(S // 512, 128, D)
        out[b].reshape(S // 512, 4, 128, D)[:, j] = o
    return out, res


def kernel(**inputs):
    S = int(np.asarray(inputs["x"]).shape[1])
    out, _ = run(inputs, S, DEPTH)
    return out
```

```python
import math
from contextlib import ExitStack

import numpy as np
import concourse.bass as bass
import concourse.mybir as mybir
from concourse.bass_utils import run_bass_kernel_spmd

F32 = mybir.dt.float32
BF16 = mybir.dt.bfloat16
I32 = mybir.dt.int32
AF = mybir.ActivationFunctionType
ALU = mybir.AluOpType
AX = mybir.AxisListType

D = 2048
DEPTH = 4
NMEM = 256
EPS = 1e-6
THETA = 500000.0
TT = 512
ARENA_BYTES = 184 * 1024
N_DMA_SEMS = 32
N_SW_SEMS = 12
N_CC_SEMS = 36
PI = math.pi


class Op:
    __slots__ = ("eng", "fn", "deps", "kind", "needs_inc", "sem", "val", "idx")

    def __init__(self, eng, fn, kind):
        self.eng = eng
        self.fn = fn
        self.kind = kind
        self.deps = []
        self.needs_inc = False
        self.sem = None
        self.val = None


class Sched:
    ENGS = ("pe", "act", "dve", "pool", "sp")

    def __init__(self):
        self.ops = {e: [] for e in self.ENGS}
        self.last_w = {}
        self.readers = {}
        self.dma_rr = 0
        self.swdma_rr = 0
        self.async_cnt = {}
        self.cc_rr = 0
        self.dma_last = {}
        self.cc_last = {}
        self.all_async = []
        self.barrier_deps = {e: [] for e in self.ENGS}

    def _emit(self, eng, fn, reads, writes, kind, sbuf=True):
        op = Op(eng, fn, kind)
        deps = []
        for r in reads:
            w = self.last_w.get(r)
            if w is not None:
                deps.append(w)
        for r in writes:
            w = self.last_w.get(r)
            if w is not None:
                deps.append(w)
            deps.extend(self.readers.get(r, ()))
        if self.barrier_deps[eng]:
            deps.extend(self.barrier_deps[eng])
            self.barrier_deps[eng] = []
        if kind == "d":
            if eng == "pool":
                s = N_DMA_SEMS + (self.swdma_rr % N_SW_SEMS)
                self.swdma_rr += 1
            else:
                s = self.dma_rr % N_DMA_SEMS
                self.dma_rr += 1
            prev = self.dma_last.get(s)
            if prev is not None:
                deps.append(prev)
            self.dma_last[s] = op
            op.sem = ("dma", s)
            op.needs_inc = True
            self.async_cnt[op.sem] = self.async_cnt.get(op.sem, 0) + 16
            op.val = self.async_cnt[op.sem]
            if sbuf:
                self.all_async.append(op)
        elif kind == "x":
            s = self.cc_rr % N_CC_SEMS
            self.cc_rr += 1
            prev = self.cc_last.get(s)
            if prev is not None:
                deps.append(prev)
            self.cc_last[s] = op
            op.sem = ("cc", s)
            op.needs_inc = True
            self.async_cnt[op.sem] = self.async_cnt.get(op.sem, 0) + 1
            op.val = self.async_cnt[op.sem]
        else:
            op.sem = ("eng", eng)
        seen = set()
        for d in deps:
            if d is op or id(d) in seen:
                continue
            seen.add(id(d))
            if d.kind == "c" and d.eng == eng and eng == "pe":
                continue
            op.deps.append(d)
            d.needs_inc = True
        for r in reads:
            self.readers.setdefault(r, []).append(op)
        for r in writes:
            self.last_w[r] = op
            self.readers[r] = []
        self.ops[eng].append(op)
        return op

    def pe(self, fn, r=(), w=()):
        return self._emit("pe", fn, r, w, "c")

    def act(self, fn, r=(), w=()):
        return self._emit("act", fn, r, w, "c")

    def dve(self, fn, r=(), w=()):
        return self._emit("dve", fn, r, w, "c")

    def pool(self, fn, r=(), w=()):
        return self._emit("pool", fn, r, w, "c")

    def dma(self, q, out, in_, r=(), w=(), sbuf=True, **kw):
        return self._emit(q, lambda e: e.dma_start(out=out, in_=in_, **kw), r, w, "d", sbuf=sbuf)

    def cc(self, fn, r=(), w=()):
        return self._emit("pool", fn, r, w, "x")

    def barrier(self):
        deps = []
        for e in self.ENGS:
            if self.ops[e]:
                deps.append(self.ops[e][-1])
        deps.extend(self.all_async)
        self.all_async = []
        for e in self.ENGS:
            self.barrier_deps[e] = list(deps)

    def finalize(self, nc, block, sems):
        cnt = {}
        for e in self.ENGS:
            for op in self.ops[e]:
                if op.needs_inc and op.kind == "c":
                    cnt[op.sem] = cnt.get(op.sem, 0) + 1
                    op.val = cnt[op.sem]
        final_vals = dict(cnt)
        final_vals.update(self.async_cnt)

        def run(engname, eng):
            waited = {}
            for op in self.ops[engname]:
                need = {}
                for d in op.deps:
                    if need.get(d.sem, 0) < d.val:
                        need[d.sem] = d.val
                for s, v in need.items():
                    if waited.get(s, 0) >= v:
                        continue
                    eng.wait_ge(sems[s], v)
                    waited[s] = v
                ins = op.fn(eng)
                if op.needs_inc:
                    ins.then_inc(sems[op.sem], 16 if op.kind == "d" else 1)
            if engname == "sp":
                for s, v in final_vals.items():
                    if waited.get(s, 0) < v:
                        eng.wait_ge(sems[s], v)

        block.tensor(lambda e: run("pe", e))
        block.scalar(lambda e: run("act", e))
        block.vector(lambda e: run("dve", e))
        block.gpsimd(lambda e: run("pool", e))
        block.sync(lambda e: run("sp", e))


HT4 = [("hT", b) for b in range(4)]


def _rl(x):
    return list(x) if isinstance(x, list) else [x]


def lambda_init(i):
    return 0.8 - 0.6 * math.exp(-0.3 * i)


class Builder:
    def __init__(self, S, depth=DEPTH, stage=99):
        self.stage = stage
        self.S = S
        self.depth = depth
        self.NT = S // 4
        self.NB = self.NT // 128
        self.NTT = self.NT // TT
        assert self.NT % TT == 0
        self.nc = bass.Bass("TRN2", target_bir_lowering=False)
        self.s = Sched()
        self.bank_rr = 0

    def declare(self):
        nc, NT, NB = self.nc, self.NT, self.NB
        nd = (self.depth + 1) // 2
        nm = self.depth // 2
        self.nd, self.nm = nd, nm
        ei = lambda n, shp, dt=F32: nc.dram_tensor(n, shp, dt, kind="ExternalInput").ap()
        self.x_in = ei("x", [NT, D])
        self.pos_in = ei("pos", [128, NB], I32)
        self.mem_in = ei("mem", [NMEM, D])
        self.ncol = 14 * 16 + nd * 2 + max(nm, 1) * 8
        self.colp_in = ei("colp", [128, self.ncol])
        self.ident_in = ei("ident", [128, 128])
        self.masks_in = ei("masks", [128, 4, 128])
        self.invf_in = ei("invf", [128, 48])
        self.fin_in = ei("final_norm", [D])
        self.lam_in = ei("da_lambda", [max(nd, 1), 512])
        self.w_in = {
            "da_wqkv": ei("da_wqkv", [nd, D, 3 * D]),
            "da_wo": ei("da_wo", [nd, D, D]),
            "ca_wq": ei("ca_wq", [self.depth, D, 512]),
            "ca_wkv": ei("ca_wkv", [self.depth, D, 1024]),
            "ca_wo": ei("ca_wo", [self.depth, 512, D]),
            "mlp_wup": ei("mlp_wup", [self.depth, D, 4 * D]),
            "mlp_wdown": ei("mlp_wdown", [self.depth, 4 * D, D]),
        }
        if nm:
            self.w_in.update({
                "mla_wdown": ei("mla_wdown", [nm, D, 1088]),
                "mla_wuq": ei("mla_wuq", [nm, 512, 3072]),
                "mla_wukv": ei("mla_wukv", [nm, 512, 4096]),
                "mla_wo": ei("mla_wo", [nm, D, D]),
            })
        self.out = nc.dram_tensor("out", [NT, D], F32, kind="ExternalOutput").ap()

        it = lambda n, shp, dt=BF16: nc.dram_tensor(n, shp, dt).ap()
        self.pan = {
            "da_wqkv": it("p_da_wqkv", [nd, 12, 128, 16 * 512]),
            "da_wo": it("p_da_wo", [nd, 4, 128, 16 * 512]),
            "ca_wq": it("p_ca_wq", [self.depth, 1, 128, 16 * 512]),
            "ca_wkv": it("p_ca_wkv", [self.depth, 2, 128, 16 * 512]),
            "ca_wo": it("p_ca_wo", [self.depth, 4, 128, 4 * 512]),
            "mlp_wup": it("p_mlp_wup", [self.depth, 16, 128, 16 * 512]),
            "mlp_wdown": it("p_mlp_wdown", [self.depth, 16, 128, 16 * 512]),
        }
        if nm:
            self.pan.update({
                "mla_wdown": it("p_mla_wdown", [nm, 3, 128, 16 * 512]),
                "mla_wuq": it("p_mla_wuq", [nm, 6, 128, 4 * 512]),
                "mla_wukv": it("p_mla_wukv", [nm, 8, 128, 4 * 512]),
                "mla_wo": it("p_mla_wo", [nm, 4, 128, 16 * 512]),
            })
        self.xs = it("xs", [NT, D], F32)
        self.ao = it("ao", [NT, D], F32)
        self.ropetab = it("ropetab", [128, NB * 96], F32)
        self.memhT_d = it("memhT", [128, 16 * 256])
        self.q_loc = it("q_loc", [16, 128, NT])
        self.qr_loc = it("qr_loc", [8, 128, NT])
        self.kt_loc = it("kt_loc", [16, 128, NT])
        self.kt_all = it("kt_all", [16, 4 * 128, NT])
        self.kr_loc = it("kr_loc", [128, NT])
        self.kr_all = it("kr_all", [4 * 128, NT])
        self.vch_da = min(NT, 2048)
        self.vch_mla = min(NT, 4096)
        self.v_loc = it("v_loc", [NT * D])
        self.v_all = it("v_all", [4 * NT * D])

    def carve(self, off, shape, dt):
        n = 1
        for d_ in shape[1:]:
            n *= d_
        nb = n * (2 if dt == BF16 else 4)
        assert off % 4 == 0 and off + nb <= ARENA_BYTES, (off, shape)
        v = self.arena[:, off // 2:(off + nb) // 2]
        if dt != BF16:
            v = v.bitcast(dt)
        if len(shape) == 3:
            v = v.rearrange("p (a b) -> p a b", b=shape[2])
        elif len(shape) == 4:
            v = v.rearrange("p (a b c) -> p a b c", b=shape[2], c=shape[3])
        return v

    def alloc(self, st):
        nc, NT, NB = self.nc, self.NT, self.NB
        sb = lambda n, shp, dt: st.enter_context(nc.sbuf_tensor("sb_" + n, shp, dt))
        self.arena = sb("arena", [128, ARENA_BYTES // 2], BF16)
        K = 1024
        self.xt = self.carve(0, [128, 4, D], F32)
        self.hT = self.carve(32 * K, [128, 16, 512], BF16)
        self.hT_f32 = self.carve(32 * K, [128, D], F32)
        self.WP = [self.carve((48 + 16 * i) * K, [128, 16, 512], BF16) for i in range(3)]
        self.actb = self.carve(96 * K, [128, 64, 512], BF16)
        self.aot = self.carve(96 * K, [128, 4, D], F32)
        self.xn = [self.carve((160 + 4 * i) * K, [128, D], BF16) for i in range(2)]
        self.stg = [self.carve((168 + 2 * i) * K, [128, 512], F32) for i in range(2)]
        self.qk_tok = self.carve(172 * K, [128, 4, 512], BF16)
        self.qkT = self.carve(176 * K, [128, 4, 512], BF16)
        self.vst = self.carve(180 * K, [128, 4, 512], BF16)
        self.KT = self.carve(0, [128, 4, NT], BF16)
        self.Vda = self.carve(32 * K, [128, 4, NB, 257], BF16)
        self.Vml = self.carve(32 * K, [128, 4, NB, 129], BF16)
        self.QT = self.carve(97 * K, [128, NT], BF16)
        self.O1n = self.carve(105 * K, [128, NB, 256], F32)
        self.KR = self.carve(105 * K, [128, 4, NT], BF16)
        self.QR = self.carve(137 * K, [128, NT], BF16)
        self.PT = [self.carve((145 + i) * K, [128, 512], BF16) for i in range(4)]
        self.osb = [self.carve((149 + 4 * i) * K, [128, 4, 256], F32) for i in range(2)]
        self.tab = self.carve(0, [128, NB, 96], F32)
        self.ident = sb("ident", [128, 128], BF16)
        self.ident_f = sb("ident_f", [128, 128], F32)
        self.masks = sb("masks", [128, 4, 128], BF16)
        self.masks_f = sb("masks_f", [128, 4, 128], F32)
        self.colp = sb("colp", [128, self.ncol], F32)
        self.invf = sb("invf", [128, 48], F32)
        self.pos_i = sb("pos_i", [128, NB], I32)
        self.pos_f = sb("pos_f", [128, NB], F32)
        self.kmT = sb("kmT", [128, 4, 256], BF16)
        self.vm = sb("vm", [128, 2, 512], BF16)
        self.ones_bf = sb("ones_bf", [128, 128], BF16)
        self.rope = [sb("rope%d" % i, [128, 4, 96], F32) for i in range(2)]
        self.ss = sb("ss", [128, 16], F32)
        self.rstd = sb("rstd", [128, 16], F32)
        self.tmp = sb("tmp", [128, 4, 256], F32)
        self.lamt = sb("lamt", [128, 512], F32)
        self.lamw = sb("lamw", [128, 256], F32)
        self.lam = sb("lam", [128, 8], F32)
        self.rz = sb("rz", [128, 8], F32)
        self.junk = sb("junk", [128, 512], BF16)
        self.ps = [st.enter_context(nc.psum_tensor("ps%d" % i, [128, 512], F32)) for i in range(8)]

    def next_bank(self):
        b = self.bank_rr % 8
        self.bank_rr += 1
        return b

    def cast_panels(self, name, li, lo=0, hi=None):
        s = self.s
        w = self.w_in[name][li]
        pan = self.pan[name][li]
        npan = pan.shape[0]
        hi = npan if hi is None else hi
        for c in range(lo, hi):
            if name in ("da_wqkv", "da_wo", "mla_wo", "ca_wq", "ca_wkv", "mlp_wup"):
                src = w.rearrange("(kc p) n -> p kc n", p=128)[:, :, c * 512:(c + 1) * 512]
                dst = pan[c].rearrange("p (kc n) -> p kc n", n=512)
            elif name == "mlp_wdown":
                cc, g = c // 4, c % 4
                src = w[g * 2048:(g + 1) * 2048].rearrange("(kc p) n -> p kc n", p=128)[:, :, cc * 512:(cc + 1) * 512]
                dst = pan[c].rearrange("p (kc n) -> p kc n", n=512)
            elif name == "ca_wo":
                src = w.rearrange("(kc p) n -> p kc n", p=128)[:, :, c * 512:(c + 1) * 512]
                dst = pan[c].rearrange("p (kc n) -> p kc n", n=512)
            elif name == "mla_wdown":
                wdt = 512 if c < 2 else 64
                src = w.rearrange("(kc p) n -> p kc n", p=128)[:, :, c * 512:c * 512 + wdt]
                dst = pan[c].rearrange("p (kc n) -> p kc n", n=512)[:, :, 0:wdt]
            elif name == "mla_wuq":
                wv = w.rearrange("(kc p) (h e) -> p kc h e", p=128, e=192)
                if c < 4:
                    src = wv[:, :, 4 * c:4 * c + 4, 0:128]
                    dst = pan[c].rearrange("p (kc h e) -> p kc h e", h=4, e=128)
                else:
                    src = wv[:, :, 8 * (c - 4):8 * (c - 4) + 8, 128:192]
                    dst = pan[c].rearrange("p (kc h e) -> p kc h e", h=8, e=64)
            elif name == "mla_wukv":
                wv = w.rearrange("(kc p) (h e) -> p kc h e", p=128, e=256)
                if c < 4:
                    src = wv[:, :, 4 * c:4 * c + 4, 0:128]
                else:
                    src = wv[:, :, 4 * (c - 4):4 * (c - 4) + 4, 128:256]
                dst = pan[c].rearrange("p (kc h e) -> p kc h e", h=4, e=128)
            else:
                raise KeyError(name)
            if len(src.shape) == 4:
                for kc in range(src.shape[1]):
                    s.dma("pool", dst[:, kc], src[:, kc], r=(), w=[("pan", name, li, c), "castq"], sbuf=False)
                continue
            s.dma("pool", dst, src, r=(), w=[("pan", name, li, c), "castq"], sbuf=False)

    def load_panel(self, name, li, c, slot, kc=16, width=512):
        src = self.pan[name][li][c][:, 0:kc * 512].rearrange("p (kc n) -> p kc n", n=512)[:, :, 0:width]
        self.s.dma("sp", self.WP[slot][:, 0:kc, 0:width], src, r=[("pan", name, li, c)], w=[("WP", slot)])

    def transpose_to(self, bank, col0, src_ap, rd, ident=None, ncols=128):
        ps = self.ps[bank]
        idn = self.ident if ident is None else ident
        self.s.pe(lambda e: e.matmul(ps[:, col0:col0 + ncols], lhsT=src_ap, rhs=idn[:, 0:ncols],
                                     start=True, stop=True),
                  r=list(rd) + ["ident"], w=[("ps", bank)])

    def rms_rstd(self, src_ap, n, col, rd, junk_ap=None, junk_res="junk"):
        s = self.s
        ss, rstd = self.ss, self.rstd
        jk = self.junk[:, 0:n] if junk_ap is None else junk_ap
        s.act(lambda e: e.activation(jk, src_ap, AF.Square, accum_out=ss[:, col:col + 1]),
              r=list(rd), w=[("ss", col), junk_res])
        s.act(lambda e: e.activation(rstd[:, col:col + 1], ss[:, col:col + 1], AF.Ln, bias=self.epsb[:, 0:1], scale=1.0 / n),
              r=[("ss", col), "epsb"], w=[("rstd", col)])
        s.act(lambda e: e.activation(rstd[:, col:col + 1], rstd[:, col:col + 1], AF.Exp, scale=-0.5),
              r=[("rstd", col)], w=[("rstd", col)])

    def norm_to_hT(self, src, nblk, gcol0, src_res, dst=None, dst_res=None, width=512):
        s = self.s
        dst = self.hT if dst is None else dst
        dst_res = "hT" if dst_res is None else dst_res
        for b in range(nblk):
            xn = self.xn[b % 2]
            xr = ("xn", b % 2)
            sap = src[:, b, :]
            self.rms_rstd(sap, D, b, rd=[src_res(b)], junk_ap=xn[:, :], junk_res=xr)
            s.dve(lambda e, xn=xn, sap=sap, b=b: e.tensor_scalar(
                xn[:, :], sap, self.rstd[:, b:b + 1], None, op0=ALU.mult),
                r=[src_res(b), ("rstd", b)], w=[xr])
            for g in range(4):
                bank = self.next_bank()
                for k in range(4):
                    kc = 4 * g + k
                    self.transpose_to(bank, k * 128, xn[:, kc * 128:(kc + 1) * 128], rd=[xr])
                ps = self.ps[bank]
                gc = self.colp[:, gcol0 + 4 * g:gcol0 + 4 * g + 4].unsqueeze(2).to_broadcast([128, 4, 128])
                s.dve(lambda e, ps=ps, g=g, b=b, gc=gc: e.tensor_tensor(
                    dst[:, 4 * g:4 * g + 4, b * 128:(b + 1) * 128],
                    ps[:, :].rearrange("p (a t) -> p a t", t=128), gc, op=ALU.mult),
                    r=[("ps", bank), "colp"], w=[(dst_res, b)])

    def proj_tok(self, lhs, lhs_res, nblk, kcn, slot, width, consume):
        s = self.s
        wp = self.WP[slot]
        for b in range(nblk):
            bank = self.next_bank()
            ps = self.ps[bank]
            for kc in range(kcn):
                s.pe(lambda e, kc=kc, b=b, ps=ps: e.matmul(
                    ps[:, 0:width], lhsT=lhs[:, kc, b * 128:(b + 1) * 128], rhs=wp[:, kc, 0:width],
                    start=(kc == 0), stop=(kc == kcn - 1)),
                    r=([("hT", b)] if lhs_res == "hT" else _rl(lhs_res)) + [("WP", slot)], w=[("ps", bank)])
            consume(b, bank)

    def proj_feat(self, rhs, rhs_res, ntok, kcn, slot, m0, consume_bank):
        s = self.s
        wp = self.WP[slot]
        bank = self.next_bank()
        ps = self.ps[bank]
        for kc in range(kcn):
            s.pe(lambda e, kc=kc: e.matmul(
                ps[:, 0:ntok], lhsT=wp[:, kc, m0:m0 + 128], rhs=rhs[:, kc, 0:ntok],
                start=(kc == 0), stop=(kc == kcn - 1)),
                r=(HT4 if rhs_res == "hT" else _rl(rhs_res)) + [("WP", slot)], w=[("ps", bank)])
        consume_bank(bank)

    def setup(self):
        s, NB = self.s, self.NB
        s.dma("sp", self.ident_f[:, :], self.ident_in, w=["ident_f"])
        s.dma("sp", self.masks_f[:, :, :], self.masks_in, w=["masks_f"])
        s.dma("sp", self.colp[:, :], self.colp_in, w=["colp"])
        s.dma("sp", self.invf[:, :], self.invf_in, w=["invf"])
        s.dma("sp", self.pos_i[:, :], self.pos_in, w=["pos_i"])
        s.dve(lambda e: e.tensor_copy(self.ident[:, :], self.ident_f[:, :]), r=["ident_f"], w=["ident"])
        s.dve(lambda e: e.tensor_copy(self.masks[:, :, :], self.masks_f[:, :, :]), r=["masks_f"], w=["masks"])
        s.dve(lambda e: e.tensor_copy(self.pos_f[:, :], self.pos_i[:, :]), r=["pos_i"], w=["pos_f"])
        s.dve(lambda e: e.memset(self.ones_bf[:, :], 1.0), w=["ones_bf"])
        tab = self.tab
        K_ = 1024
        C1 = 6.28125
        C2 = 2 * PI - C1
        for (f0, nf, c0, s0) in ((0, 16, 0, 16), (16, 32, 32, 64)):
            tf = [self.carve((48 + 4 * i) * K_, [128, NB, 32], F32)[:, :, 0:nf] for i in range(6)]
            ti = self.carve((48 + 4 * 6) * K_, [128, NB, 32], I32)[:, :, 0:nf]
            a, q, kf, r_, y, w1 = tf
            pb = self.pos_f[:, :].unsqueeze(2).to_broadcast([128, NB, nf])
            fb = self.invf[:, f0:f0 + nf].unsqueeze(1).to_broadcast([128, NB, nf])
            s.dve(lambda e, a=a, pb=pb, fb=fb: e.tensor_tensor(a, pb, fb, op=ALU.mult), r=["pos_f", "invf", "tab"], w=["ta"])
            s.dve(lambda e, a=a, q=q: e.tensor_scalar(q, a, 1.0 / (2 * PI), None, op0=ALU.mult), r=["ta"], w=["tq"])
            s.dve(lambda e, ti=ti, q=q: e.tensor_copy(ti, q), r=["tq"], w=["ti"])
            s.dve(lambda e, ti=ti, kf=kf: e.tensor_copy(kf, ti), r=["ti"], w=["tk"])
            s.dve(lambda e, kf=kf, a=a, r_=r_: e.scalar_tensor_tensor(r_, kf, -C1, a, op0=ALU.mult, op1=ALU.add),
                  r=["tk", "ta"], w=["tr"])
            s.dve(lambda e, kf=kf, r_=r_: e.scalar_tensor_tensor(r_, kf, -C2, r_, op0=ALU.mult, op1=ALU.add),
                  r=["tk", "tr"], w=["tr"])
            for (shift, col) in ((PI / 2, c0), (0.0, s0)):
                s.dve(lambda e, y=y, r_=r_, shift=shift: e.tensor_scalar(y, r_, shift, None, op0=ALU.add),
                      r=["tr", "tab"], w=["ty"])
                s.dve(lambda e, y=y, w1=w1: e.tensor_scalar(w1, y, PI, -2 * PI, op0=ALU.is_gt, op1=ALU.mult),
                      r=["ty"], w=["tw"])
                s.dve(lambda e, y=y, w1=w1: e.tensor_tensor(y, y, w1, op=ALU.add), r=["ty", "tw"], w=["ty"])
                s.dve(lambda e, y=y, w1=w1: e.tensor_scalar(w1, y, -PI, 2 * PI, op0=ALU.is_lt, op1=ALU.mult),
                      r=["ty"], w=["tw"])
                s.dve(lambda e, y=y, w1=w1: e.tensor_tensor(y, y, w1, op=ALU.add), r=["ty", "tw"], w=["ty"])
                s.act(lambda e, y=y, col=col, nf=nf: e.activation(tab[:, :, col:col + nf], y, AF.Sin),
                      r=["ty"], w=["tab"])
        s.dma("sp", self.ropetab.rearrange("p (m f) -> p m f", f=96), tab[:, :, :], r=["tab"], w=["ropetab"])
        s.barrier()
        s.dma("sp", self.xt[:, 0:2, :], self.mem_in.rearrange("(b p) d -> p b d", p=128), w=[("xt", 0), ("xt", 1)])
        mh = self.hT[:, :, 0:256]
        self.norm_to_hT(self.xt, 2, 12 * 16, lambda b: ("xt", b))
        s.dma("sp", self.memhT_d.rearrange("p (kc t) -> p kc t", t=256), mh, r=HT4, w=["memhT_d"])

    def mem_kv(self, L):
        s = self.s
        mh = self.actb[:, 0:8, :].rearrange("p a b -> p (a b)").rearrange("p (kc t) -> p kc t", t=256)
        s.dma("sp", mh, self.memhT_d.rearrange("p (kc t) -> p kc t", t=256), r=["memhT_d"],
              w=[("act", i) for i in range(8)])
        mres = ("act", 0)
        self.load_panel("ca_wkv", L, 0, 0)
        self.load_panel("ca_wkv", L, 1, 1)
        kmT, vm = self.kmT, self.vm
        for h in range(4):
            def cons(bank, h=h):
                ps = self.ps[bank]
                s.act(lambda e: e.copy(kmT[:, h, :], ps[:, 0:256]), r=[("ps", bank)], w=["kmT"])
            self.proj_feat(mh, mres, 256, 16, 0, h * 128, cons)

        def consv(b, bank):
            ps = self.ps[bank]
            s.act(lambda e: e.copy(vm[:, b, :], ps[:, :]), r=[("ps", bank)], w=["vm"])
        self.proj_tok(mh, mres, 2, 16, 1, 512, consv)

    def load_rope(self, t):
        rp = self.rope[t % 2]
        self.s.dma("sp", rp[:, :, :],
                   self.ropetab.rearrange("p (m f) -> p m f", f=96)[:, 4 * t:4 * t + 4, :],
                   r=["ropetab"], w=[("rope", t % 2)])
        return rp, ("rope", t % 2)

    def rope_ops(self, dst, src, cos, sin, nh, half, rd, wr):
        s = self.s
        tmp = self.tmp
        cb = cos.unsqueeze(1).to_broadcast([128, nh, half])
        sn = sin.unsqueeze(1).to_broadcast([128, nh, half])
        x1, x2 = src[:, :, 0:half], src[:, :, half:2 * half]
        t = [tmp[:, i, 0:nh * half].rearrange("p (h f) -> p h f", f=half) for i in range(4)]
        s.dve(lambda e: e.tensor_tensor(t[0], x1, cb, op=ALU.mult), r=rd, w=[("tmp", 0)])
        s.dve(lambda e: e.tensor_tensor(t[1], x2, sn, op=ALU.mult), r=rd, w=[("tmp", 1)])
        s.dve(lambda e: e.tensor_tensor(t[2], x2, cb, op=ALU.mult), r=rd, w=[("tmp", 2)])
        s.dve(lambda e: e.tensor_tensor(t[3], x1, sn, op=ALU.mult), r=rd, w=[("tmp", 3)])
        s.dve(lambda e: e.tensor_tensor(dst[:, :, 0:half], t[0], t[1], op=ALU.subtract),
              r=[("tmp", 0), ("tmp", 1)], w=wr)
        s.dve(lambda e: e.tensor_tensor(dst[:, :, half:2 * half], t[2], t[3], op=ALU.add),
              r=[("tmp", 2), ("tmp", 3)], w=wr)

    def p_phase_da(self, L, t):
        s, NT = self.s, self.NT
        j = L // 2
        rp, rres = self.load_rope(t)
        self.norm_to_hT(self.xt, 4, (0 * 4 + L) * 16, lambda b: ("xt", b))
        for c in range(12):
            slot = self.wp_rr % 3
            self.wp_rr += 1
            self.load_panel("da_wqkv", j, c, slot)
            if c < 8:
                def cons(b, bank, c=c):
                    ps = self.ps[bank]
                    stg = self.stg[b % 2]
                    sr = ("stg", b % 2)
                    s.act(lambda e: e.copy(stg[:, :], ps[:, :]), r=[("ps", bank)], w=[sr])
                    sv = stg[:, :].rearrange("p (h e) -> p h e", e=128)
                    dv = self.qk_tok[:, b, :].rearrange("p (h e) -> p h e", e=128)
                    self.rope_ops(dv[:, :, 0:32], sv[:, :, 0:32], rp[:, b, 0:16], rp[:, b, 16:32], 4, 16,
                                  rd=[sr, rres], wr=[("qk_tok", b)])
                    s.pool(lambda e: e.tensor_copy(dv[:, :, 32:128], sv[:, :, 32:128]), r=[sr], w=[("qk_tok", b)])
                self.proj_tok(self.hT, "hT", 4, 16, slot, 512, cons)
                for hh in range(4):
                    bank = self.next_bank()
                    for b in range(4):
                        self.transpose_to(bank, b * 128, self.qk_tok[:, b, hh * 128:(hh + 1) * 128], rd=[("qk_tok", b)])
                    ps = self.ps[bank]
                    s.act(lambda e, ps=ps, hh=hh: e.copy(self.qkT[:, hh, :], ps[:, :]), r=[("ps", bank)], w=["qkT"])
                u0 = (c % 4) * 4
                dstT = self.q_loc if c < 4 else self.kt_loc
                nm = "q_loc" if c < 4 else "kt_loc"
                s.dma("act", dstT[u0:u0 + 4, :, t * TT:(t + 1) * TT].rearrange("u p t -> p u t"), self.qkT[:, :, :],
                      r=["qkT"], w=[(nm, u0 + i) for i in range(4)])
            else:
                def consv(b, bank):
                    ps = self.ps[bank]
                    s.act(lambda e: e.copy(self.vst[:, b, :], ps[:, :]), r=[("ps", bank)], w=[("vst", b)])
                self.proj_tok(self.hT, "hT", 4, 16, slot, 512, consv)
                vl = self.v_loc.rearrange("(h n d) -> h n d", n=NT, d=256)
                h0 = 2 * (c - 8)
                for hd in range(2):
                    s.dma("act", vl[h0 + hd, t * TT:(t + 1) * TT, :].rearrange("(b p) d -> p b d", p=128),
                          self.vst[:, :, hd * 256:(hd + 1) * 256],
                          r=[("vst", b) for b in range(4)], w=[("v_loc", h0 + hd)])

    def p_phase_mla(self, L, t):
        s, NT = self.s, self.NT
        j = L // 2
        cb = 14 * 16 + self.nd * 2 + j * 8
        rp, rres = self.load_rope(t)
        self.norm_to_hT(self.xt, 4, (0 * 4 + L) * 16, lambda b: ("xt", b))
        cqT = self.actb[:, 0:4, :]
        ckvT = self.actb[:, 4:8, :]
        for c in range(3):
            slot = self.wp_rr % 3
            self.wp_rr += 1
            wdt = 512 if c < 2 else 64
            self.load_panel("mla_wdown", j, c, slot, width=wdt)
            if c < 2:
                dstT = cqT if c == 0 else ckvT
                dres = ("act", 0) if c == 0 else ("act", 4)

                def cons(b, bank, c=c, dstT=dstT, dres=dres):
                    ps = self.ps[bank]
                    col = 4 + b
                    self.rms_rstd(ps[:, :], 512, col, rd=[("ps", bank)])
                    xn = self.xn[b % 2]
                    xr = ("xn", b % 2)
                    s.dve(lambda e: e.tensor_scalar(xn[:, 0:512], ps[:, :], self.rstd[:, col:col + 1], None,
                                                    op0=ALU.mult),
                          r=[("ps", bank), ("rstd", col)], w=[xr])
                    bank2 = self.next_bank()
                    for k in range(4):
                        self.transpose_to(bank2, k * 128, xn[:, k * 128:(k + 1) * 128], rd=[xr])
                    ps2 = self.ps[bank2]
                    gc = self.colp[:, cb + 4 * c:cb + 4 * c + 4].unsqueeze(2).to_broadcast([128, 4, 128])
                    s.dve(lambda e: e.tensor_tensor(dstT[:, :, b * 128:(b + 1) * 128],
                                                    ps2[:, :].rearrange("p (a t) -> p a t", t=128), gc, op=ALU.mult),
                          r=[("ps", bank2), "colp"], w=[dres])
                self.proj_tok(self.hT, "hT", 4, 16, slot, 512, cons)
            else:
                def consr(b, bank):
                    ps = self.ps[bank]
                    stg = self.stg[b % 2]
                    sr = ("stg", b % 2)
                    s.act(lambda e: e.copy(stg[:, 0:64], ps[:, 0:64]), r=[("ps", bank)], w=[sr])
                    sv = stg[:, 0:64].rearrange("p (h e) -> p h e", e=64)
                    dv = self.qk_tok[:, b, 0:64].rearrange("p (h e) -> p h e", e=64)
                    self.rope_ops(dv, sv, rp[:, b, 32:64], rp[:, b, 64:96], 1, 32, rd=[sr, rres], wr=[("qk_tok", b)])
                    s.pool(lambda e: e.tensor_copy(self.qk_tok[:, b, 64:128], self.qk_tok[:, b, 0:64]),
                           r=[("qk_tok", b)], w=[("qk_tok", b)])
                self.proj_tok(self.hT, "hT", 4, 16, slot, 64, consr)
                bank = self.next_bank()
                for b in range(4):
                    self.transpose_to(bank, b * 128, self.qk_tok[:, b, 0:128], rd=[("qk_tok", b)])
                ps = self.ps[bank]
                s.act(lambda e, ps=ps: e.copy(self.qkT[:, 0, :], ps[:, :]), r=[("ps", bank)], w=["qkT"])
                s.dma("act", self.kr_loc[:, t * TT:(t + 1) * TT], self.qkT[:, 0, :], r=["qkT"], w=["kr_loc"])
        for c in range(4):
            slot = self.wp_rr % 3
            self.wp_rr += 1
            self.load_panel("mla_wuq", j, c, slot, kc=4)
            for hh in range(4):
                def cons(bank, hh=hh):
                    ps = self.ps[bank]
                    s.act(lambda e: e.copy(self.qkT[:, hh, :], ps[:, :]), r=[("ps", bank)], w=["qkT"])
                self.proj_feat(cqT, ("act", 0), 512, 4, slot, hh * 128, cons)
            s.dma("act", self.q_loc[4 * c:4 * c + 4, :, t * TT:(t + 1) * TT].rearrange("u p t -> p u t"), self.qkT[:, :, :],
                  r=["qkT"], w=[("q_loc", 4 * c + i) for i in range(4)])
        for c in range(2):
            slot = self.wp_rr % 3
            self.wp_rr += 1
            self.load_panel("mla_wuq", j, 4 + c, slot, kc=4)

            def consq(b, bank):
                ps = self.ps[bank]
                stg = self.stg[b % 2]
                sr = ("stg", b % 2)
                s.act(lambda e: e.copy(stg[:, :], ps[:, :]), r=[("ps", bank)], w=[sr])
                sv = stg[:, :].rearrange("p (h e) -> p h e", e=64)
                dv = self.qk_tok[:, b, :].rearrange("p (h e) -> p h e", e=64)
                self.rope_ops(dv, sv, rp[:, b, 32:64], rp[:, b, 64:96], 8, 32, rd=[sr, rres], wr=[("qk_tok", b)])
            self.proj_tok(cqT, ("act", 0), 4, 4, slot, 512, consq)
            for pr in range(4):
                bank = self.next_bank()
                for b in range(4):
                    self.transpose_to(bank, b * 128, self.qk_tok[:, b, pr * 128:(pr + 1) * 128], rd=[("qk_tok", b)])
                ps = self.ps[bank]
                s.act(lambda e, ps=ps, pr=pr: e.copy(self.qkT[:, pr, :], ps[:, :]), r=[("ps", bank)], w=["qkT"])
            s.dma("act", self.qr_loc[4 * c:4 * c + 4, :, t * TT:(t + 1) * TT].rearrange("u p t -> p u t"), self.qkT[:, :, :],
                  r=["qkT"], w=[("qr_loc", 4 * c + i) for i in range(4)])
        for c in range(4):
            slot = self.wp_rr % 3
            self.wp_rr += 1
            self.load_panel("mla_wukv", j, c, slot, kc=4)
            for hh in range(4):
                def cons(bank, hh=hh):
                    ps = self.ps[bank]
                    s.act(lambda e: e.copy(self.qkT[:, hh, :], ps[:, :]), r=[("ps", bank)], w=["qkT"])
                self.proj_feat(ckvT, ("act", 4), 512, 4, slot, hh * 128, cons)
            s.dma("act", self.kt_loc[4 * c:4 * c + 4, :, t * TT:(t + 1) * TT].rearrange("u p t -> p u t"), self.qkT[:, :, :],
                  r=["qkT"], w=[("kt_loc", 4 * c + i) for i in range(4)])
        vl = self.v_loc.rearrange("(h n d) -> h n d", n=NT, d=128)
        for c in range(4):
            slot = self.wp_rr % 3
            self.wp_rr += 1
            self.load_panel("mla_wukv", j, 4 + c, slot, kc=4)

            def consv(b, bank):
                ps = self.ps[bank]
                s.act(lambda e: e.copy(self.vst[:, b, :], ps[:, :]), r=[("ps", bank)], w=[("vst", b)])
            self.proj_tok(ckvT, ("act", 4), 4, 4, slot, 512, consv)
            for hd in range(4):
                s.dma("act", vl[4 * c + hd, t * TT:(t + 1) * TT, :].rearrange("(b p) d -> p b d", p=128),
                      self.vst[:, :, hd * 128:(hd + 1) * 128],
                      r=[("vst", b) for b in range(4)], w=[("v_loc", 4 * c + hd)])

    def exchange(self, L):
        s, NT = self.s, self.NT
        rg = [[0, 1, 2, 3], [4, 5, 6, 7]]
        mla = (L % 2 == 1)

        def ag(src, dst, rd, wr):
            s.cc(lambda e: e.collective_compute("AllGather", ALU.bypass, replica_groups=rg,
                                                ins=[src.opt()], outs=[dst.opt()]), r=rd, w=wr)
        if mla:
            ag(self.kr_loc, self.kr_all, ["kr_loc"], ["kr_all"])
        for u in range(16):
            ag(self.kt_loc[u], self.kt_all[u], [("kt_loc", u)], [("kt_all", u)])
            if not mla and u % 2 == 0:
                h = u // 2
                ch = self.vch_da
                vl = self.v_loc.rearrange("(h n d) -> h n d", n=NT, d=256)
                va = self.v_all.rearrange("(h c r n d) -> h c (r n) d", c=NT // ch, r=4, n=ch, d=256)
                for c2 in range(NT // ch):
                    ag(vl[h, c2 * ch:(c2 + 1) * ch, :], va[h, c2], [("v_loc", h)], [("v_all", h)])
            if mla:
                ch = self.vch_mla
                vl = self.v_loc.rearrange("(h n d) -> h n d", n=NT, d=128)
                va = self.v_all.rearrange("(h c r n d) -> h c (r n) d", c=NT // ch, r=4, n=ch, d=128)
                for c2 in range(NT // ch):
                    ag(vl[u, c2 * ch:(c2 + 1) * ch, :], va[u, c2], [("v_loc", u)], [("v_all", u)])

    def attention(self, L):
        s, NT, NB = self.s, self.NT, self.NB
        mla = (L % 2 == 1)
        j = L // 2
        NQT = NT // 512
        dv = 128 if mla else 256
        V = self.Vml if mla else self.Vda
        sc = 1.0 / math.sqrt(192.0 if mla else 128.0)
        KT, QT, KR, QR, PT = self.KT, self.QT, self.KR, self.QR, self.PT
        s.pool(lambda e: e.memset(V[:, :, :, dv:dv + 1], 1.0), w=["Vones"])
        if mla:
            s.dma("sp", KR[:, :, :], self.kr_all.rearrange("(r p) t -> p r t", p=128), r=["kr_all"], w=["KR"])
        else:
            li = lambda_init(L)
            lamt, lamw, lam = self.lamt, self.lamw, self.lam
            s.dma("sp", lamt[:, :], self.lam_in[j].partition_broadcast(128), w=["lamt"])
            lv = lamt[:, :].rearrange("p (a b d) -> p a b d", a=2, b=2)
            s.dve(lambda e: e.tensor_tensor(lamw[:, :].rearrange("p (a d) -> p a d", a=2), lv[:, :, 0, :], lv[:, :, 1, :],
                                            op=ALU.mult), r=["lamt"], w=["lamw"])
            s.dve(lambda e: e.reduce_sum(lam[:, 0:2], lamw[:, :].rearrange("p (a d) -> p a d", a=2), axis=AX.X),
                  r=["lamw"], w=[("lam", 0)])
            s.act(lambda e: e.activation(lam[:, 2:4], lam[:, 0:2], AF.Exp), r=[("lam", 0)], w=[("lam", 1)])
            s.dve(lambda e: e.tensor_tensor(lam[:, 4:5], lam[:, 2:3], lam[:, 3:4], op=ALU.subtract),
                  r=[("lam", 1)], w=[("lam", 2)])
            s.dve(lambda e: e.tensor_scalar(lam[:, 5:6], lam[:, 4:5], li, -1.0, op0=ALU.add, op1=ALU.mult),
                  r=[("lam", 2)], w=[("lam", 3)])
        kb_global = [0]
        osb_rr = [0]
        for u in range(16):
            hb = (u % 2) * 64
            s.dma("sp", KT[:, :, :], self.kt_all[u].rearrange("(r p) t -> p r t", p=128), r=[("kt_all", u)], w=["KT"])
            s.dma("sp", QT[:, :], self.q_loc[u], r=[("q_loc", u)], w=["QT"])
            if mla:
                if u % 2 == 0:
                    s.dma("sp", QR[:, :], self.qr_loc[u // 2], r=[("qr_loc", u // 2)], w=["QR"])
                ch = self.vch_mla
                va = self.v_all.rearrange("(h c r n d) -> h c r n d", c=NT // ch, r=4, n=ch, d=128)
                mb = ch // 128
                for c2 in range(NT // ch):
                    for r in range(4):
                        s.dma("sp", V[:, r, c2 * mb:(c2 + 1) * mb, 0:128],
                              va[u, c2, r].rearrange("(m p) d -> p m d", p=128), r=[("v_all", u)], w=[("V", r)])
            elif u % 2 == 0:
                ch = self.vch_da
                va = self.v_all.rearrange("(h c r n d) -> h c r n d", c=NT // ch, r=4, n=ch, d=256)
                mb = ch // 128
                for c2 in range(NT // ch):
                    for r in range(4):
                        s.dma("sp", V[:, r, c2 * mb:(c2 + 1) * mb, 0:256],
                              va[u // 2, c2, r].rearrange("(m p) d -> p m d", p=128), r=[("v_all", u // 2)], w=[("V", r)])
            items = []
            for t in range(NQT):
                for r in range(4):
                    for m in range(4 * t + 4):
                        items.append((t, r, m))
            n = len(items)
            LA = 2

            def qk(idx):
                t, r, m = items[idx]
                i0 = max(0, m - 4 * t)
                ncols = 512 - 128 * i0
                q0 = t * 512 + 128 * i0
                kb = kb_global[0] + idx
                bank = kb % 4
                ps = self.ps[bank]
                pt = PT[kb % 4]
                s.pe(lambda e: e.matmul(ps[:, 0:ncols], lhsT=KT[:, r, m * 128:(m + 1) * 128], rhs=QT[:, q0:q0 + ncols],
                                        start=True, stop=(not mla)), r=["KT", "QT"], w=[("ps", bank)])
                if mla:
                    s.pe(lambda e, hb=hb: e.matmul(ps[:, 0:ncols], lhsT=KR[hb:hb + 64, r, m * 128:(m + 1) * 128],
                                                   rhs=QR[hb:hb + 64, q0:q0 + ncols], start=False, stop=True),
                         r=["KR", "QR"], w=[("ps", bank)])
                s.act(lambda e: e.activation(pt[:, 0:ncols], ps[:, 0:ncols], AF.Exp, scale=sc),
                      r=[("ps", bank)], w=[("PT", kb % 4)])
                if m >= 4 * t:
                    s.dve(lambda e: e.tensor_tensor(pt[:, 0:128], pt[:, 0:128], self.masks[:, r, :], op=ALU.mult),
                          r=[("PT", kb % 4), "masks"], w=[("PT", kb % 4)])

            def pv(idx):
                t, r, m = items[idx]
                i0 = max(0, m - 4 * t)
                kb = kb_global[0] + idx
                pt = PT[kb % 4]
                for i in range(i0, 4):
                    first = (r == 0 and m == 0)
                    last = (r == 3 and m == 4 * t + i)
                    ob = 4 + i
                    po = self.ps[ob]
                    s.pe(lambda e, i=i, po=po, first=first, last=last: e.matmul(
                        po[:, 0:dv + 1], lhsT=pt[:, (i - i0) * 128:(i - i0 + 1) * 128],
                        rhs=V[:, r, m, 0:dv + 1], start=first, stop=last),
                         r=[("PT", kb % 4), ("V", r), "Vones"], w=[("ps", ob)])
                if r == 3 and m == 4 * t + 3:
                    finish(t)

            def finish(t):
                ob_i = osb_rr[0] % 2
                osb_rr[0] += 1
                osb = self.osb[ob_i]
                rz = self.rz
                for i in range(4):
                    po = self.ps[4 + i]
                    blk = 4 * t + i
                    s.dve(lambda e, po=po, i=i: e.reciprocal(rz[:, i:i + 1], po[:, dv:dv + 1]),
                          r=[("ps", 4 + i)], w=[("rz", i)])
                    if mla:
                        s.dve(lambda e, po=po, i=i: e.tensor_scalar(osb[:, i, 0:128], po[:, 0:128], rz[:, i:i + 1], None,
                                                                    op0=ALU.mult),
                              r=[("ps", 4 + i), ("rz", i)], w=[("osb", ob_i)])
                    elif u % 2 == 0:
                        s.dve(lambda e, po=po, i=i, blk=blk: e.tensor_scalar(self.O1n[:, blk, :], po[:, 0:256], rz[:, i:i + 1],
                                                                             None, op0=ALU.mult),
                              r=[("ps", 4 + i), ("rz", i)], w=[("O1n", blk)])
                    else:
                        s.dve(lambda e, i=i: e.tensor_tensor(rz[:, 4 + i:5 + i], rz[:, i:i + 1], self.lam[:, 5:6], op=ALU.mult),
                              r=[("rz", i), ("lam", 3)], w=[("rz", 4 + i)])
                        s.dve(lambda e, po=po, i=i, blk=blk: e.scalar_tensor_tensor(
                            osb[:, i, :], po[:, 0:256], rz[:, 4 + i:5 + i], self.O1n[:, blk, :],
                            op0=ALU.mult, op1=ALU.add),
                            r=[("ps", 4 + i), ("rz", 4 + i), ("O1n", blk)], w=[("osb", ob_i)])
                if mla:
                    s.dma("sp", self.ao[t * 512:(t + 1) * 512, u * 128:(u + 1) * 128].rearrange("(b p) d -> p b d", p=128),
                          osb[:, :, 0:128], r=[("osb", ob_i)], w=[("ao", t)])
                elif u % 2 == 1:
                    h = u // 2
                    s.dma("sp", self.ao[t * 512:(t + 1) * 512, h * 256:(h + 1) * 256].rearrange("(b p) d -> p b d", p=128),
                          osb[:, :, :], r=[("osb", ob_i)], w=[("ao", t)])

            for idx in range(n + LA):
                if idx < n:
                    qk(idx)
                if idx - LA >= 0:
                    pv(idx - LA)
            kb_global[0] += n

    def c_phase(self, L, t):
        s, NT = self.s, self.NT
        mla = (L % 2 == 1)
        j = L // 2
        xsrc = self.x_in if L == 0 else self.xs
        xres = "x_in" if L == 0 else ("xs", t)
        xt = self.xt
        s.dma("sp", xt[:, :, :], xsrc[t * TT:(t + 1) * TT, :].rearrange("(b p) d -> p b d", p=128),
              r=[xres], w=[("xt", b) for b in range(4)])
        aot = self.aot
        for b in range(4):
            s.dma("sp", aot[:, b, :], self.ao[t * TT + b * 128:t * TT + (b + 1) * 128, :],
                  r=[("ao", t)], w=[("act", 8 * b + i) for i in range(8)])
        hT = self.hT
        for b in range(4):
            ares = ("act", 8 * b)
            xn = self.xn[b % 2]
            xr = ("xn", b % 2)
            if not mla:
                av = aot[:, b, :].rearrange("p (h e) -> p h e", e=256)
                for h in range(8):
                    s.act(lambda e, h=h, av=av: e.activation(self.junk[:, 0:256], av[:, h, :], AF.Square,
                                                             accum_out=self.ss[:, 8 + h:9 + h]),
                          r=[ares], w=[("ss", 8 + h), "junk"])
                s.act(lambda e: e.activation(self.rstd[:, 8:16], self.ss[:, 8:16], AF.Ln, bias=self.epsb[:, 0:1], scale=1.0 / 256),
                      r=[("ss", 8 + h) for h in range(8)] + ["epsb"], w=[("rstd", 8)])
                s.act(lambda e: e.activation(self.rstd[:, 8:16], self.rstd[:, 8:16], AF.Exp, scale=-0.5),
                      r=[("rstd", 8)], w=[("rstd", 8)])
                rb = self.rstd[:, 8:16].unsqueeze(2).to_broadcast([128, 8, 256])
                s.dve(lambda e, av=av, xn=xn, rb=rb: e.tensor_tensor(xn[:, :].rearrange("p (h e) -> p h e", e=256), av, rb,
                                                                      op=ALU.mult),
                      r=[ares, ("rstd", 8)], w=[xr])
            else:
                s.dve(lambda e, xn=xn, b=b: e.tensor_copy(xn[:, :], aot[:, b, :]), r=[ares], w=[xr])
            for g in range(4):
                bank = self.next_bank()
                for k in range(4):
                    kc = 4 * g + k
                    self.transpose_to(bank, k * 128, xn[:, kc * 128:(kc + 1) * 128], rd=[xr])
                ps = self.ps[bank]
                dstv = hT[:, 4 * g:4 * g + 4, b * 128:(b + 1) * 128]
                psv = ps[:, :].rearrange("p (a t) -> p a t", t=128)
                if not mla:
                    c0 = 14 * 16 + j * 2
                    for k in range(4):
                        kc = 4 * g + k
                        gcol = self.colp[:, c0 + (kc % 2):c0 + (kc % 2) + 1]
                        s.dve(lambda e, k=k, gcol=gcol, dstv=dstv, psv=psv: e.tensor_scalar(
                            dstv[:, k, :], psv[:, k, :], gcol, 1.0 - lambda_init(L), op0=ALU.mult, op1=ALU.mult),
                            r=[("ps", bank), "colp"], w=[("hT", b)])
                else:
                    s.act(lambda e, dstv=dstv, psv=psv: e.copy(dstv, psv), r=[("ps", bank)], w=[("hT", b)])
        wname = "mla_wo" if mla else "da_wo"
        self.linear_residual(wname, j, hT, "hT", 16)
        self.norm_to_hT(xt, 4, (1 * 4 + L) * 16, lambda b: ("xt", b))
        slot = self.wp_rr % 3
        self.wp_rr += 1
        self.load_panel("ca_wq", L, 0, slot)
        qcT = self.qkT
        for h in range(4):
            def cons(bank, h=h):
                ps = self.ps[bank]
                s.act(lambda e: e.copy(qcT[:, h, :], ps[:, :]), r=[("ps", bank)], w=["qkT"])
            self.proj_feat(hT, "hT", 512, 16, slot, h * 128, cons)
        ocT = self.qk_tok
        ptc = self.vst
        scc = 1.0 / math.sqrt(128.0)
        for h in range(4):
            for mb in range(2):
                bank = self.next_bank()
                ps = self.ps[bank]
                s.pe(lambda e, ps=ps, mb=mb, h=h: e.matmul(ps[:, :], lhsT=self.kmT[:, h, mb * 128:(mb + 1) * 128],
                                                            rhs=qcT[:, h, :], start=True, stop=True),
                     r=["kmT", "qkT"], w=[("ps", bank)])
                s.act(lambda e, ps=ps, mb=mb: e.activation(ptc[:, mb, :], ps[:, :], AF.Exp, scale=scc),
                      r=[("ps", bank)], w=[("vst", mb)])
            bo = self.next_bank()
            bz = self.next_bank()
            po, pz = self.ps[bo], self.ps[bz]
            for mb in range(2):
                s.pe(lambda e, mb=mb, h=h, po=po: e.matmul(po[:, :], lhsT=self.vm[:, mb, h * 128:(h + 1) * 128],
                                                           rhs=ptc[:, mb, :], start=(mb == 0), stop=(mb == 1)),
                     r=["vm", ("vst", mb)], w=[("ps", bo)])
            for mb in range(2):
                s.pe(lambda e, mb=mb, pz=pz: e.matmul(pz[:, :], lhsT=self.ones_bf[:, :], rhs=ptc[:, mb, :],
                                                      start=(mb == 0), stop=(mb == 1)),
                     r=["ones_bf", ("vst", mb)], w=[("ps", bz)])
            stg = self.stg[h % 2]
            sr = ("stg", h % 2)
            s.dve(lambda e, stg=stg, pz=pz: e.reciprocal(stg[:, :], pz[:, :]), r=[("ps", bz)], w=[sr])
            s.dve(lambda e, stg=stg, po=po, h=h: e.tensor_tensor(ocT[:, h, :], po[:, :], stg[:, :], op=ALU.mult),
                  r=[("ps", bo), sr], w=[("qk_tok", h)])
        self.linear_residual("ca_wo", L, ocT, [("qk_tok", h) for h in range(4)], 4)
        self.norm_to_hT(xt, 4, (2 * 4 + L) * 16, lambda b: ("xt", b))
        actb = self.actb
        for fp in range(16):
            slot = self.wp_rr % 3
            self.wp_rr += 1
            self.load_panel("mlp_wup", L, fp, slot)
            for n_ in range(4):
                fc = fp * 4 + n_

                def cons(bank, fc=fc):
                    ps = self.ps[bank]
                    stg = self.stg[fc % 2]
                    sr = ("stg", fc % 2)
                    s.act(lambda e: e.activation(stg[:, :], ps[:, :], AF.Relu), r=[("ps", bank)], w=[sr])
                    s.pool(lambda e: e.tensor_tensor(actb[:, fc, :], stg[:, :], stg[:, :], op=ALU.mult),
                           r=[sr], w=[("act", fc)])
                self.proj_feat(hT, "hT", 512, 16, slot, n_ * 128, cons)
        for c in range(4):
            banks = [self.next_bank() for _ in range(4)]
            for g in range(4):
                slot = self.wp_rr % 3
                self.wp_rr += 1
                self.load_panel("mlp_wdown", L, c * 4 + g, slot)
                wp = self.WP[slot]
                for b in range(4):
                    ps = self.ps[banks[b]]
                    for k in range(16):
                        fc = g * 16 + k
                        s.pe(lambda e, ps=ps, fc=fc, b=b, k=k, wp=wp, g=g: e.matmul(
                            ps[:, :], lhsT=actb[:, fc, b * 128:(b + 1) * 128], rhs=wp[:, k, :],
                            start=(g == 0 and k == 0), stop=(g == 3 and k == 15)),
                            r=[("act", fc), ("WP", slot)], w=[("ps", banks[b])])
            for b in range(4):
                ps = self.ps[banks[b]]
                s.dve(lambda e, ps=ps, b=b, c=c: e.tensor_tensor(xt[:, b, c * 512:(c + 1) * 512], xt[:, b, c * 512:(c + 1) * 512],
                                                                 ps[:, :], op=ALU.add),
                      r=[("ps", banks[b]), ("xt", b)], w=[("xt", b)])

    def linear_residual(self, wname, li, lhs, lhs_res, kcn):
        s = self.s
        xt = self.xt
        for c in range(4):
            slot = self.wp_rr % 3
            self.wp_rr += 1
            self.load_panel(wname, li, c, slot, kc=kcn)

            def cons(b, bank, c=c):
                ps = self.ps[bank]
                s.dve(lambda e: e.tensor_tensor(xt[:, b, c * 512:(c + 1) * 512], xt[:, b, c * 512:(c + 1) * 512], ps[:, :],
                                                op=ALU.add),
                      r=[("ps", bank), ("xt", b)], w=[("xt", b)])
            self.proj_tok(lhs, lhs_res, 4, kcn, slot, 512, cons)

    def final_norm(self, t):
        s = self.s
        xt = self.xt
        g = self.hT_f32
        s.dma("sp", g[:, :], self.fin_in.partition_broadcast(128), r=HT4, w=HT4)
        for b in range(4):
            self.rms_rstd(xt[:, b, :], D, b, rd=[("xt", b)], junk_ap=self.xn[b % 2][:, :], junk_res=("xn", b % 2))
            s.dve(lambda e, b=b: e.scalar_tensor_tensor(xt[:, b, :], xt[:, b, :], self.rstd[:, b:b + 1], g[:, :],
                                                        op0=ALU.mult, op1=ALU.mult),
                  r=[("xt", b), ("rstd", b)] + HT4, w=[("xt", b)])
        s.dma("act", self.out[t * TT:(t + 1) * TT, :].rearrange("(b p) d -> p b d", p=128), xt[:, :, :],
              r=[("xt", b) for b in range(4)], w=[("out", t)])

    def cast_layer_c(self, L):
        import os
        j = L // 2
        sel = os.environ.get("CASTS", "wo,ca_wkv,ca_wq,ca_wo,mlp_wup,mlp_wdown").split(",")
        if "wo" in sel:
            self.cast_panels("mla_wo" if L % 2 else "da_wo", j)
        for nme in ("ca_wkv", "ca_wq", "ca_wo", "mlp_wup", "mlp_wdown"):
            if nme in sel:
                self.cast_panels(nme, L)

    def cast_layer_p(self, L):
        j = L // 2
        if L % 2 == 0:
            self.cast_panels("da_wqkv", j)
        else:
            self.cast_panels("mla_wdown", j)
            self.cast_panels("mla_wuq", j)
            self.cast_panels("mla_wukv", j)

    def p_phase(self, L, t):
        if L % 2 == 0:
            self.p_phase_da(L, t)
        else:
            self.p_phase_mla(L, t)

    def build(self):
        nc, s = self.nc, self.s
        self.declare()
        self.wp_rr = 0
        with ExitStack() as st:
            self.alloc(st)
            self.pibias = st.enter_context(nc.sbuf_tensor("pibias", [128, 1], F32))
            s.dve(lambda e: e.memset(self.pibias[:, :], PI), w=["pibias"])
            self.epsb = st.enter_context(nc.sbuf_tensor("epsb", [128, 1], F32))
            s.dve(lambda e: e.memset(self.epsb[:, :], EPS), w=["epsb"])
            self.cast_layer_p(0)
            if self.stage >= 1:
                self.setup()
            for t in range(self.NTT):
                if self.stage < 2:
                    break
                s.dma("sp", self.xt[:, :, :], self.x_in[t * TT:(t + 1) * TT, :].rearrange("(b p) d -> p b d", p=128),
                      r=["x_in"], w=[("xt", b) for b in range(4)])
                self.p_phase(0, t)
            import os
            if self.stage >= 3 and not os.environ.get("NOEX"):
                self.exchange(0)
            for L in range(self.depth):
                if self.stage < 4:
                    break
                self.cast_layer_c(L)
                if L + 1 < self.depth:
                    self.cast_layer_p(L + 1)
                s.barrier()
                if self.stage < 5:
                    break
                self.attention(L)
                s.barrier()
                if self.stage < 6:
                    break
                self.mem_kv(L)
                for t in range(self.NTT):
                    self.c_phase(L, t)
                    if L + 1 < self.depth:
                        s.dma("act", self.xs[t * TT:(t + 1) * TT, :].rearrange("(b p) d -> p b d", p=128), self.xt[:, :, :],
                              r=[("xt", b) for b in range(4)], w=[("xs", t)])
                        self.p_phase(L + 1, t)
                    else:
                        self.final_norm(t)
                if L + 1 < self.depth:
                    self.exchange(L + 1)
            sems = {}
            for e in ("pe", "act", "dve", "pool"):
                sems[("eng", e)] = st.enter_context(nc.semaphore("s_" + e))
            for i in range(N_DMA_SEMS + N_SW_SEMS):
                sems[("dma", i)] = st.enter_context(nc.semaphore("s_dma%d" % i))
            for i in range(N_CC_SEMS):
                sems[("cc", i)] = st.enter_context(nc.semaphore("s_cc%d" % i))
            block = st.enter_context(nc.Block())
            s.finalize(nc, block, sems)
        return nc


def make_core_inputs(inputs, S, depth, b, j):
    NT = S // 4
    NB = NT // 128
    nd = (depth + 1) // 2
    nm = depth // 2
    f = lambda a: np.ascontiguousarray(np.asarray(a), dtype=np.float32)
    x = np.asarray(inputs["x"])[b, :S].reshape(S // 512, 4, 128, D)[:, j].reshape(NT, D)
    pos = np.asarray(inputs["positions"])[b, :S].reshape(S // 512, 4, 128)[:, j]
    colv = []
    for nme in ("attn_norm", "cross_norm", "mlp_norm"):
        a = np.asarray(inputs[nme])
        for L in range(4):
            colv.append(a[min(L, a.shape[0] - 1)].reshape(16, 128).T)
    colv.append(np.asarray(inputs["mem_norm"]).reshape(16, 128).T)
    colv.append(np.asarray(inputs["final_norm"]).reshape(16, 128).T)
    for jj in range(nd):
        colv.append(np.asarray(inputs["da_subln"])[jj].reshape(2, 128).T)
    for jj in range(max(nm, 1)):
        if nm:
            colv.append(np.asarray(inputs["mla_q_norm"])[jj].reshape(4, 128).T)
            colv.append(np.asarray(inputs["mla_kv_norm"])[jj].reshape(4, 128).T)
        else:
            colv.append(np.zeros((128, 8), np.float32))
    colp = np.concatenate(colv, axis=1)
    kk = np.arange(128)[:, None]
    qq = np.arange(128)[None, :]
    masks = np.zeros((128, 4, 128), np.float32)
    for r in range(4):
        if r < j:
            masks[:, r, :] = 1.0
        elif r == j:
            masks[:, r, :] = (kk <= qq).astype(np.float32)
    invf = np.zeros((128, 48), np.float32)
    invf[:, 0:16] = (THETA ** (-np.arange(0, 32, 2, dtype=np.float32) / 32)).astype(np.float32)[None, :]
    invf[:, 16:48] = (THETA ** (-np.arange(0, 64, 2, dtype=np.float32) / 64)).astype(np.float32)[None, :]
    m = {
        "x": f(x),
        "pos": np.ascontiguousarray(pos.T.astype(np.int32)),
        "mem": f(np.asarray(inputs["mem"])[b]),
        "colp": f(colp),
        "ident": np.eye(128, dtype=np.float32),
        "masks": masks,
        "invf": invf,
        "final_norm": f(inputs["final_norm"]),
        "da_lambda": f(np.asarray(inputs["da_lambda"])[:nd].reshape(nd, 512)),
        "da_wqkv": f(np.asarray(inputs["da_wqkv"])[:nd]),
        "da_wo": f(np.asarray(inputs["da_wo"])[:nd]),
        "ca_wq": f(np.asarray(inputs["ca_wq"])[:depth]),
        "ca_wkv": f(np.asarray(inputs["ca_wkv"])[:depth]),
        "ca_wo": f(np.asarray(inputs["ca_wo"])[:depth]),
        "mlp_wup": f(np.asarray(inputs["mlp_wup"])[:depth]),
        "mlp_wdown": f(np.asarray(inputs["mlp_wdown"])[:depth]),
    }
    if nm:
        m.update({
            "mla_wdown": f(np.asarray(inputs["mla_wdown"])[:nm]),
            "mla_wuq": f(np.asarray(inputs["mla_wuq"])[:nm]),
            "mla_wukv": f(np.asarray(inputs["mla_wukv"])[:nm]),
            "mla_wo": f(np.asarray(inputs["mla_wo"])[:nm]),
        })
    return m


def run(inputs, S, depth, trace=False, stage=99):
    bld = Builder(S, depth, stage)
    nc = bld.build()
    in_maps = []
    shared = None
    for c in range(8):
        b, j = c // 4, c % 4
        m = make_core_inputs(inputs, S, depth, b, j)
        if shared is None:
            shared = {k: v for k, v in m.items() if k not in ("x", "pos", "mem", "masks")}
        else:
            for k in shared:
                m[k] = shared[k]
        in_maps.append(m)
    res = run_bass_kernel_spmd(nc, in_maps, core_ids=list(range(8)), trace=trace)
    NT = S // 4
    out = np.zeros((2, S, D), np.float32)
    for c in range(8):
        b, j = c // 4, c % 4
        o = np.asarray(res.results[c]["out"]).reshape(S // 512, 128, D)
        out[b].reshape(S // 512, 4, 128, D)[:, j] = o
    return out, res


def kernel(**inputs):
    S = int(np.asarray(inputs["x"]).shape[1])
    out, _ = run(inputs, S, DEPTH)
    return out
```

```python
import math
from contextlib import ExitStack

import numpy as np
import concourse.bass as bass
import concourse.mybir as mybir
from concourse.bass_utils import run_bass_kernel_spmd

F32 = mybir.dt.float32
BF16 = mybir.dt.bfloat16
I32 = mybir.dt.int32
AF = mybir.ActivationFunctionType
ALU = mybir.AluOpType
AX = mybir.AxisListType

D = 2048
DEPTH = 4
NMEM = 256
EPS = 1e-6
THETA = 500000.0
TT = 512
ARENA_BYTES = 184 * 1024
N_DMA_SEMS = 32
N_SW_SEMS = 12
N_CC_SEMS = 36
PI = math.pi


class Op:
    __slots__ = ("eng", "fn", "deps", "kind", "needs_inc", "sem", "val", "idx")

    def __init__(self, eng, fn, kind):
        self.eng = eng
        self.fn = fn
        self.kind = kind
        self.deps = []
        self.needs_inc = False
        self.sem = None
        self.val = None


class Sched:
    ENGS = ("pe", "act", "dve", "pool", "sp")

    def __init__(self):
        self.ops = {e: [] for e in self.ENGS}
        self.last_w = {}
        self.readers = {}
        self.dma_rr = 0
        self.swdma_rr = 0
        self.async_cnt = {}
        self.cc_rr = 0
        self.dma_last = {}
        self.cc_last = {}
        self.all_async = []
        self.barrier_deps = {e: [] for e in self.ENGS}

    def _emit(self, eng, fn, reads, writes, kind, sbuf=True):
        op = Op(eng, fn, kind)
        deps = []
        for r in reads:
            w = self.last_w.get(r)
            if w is not None:
                deps.append(w)
        for r in writes:
            w = self.last_w.get(r)
            if w is not None:
                deps.append(w)
            deps.extend(self.readers.get(r, ()))
        if self.barrier_deps[eng]:
            deps.extend(self.barrier_deps[eng])
            self.barrier_deps[eng] = []
        if kind == "d":
            if eng == "pool":
                s = N_DMA_SEMS + (self.swdma_rr % N_SW_SEMS)
                self.swdma_rr += 1
            else:
                s = self.dma_rr % N_DMA_SEMS
                self.dma_rr += 1
            prev = self.dma_last.get(s)
            if prev is not None:
                deps.append(prev)
            self.dma_last[s] = op
            op.sem = ("dma", s)
            op.needs_inc = True
            self.async_cnt[op.sem] = self.async_cnt.get(op.sem, 0) + 16
            op.val = self.async_cnt[op.sem]
            if sbuf:
                self.all_async.append(op)
        elif kind == "x":
            s = self.cc_rr % N_CC_SEMS
            self.cc_rr += 1
            prev = self.cc_last.get(s)
            if prev is not None:
                deps.append(prev)
            self.cc_last[s] = op
            op.sem = ("cc", s)
            op.needs_inc = True
            self.async_cnt[op.sem] = self.async_cnt.get(op.sem, 0) + 1
            op.val = self.async_cnt[op.sem]
        else:
            op.sem = ("eng", eng)
        seen = set()
        for d in deps:
            if d is op or id(d) in seen:
                continue
            seen.add(id(d))
            if d.kind == "c" and d.eng == eng and eng == "pe":
                continue
            op.deps.append(d)
            d.needs_inc = True
        for r in reads:
            self.readers.setdefault(r, []).append(op)
        for r in writes:
            self.last_w[r] = op
            self.readers[r] = []
        self.ops[eng].append(op)
        return op

    def pe(self, fn, r=(), w=()):
        return self._emit("pe", fn, r, w, "c")

    def act(self, fn, r=(), w=()):
        return self._emit("act", fn, r, w, "c")

    def dve(self, fn, r=(), w=()):
        return self._emit("dve", fn, r, w, "c")

    def pool(self, fn, r=(), w=()):
        return self._emit("pool", fn, r, w, "c")

    def dma(self, q, out, in_, r=(), w=(), sbuf=True, **kw):
        return self._emit(q, lambda e: e.dma_start(out=out, in_=in_, **kw), r, w, "d", sbuf=sbuf)

    def cc(self, fn, r=(), w=()):
        return self._emit("pool", fn, r, w, "x")

    def barrier(self):
        deps = []
        for e in self.ENGS:
            if self.ops[e]:
                deps.append(self.ops[e][-1])
        deps.extend(self.all_async)
        self.all_async = []
        for e in self.ENGS:
            self.barrier_deps[e] = list(deps)

    def finalize(self, nc, block, sems):
        cnt = {}
        for e in self.ENGS:
            for op in self.ops[e]:
                if op.needs_inc and op.kind == "c":
                    cnt[op.sem] = cnt.get(op.sem, 0) + 1
                    op.val = cnt[op.sem]
        final_vals = dict(cnt)
        final_vals.update(self.async_cnt)

        def run(engname, eng):
            waited = {}
            for op in self.ops[engname]:
                need = {}
                for d in op.deps:
                    if need.get(d.sem, 0) < d.val:
                        need[d.sem] = d.val
                for s, v in need.items():
                    if waited.get(s, 0) >= v:
                        continue
                    eng.wait_ge(sems[s], v)
                    waited[s] = v
                ins = op.fn(eng)
                if op.needs_inc:
                    ins.then_inc(sems[op.sem], 16 if op.kind == "d" else 1)
            if engname == "sp":
                for s, v in final_vals.items():
                    if waited.get(s, 0) < v:
                        eng.wait_ge(sems[s], v)

        block.tensor(lambda e: run("pe", e))
        block.scalar(lambda e: run("act", e))
        block.vector(lambda e: run("dve", e))
        block.gpsimd(lambda e: run("pool", e))
        block.sync(lambda e: run("sp", e))


HT4 = [("hT", b) for b in range(4)]


def _rl(x):
    return list(x) if isinstance(x, list) else [x]


def lambda_init(i):
    return 0.8 - 0.6 * math.exp(-0.3 * i)


class Builder:
    def __init__(self, S, depth=DEPTH, stage=99):
        self.stage = stage
        self.S = S
        self.depth = depth
        self.NT = S // 4
        self.NB = self.NT // 128
        self.NTT = self.NT // TT
        assert self.NT % TT == 0
        self.nc = bass.Bass("TRN2", target_bir_lowering=False)
        self.s = Sched()
        self.bank_rr = 0

    def declare(self):
        nc, NT, NB = self.nc, self.NT, self.NB
        nd = (self.depth + 1) // 2
        nm = self.depth // 2
        self.nd, self.nm = nd, nm
        ei = lambda n, shp, dt=F32: nc.dram_tensor(n, shp, dt, kind="ExternalInput").ap()
        self.x_in = ei("x", [NT, D])
        self.pos_in = ei("pos", [128, NB], I32)
        self.mem_in = ei("mem", [NMEM, D])
        self.ncol = 14 * 16 + nd * 2 + max(nm, 1) * 8
        self.colp_in = ei("colp", [128, self.ncol])
        self.ident_in = ei("ident", [128, 128])
        self.masks_in = ei("masks", [128, 4, 128])
        self.invf_in = ei("invf", [128, 48])
        self.fin_in = ei("final_norm", [D])
        self.lam_in = ei("da_lambda", [max(nd, 1), 512])
        self.w_in = {
            "da_wqkv": ei("da_wqkv", [nd, D, 3 * D]),
            "da_wo": ei("da_wo", [nd, D, D]),
            "ca_wq": ei("ca_wq", [self.depth, D, 512]),
            "ca_wkv": ei("ca_wkv", [self.depth, D, 1024]),
            "ca_wo": ei("ca_wo", [self.depth, 512, D]),
            "mlp_wup": ei("mlp_wup", [self.depth, D, 4 * D]),
            "mlp_wdown": ei("mlp_wdown", [self.depth, 4 * D, D]),
        }
        if nm:
            self.w_in.update({
                "mla_wdown": ei("mla_wdown", [nm, D, 1088]),
                "mla_wuq": ei("mla_wuq", [nm, 512, 3072]),
                "mla_wukv": ei("mla_wukv", [nm, 512, 4096]),
                "mla_wo": ei("mla_wo", [nm, D, D]),
            })
        self.out = nc.dram_tensor("out", [NT, D], F32, kind="ExternalOutput").ap()

        it = lambda n, shp, dt=BF16: nc.dram_tensor(n, shp, dt).ap()
        self.pan = {
            "da_wqkv": it("p_da_wqkv", [nd, 12, 128, 16 * 512]),
            "da_wo": it("p_da_wo", [nd, 4, 128, 16 * 512]),
            "ca_wq": it("p_ca_wq", [self.depth, 1, 128, 16 * 512]),
            "ca_wkv": it("p_ca_wkv", [self.depth, 2, 128, 16 * 512]),
            "ca_wo": it("p_ca_wo", [self.depth, 4, 128, 4 * 512]),
            "mlp_wup": it("p_mlp_wup", [self.depth, 16, 128, 16 * 512]),
            "mlp_wdown": it("p_mlp_wdown", [self.depth, 16, 128, 16 * 512]),
        }
        if nm:
            self.pan.update({
                "mla_wdown": it("p_mla_wdown", [nm, 3, 128, 16 * 512]),
                "mla_wuq": it("p_mla_wuq", [nm, 6, 128, 4 * 512]),
                "mla_wukv": it("p_mla_wukv", [nm, 8, 128, 4 * 512]),
                "mla_wo": it("p_mla_wo", [nm, 4, 128, 16 * 512]),
            })
        self.xs = it("xs", [NT, D], F32)
        self.ao = it("ao", [NT, D], F32)
        self.ropetab = it("ropetab", [128, NB * 96], F32)
        self.memhT_d = it("memhT", [128, 16 * 256])
        self.q_loc = it("q_loc", [16, 128, NT])
        self.qr_loc = it("qr_loc", [8, 128, NT])
        self.kt_loc = it("kt_loc", [16, 128, NT])
        self.kt_all = it("kt_all", [16, 4 * 128, NT])
        self.kr_loc = it("kr_loc", [128, NT])
        self.kr_all = it("kr_all", [4 * 128, NT])
        self.vch_da = min(NT, 2048)
        self.vch_mla = min(NT, 4096)
        self.v_loc = it("v_loc", [NT * D])
        self.v_all = it("v_all", [4 * NT * D])

    def carve(self, off, shape, dt):
        n = 1
        for d_ in shape[1:]:
            n *= d_
        nb = n * (2 if dt == BF16 else 4)
        assert off % 4 == 0 and off + nb <= ARENA_BYTES, (off, shape)
        v = self.arena[:, off // 2:(off + nb) // 2]
        if dt != BF16:
            v = v.bitcast(dt)
        if len(shape) == 3:
            v = v.rearrange("p (a b) -> p a b", b=shape[2])
        elif len(shape) == 4:
            v = v.rearrange("p (a b c) -> p a b c", b=shape[2], c=shape[3])
        return v

    def alloc(self, st):
        nc, NT, NB = self.nc, self.NT, self.NB
        sb = lambda n, shp, dt: st.enter_context(nc.sbuf_tensor("sb_" + n, shp, dt))
        self.arena = sb("arena", [128, ARENA_BYTES // 2], BF16)
        K = 1024
        self.xt = self.carve(0, [128, 4, D], F32)
        self.hT = self.carve(32 * K, [128, 16, 512], BF16)
        self.hT_f32 = self.carve(32 * K, [128, D], F32)
        self.WP = [self.carve((48 + 16 * i) * K, [128, 16, 512], BF16) for i in range(3)]
        self.actb = self.carve(96 * K, [128, 64, 512], BF16)
        self.aot = self.carve(96 * K, [128, 4, D], F32)
        self.xn = [self.carve((160 + 4 * i) * K, [128, D], BF16) for i in range(2)]
        self.stg = [self.carve((168 + 2 * i) * K, [128, 512], F32) for i in range(2)]
        self.qk_tok = self.carve(172 * K, [128, 4, 512], BF16)
        self.qkT = self.carve(176 * K, [128, 4, 512], BF16)
        self.vst = self.carve(180 * K, [128, 4, 512], BF16)
        self.KT = self.carve(0, [128, 4, NT], BF16)
        self.Vda = self.carve(32 * K, [128, 4, NB, 257], BF16)
        self.Vml = self.carve(32 * K, [128, 4, NB, 129], BF16)
        self.QT = self.carve(97 * K, [128, NT], BF16)
        self.O1n = self.carve(105 * K, [128, NB, 256], F32)
        self.KR = self.carve(105 * K, [128, 4, NT], BF16)
        self.QR = self.carve(137 * K, [128, NT], BF16)
        self.PT = [self.carve((145 + i) * K, [128, 512], BF16) for i in range(4)]
        self.osb = [self.carve((149 + 4 * i) * K, [128, 4, 256], F32) for i in range(2)]
        self.tab = self.carve(0, [128, NB, 96], F32)
        self.ident = sb("ident", [128, 128], BF16)
        self.ident_f = sb("ident_f", [128, 128], F32)
        self.masks = sb("masks", [128, 4, 128], BF16)
        self.masks_f = sb("masks_f", [128, 4, 128], F32)
        self.colp = sb("colp", [128, self.ncol], F32)
        self.invf = sb("invf", [128, 48], F32)
        self.pos_i = sb("pos_i", [128, NB], I32)
        self.pos_f = sb("pos_f", [128, NB], F32)
        self.kmT = sb("kmT", [128, 4, 256], BF16)
        self.vm = sb("vm", [128, 2, 512], BF16)
        self.ones_bf = sb("ones_bf", [128, 128], BF16)
        self.rope = [sb("rope%d" % i, [128, 4, 96], F32) for i in range(2)]
        self.ss = sb("ss", [128, 16], F32)
        self.rstd = sb("rstd", [128, 16], F32)
        self.tmp = sb("tmp", [128, 4, 256], F32)
        self.lamt = sb("lamt", [128, 512], F32)
        self.lamw = sb("lamw", [128, 256], F32)
        self.lam = sb("lam", [128, 8], F32)
        self.rz = sb("rz", [128, 8], F32)
        self.junk = sb("junk", [128, 512], BF16)
        self.ps = [st.enter_context(nc.psum_tensor("ps%d" % i, [128, 512], F32)) for i in range(8)]

    def next_bank(self):
        b = self.bank_rr % 8
        self.bank_rr += 1
        return b

    def cast_panels(self, name, li, lo=0, hi=None):
        s = self.s
        w = self.w_in[name][li]
        pan = self.pan[name][li]
        npan = pan.shape[0]
        hi = npan if hi is None else hi
        for c in range(lo, hi):
            if name in ("da_wqkv", "da_wo", "mla_wo", "ca_wq", "ca_wkv", "mlp_wup"):
                src = w.rearrange("(kc p) n -> p kc n", p=128)[:, :, c * 512:(c + 1) * 512]
                dst = pan[c].rearrange("p (kc n) -> p kc n", n=512)
            elif name == "mlp_wdown":
                cc, g = c // 4, c % 4
                src = w[g * 2048:(g + 1) * 2048].rearrange("(kc p) n -> p kc n", p=128)[:, :, cc * 512:(cc + 1) * 512]
                dst = pan[c].rearrange("p (kc n) -> p kc n", n=512)
            elif name == "ca_wo":
                src = w.rearrange("(kc p) n -> p kc n", p=128)[:, :, c * 512:(c + 1) * 512]
                dst = pan[c].rearrange("p (kc n) -> p kc n", n=512)
            elif name == "mla_wdown":
                wdt = 512 if c < 2 else 64
                src = w.rearrange("(kc p) n -> p kc n", p=128)[:, :, c * 512:c * 512 + wdt]
                dst = pan[c].rearrange("p (kc n) -> p kc n", n=512)[:, :, 0:wdt]
            elif name == "mla_wuq":
                wv = w.rearrange("(kc p) (h e) -> p kc h e", p=128, e=192)
                if c < 4:
                    src = wv[:, :, 4 * c:4 * c + 4, 0:128]
                    dst = pan[c].rearrange("p (kc h e) -> p kc h e", h=4, e=128)
                else:
                    src = wv[:, :, 8 * (c - 4):8 * (c - 4) + 8, 128:192]
                    dst = pan[c].rearrange("p (kc h e) -> p kc h e", h=8, e=64)
            elif name == "mla_wukv":
                wv = w.rearrange("(kc p) (h e) -> p kc h e", p=128, e=256)
                if c < 4:
                    src = wv[:, :, 4 * c:4 * c + 4, 0:128]
                else:
                    src = wv[:, :, 4 * (c - 4):4 * (c - 4) + 4, 128:256]
                dst = pan[c].rearrange("p (kc h e) -> p kc h e", h=4, e=128)
            else:
                raise KeyError(name)
            if len(src.shape) == 4:
                for kc in range(src.shape[1]):
                    s.dma("pool", dst[:, kc], src[:, kc], r=(), w=[("pan", name, li, c), "castq"], sbuf=False)
                continue
            s.dma("pool", dst, src, r=(), w=[("pan", name, li, c), "castq"], sbuf=False)

    def load_panel(self, name, li, c, slot, kc=16, width=512):
        src = self.pan[name][li][c][:, 0:kc * 512].rearrange("p (kc n) -> p kc n", n=512)[:, :, 0:width]
        self.s.dma("sp", self.WP[slot][:, 0:kc, 0:width], src, r=[("pan", name, li, c)], w=[("WP", slot)])

    def transpose_to(self, bank, col0, src_ap, rd, ident=None, ncols=128):
        ps = self.ps[bank]
        idn = self.ident if ident is None else ident
        self.s.pe(lambda e: e.matmul(ps[:, col0:col0 + ncols], lhsT=src_ap, rhs=idn[:, 0:ncols],
                                     start=True, stop=True),
                  r=list(rd) + ["ident"], w=[("ps", bank)])

    def rms_rstd(self, src_ap, n, col, rd, junk_ap=None, junk_res="junk"):
        s = self.s
        ss, rstd = self.ss, self.rstd
        jk = self.junk[:, 0:n] if junk_ap is None else junk_ap
        s.act(lambda e: e.activation(jk, src_ap, AF.Square, accum_out=ss[:, col:col + 1]),
              r=list(rd), w=[("ss", col), junk_res])
        s.act(lambda e: e.activation(rstd[:, col:col + 1], ss[:, col:col + 1], AF.Ln, bias=self.epsb[:, 0:1], scale=1.0 / n),
              r=[("ss", col), "epsb"], w=[("rstd", col)])
        s.act(lambda e: e.activation(rstd[:, col:col + 1], rstd[:, col:col + 1], AF.Exp, scale=-0.5),
              r=[("rstd", col)], w=[("rstd", col)])

    def norm_to_hT(self, src, nblk, gcol0, src_res, dst=None, dst_res=None, width=512):
        s = self.s
        dst = self.hT if dst is None else dst
        dst_res = "hT" if dst_res is None else dst_res
        for b in range(nblk):
            xn = self.xn[b % 2]
            xr = ("xn", b % 2)
            sap = src[:, b, :]
            self.rms_rstd(sap, D, b, rd=[src_res(b)], junk_ap=xn[:, :], junk_res=xr)
            s.dve(lambda e, xn=xn, sap=sap, b=b: e.tensor_scalar(
                xn[:, :], sap, self.rstd[:, b:b + 1], None, op0=ALU.mult),
                r=[src_res(b), ("rstd", b)], w=[xr])
            for g in range(4):
                bank = self.next_bank()
                for k in range(4):
                    kc = 4 * g + k
                    self.transpose_to(bank, k * 128, xn[:, kc * 128:(kc + 1) * 128], rd=[xr])
                ps = self.ps[bank]
                gc = self.colp[:, gcol0 + 4 * g:gcol0 + 4 * g + 4].unsqueeze(2).to_broadcast([128, 4, 128])
                s.dve(lambda e, ps=ps, g=g, b=b, gc=gc: e.tensor_tensor(
                    dst[:, 4 * g:4 * g + 4, b * 128:(b + 1) * 128],
                    ps[:, :].rearrange("p (a t) -> p a t", t=128), gc, op=ALU.mult),
                    r=[("ps", bank), "colp"], w=[(dst_res, b)])

    def proj_tok(self, lhs, lhs_res, nblk, kcn, slot, width, consume):
        s = self.s
        wp = self.WP[slot]
        for b in range(nblk):
            bank = self.next_bank()
            ps = self.ps[bank]
            for kc in range(kcn):
                s.pe(lambda e, kc=kc, b=b, ps=ps: e.matmul(
                    ps[:, 0:width], lhsT=lhs[:, kc, b * 128:(b + 1) * 128], rhs=wp[:, kc, 0:width],
                    start=(kc == 0), stop=(kc == kcn - 1)),
                    r=([("hT", b)] if lhs_res == "hT" else _rl(lhs_res)) + [("WP", slot)], w=[("ps", bank)])
            consume(b, bank)

    def proj_feat(self, rhs, rhs_res, ntok, kcn, slot, m0, consume_bank):
        s = self.s
        wp = self.WP[slot]
        bank = self.next_bank()
        ps = self.ps[bank]
        for kc in range(kcn):
            s.pe(lambda e, kc=kc: e.matmul(
                ps[:, 0:ntok], lhsT=wp[:, kc, m0:m0 + 128], rhs=rhs[:, kc, 0:ntok],
                start=(kc == 0), stop=(kc == kcn - 1)),
                r=(HT4 if rhs_res == "hT" else _rl(rhs_res)) + [("WP", slot)], w=[("ps", bank)])
        consume_bank(bank)

    def setup(self):
        s, NB = self.s, self.NB
        s.dma("sp", self.ident_f[:, :], self.ident_in, w=["ident_f"])
        s.dma("sp", self.masks_f[:, :, :], self.masks_in, w=["masks_f"])
        s.dma("sp", self.colp[:, :], self.colp_in, w=["colp"])
        s.dma("sp", self.invf[:, :], self.invf_in, w=["invf"])
        s.dma("sp", self.pos_i[:, :], self.pos_in, w=["pos_i"])
        s.dve(lambda e: e.tensor_copy(self.ident[:, :], self.ident_f[:, :]), r=["ident_f"], w=["ident"])
        s.dve(lambda e: e.tensor_copy(self.masks[:, :, :], self.masks_f[:, :, :]), r=["masks_f"], w=["masks"])
        s.dve(lambda e: e.tensor_copy(self.pos_f[:, :], self.pos_i[:, :]), r=["pos_i"], w=["pos_f"])
        s.dve(lambda e: e.memset(self.ones_bf[:, :], 1.0), w=["ones_bf"])
        tab = self.tab
        K_ = 1024
        C1 = 6.28125
        C2 = 2 * PI - C1
        for (f0, nf, c0, s0) in ((0, 16, 0, 16), (16, 32, 32, 64)):
            tf = [self.carve((48 + 4 * i) * K_, [128, NB, 32], F32)[:, :, 0:nf] for i in range(6)]
            ti = self.carve((48 + 4 * 6) * K_, [128, NB, 32], I32)[:, :, 0:nf]
            a, q, kf, r_, y, w1 = tf
            pb = self.pos_f[:, :].unsqueeze(2).to_broadcast([128, NB, nf])
            fb = self.invf[:, f0:f0 + nf].unsqueeze(1).to_broadcast([128, NB, nf])
            s.dve(lambda e, a=a, pb=pb, fb=fb: e.tensor_tensor(a, pb, fb, op=ALU.mult), r=["pos_f", "invf", "tab"], w=["ta"])
            s.dve(lambda e, a=a, q=q: e.tensor_scalar(q, a, 1.0 / (2 * PI), None, op0=ALU.mult), r=["ta"], w=["tq"])
            s.dve(lambda e, ti=ti, q=q: e.tensor_copy(ti, q), r=["tq"], w=["ti"])
            s.dve(lambda e, ti=ti, kf=kf: e.tensor_copy(kf, ti), r=["ti"], w=["tk"])
            s.dve(lambda e, kf=kf, a=a, r_=r_: e.scalar_tensor_tensor(r_, kf, -C1, a, op0=ALU.mult, op1=ALU.add),
                  r=["tk", "ta"], w=["tr"])
            s.dve(lambda e, kf=kf, r_=r_: e.scalar_tensor_tensor(r_, kf, -C2, r_, op0=ALU.mult, op1=ALU.add),
                  r=["tk", "tr"], w=["tr"])
            for (shift, col) in ((PI / 2, c0), (0.0, s0)):
                s.dve(lambda e, y=y, r_=r_, shift=shift: e.tensor_scalar(y, r_, shift, None, op0=ALU.add),
                      r=["tr", "tab"], w=["ty"])
                s.dve(lambda e, y=y, w1=w1: e.tensor_scalar(w1, y, PI, -2 * PI, op0=ALU.is_gt, op1=ALU.mult),
                      r=["ty"], w=["tw"])
                s.dve(lambda e, y=y, w1=w1: e.tensor_tensor(y, y, w1, op=ALU.add), r=["ty", "tw"], w=["ty"])
                s.dve(lambda e, y=y, w1=w1: e.tensor_scalar(w1, y, -PI, 2 * PI, op0=ALU.is_lt, op1=ALU.mult),
                      r=["ty"], w=["tw"])
                s.dve(lambda e, y=y, w1=w1: e.tensor_tensor(y, y, w1, op=ALU.add), r=["ty", "tw"], w=["ty"])
                s.act(lambda e, y=y, col=col, nf=nf: e.activation(tab[:, :, col:col + nf], y, AF.Sin),
                      r=["ty"], w=["tab"])
        s.dma("sp", self.ropetab.rearrange("p (m f) -> p m f", f=96), tab[:, :, :], r=["tab"], w=["ropetab"])
        s.barrier()
        s.dma("sp", self.xt[:, 0:2, :], self.mem_in.rearrange("(b p) d -> p b d", p=128), w=[("xt", 0), ("xt", 1)])
        mh = self.hT[:, :, 0:256]
        self.norm_to_hT(self.xt, 2, 12 * 16, lambda b: ("xt", b))
        s.dma("sp", self.memhT_d.rearrange("p (kc t) -> p kc t", t=256), mh, r=HT4, w=["memhT_d"])

    def mem_kv(self, L):
        s = self.s
        mh = self.actb[:, 0:8, :].rearrange("p a b -> p (a b)").rearrange("p (kc t) -> p kc t", t=256)
        s.dma("sp", mh, self.memhT_d.rearrange("p (kc t) -> p kc t", t=256), r=["memhT_d"],
              w=[("act", i) for i in range(8)])
        mres = ("act", 0)
        self.load_panel("ca_wkv", L, 0, 0)
        self.load_panel("ca_wkv", L, 1, 1)
        kmT, vm = self.kmT, self.vm
        for h in range(4):
            def cons(bank, h=h):
                ps = self.ps[bank]
                s.act(lambda e: e.copy(kmT[:, h, :], ps[:, 0:256]), r=[("ps", bank)], w=["kmT"])
            self.proj_feat(mh, mres, 256, 16, 0, h * 128, cons)

        def consv(b, bank):
            ps = self.ps[bank]
            s.act(lambda e: e.copy(vm[:, b, :], ps[:, :]), r=[("ps", bank)], w=["vm"])
        self.proj_tok(mh, mres, 2, 16, 1, 512, consv)

    def load_rope(self, t):
        rp = self.rope[t % 2]
        self.s.dma("sp", rp[:, :, :],
                   self.ropetab.rearrange("p (m f) -> p m f", f=96)[:, 4 * t:4 * t + 4, :],
                   r=["ropetab"], w=[("rope", t % 2)])
        return rp, ("rope", t % 2)

    def rope_ops(self, dst, src, cos, sin, nh, half, rd, wr):
        s = self.s
        tmp = self.tmp
        cb = cos.unsqueeze(1).to_broadcast([128, nh, half])
        sn = sin.unsqueeze(1).to_broadcast([128, nh, half])
        x1, x2 = src[:, :, 0:half], src[:, :, half:2 * half]
        t = [tmp[:, i, 0:nh * half].rearrange("p (h f) -> p h f", f=half) for i in range(4)]
        s.dve(lambda e: e.tensor_tensor(t[0], x1, cb, op=ALU.mult), r=rd, w=[("tmp", 0)])
        s.dve(lambda e: e.tensor_tensor(t[1], x2, sn, op=ALU.mult), r=rd, w=[("tmp", 1)])
        s.dve(lambda e: e.tensor_tensor(t[2], x2, cb, op=ALU.mult), r=rd, w=[("tmp", 2)])
        s.dve(lambda e: e.tensor_tensor(t[3], x1, sn, op=ALU.mult), r=rd, w=[("tmp", 3)])
        s.dve(lambda e: e.tensor_tensor(dst[:, :, 0:half], t[0], t[1], op=ALU.subtract),
              r=[("tmp", 0), ("tmp", 1)], w=wr)
        s.dve(lambda e: e.tensor_tensor(dst[:, :, half:2 * half], t[2], t[3], op=ALU.add),
              r=[("tmp", 2), ("tmp", 3)], w=wr)

    def p_phase_da(self, L, t):
        s, NT = self.s, self.NT
        j = L // 2
        rp, rres = self.load_rope(t)
        self.norm_to_hT(self.xt, 4, (0 * 4 + L) * 16, lambda b: ("xt", b))
        for c in range(12):
            slot = self.wp_rr % 3
            self.wp_rr += 1
            self.load_panel("da_wqkv", j, c, slot)
            if c < 8:
                def cons(b, bank, c=c):
                    ps = self.ps[bank]
                    stg = self.stg[b % 2]
                    sr = ("stg", b % 2)
                    s.act(lambda e: e.copy(stg[:, :], ps[:, :]), r=[("ps", bank)], w=[sr])
                    sv = stg[:, :].rearrange("p (h e) -> p h e", e=128)
                    dv = self.qk_tok[:, b, :].rearrange("p (h e) -> p h e", e=128)
                    self.rope_ops(dv[:, :, 0:32], sv[:, :, 0:32], rp[:, b, 0:16], rp[:, b, 16:32], 4, 16,
                                  rd=[sr, rres], wr=[("qk_tok", b)])
                    s.pool(lambda e: e.tensor_copy(dv[:, :, 32:128], sv[:, :, 32:128]), r=[sr], w=[("qk_tok", b)])
                self.proj_tok(self.hT, "hT", 4, 16, slot, 512, cons)
                for hh in range(4):
                    bank = self.next_bank()
                    for b in range(4):
                        self.transpose_to(bank, b * 128, self.qk_tok[:, b, hh * 128:(hh + 1) * 128], rd=[("qk_tok", b)])
                    ps = self.ps[bank]
                    s.act(lambda e, ps=ps, hh=hh: e.copy(self.qkT[:, hh, :], ps[:, :]), r=[("ps", bank)], w=["qkT"])
                u0 = (c % 4) * 4
                dstT = self.q_loc if c < 4 else self.kt_loc
                nm = "q_loc" if c < 4 else "kt_loc"
                s.dma("act", dstT[u0:u0 + 4, :, t * TT:(t + 1) * TT].rearrange("u p t -> p u t"), self.qkT[:, :, :],
                      r=["qkT"], w=[(nm, u0 + i) for i in range(4)])
            else:
                def consv(b, bank):
                    ps = self.ps[bank]
                    s.act(lambda e: e.copy(self.vst[:, b, :], ps[:, :]), r=[("ps", bank)], w=[("vst", b)])
                self.proj_tok(self.hT, "hT", 4, 16, slot, 512, consv)
                vl = self.v_loc.rearrange("(h n d) -> h n d", n=NT, d=256)
                h0 = 2 * (c - 8)
                for hd in range(2):
                    s.dma("act", vl[h0 + hd, t * TT:(t + 1) * TT, :].rearrange("(b p) d -> p b d", p=128),
                          self.vst[:, :, hd * 256:(hd + 1) * 256],
                          r=[("vst", b) for b in range(4)], w=[("v_loc", h0 + hd)])

    def p_phase_mla(self, L, t):
        s, NT = self.s, self.NT
        j = L // 2
        cb = 14 * 16 + self.nd * 2 + j * 8
        rp, rres = self.load_rope(t)
        self.norm_to_hT(self.xt, 4, (0 * 4 + L) * 16, lambda b: ("xt", b))
        cqT = self.actb[:, 0:4, :]
        ckvT = self.actb[:, 4:8, :]
        for c in range(3):
            slot = self.wp_rr % 3
            self.wp_rr += 1
            wdt = 512 if c < 2 else 64
            self.load_panel("mla_wdown", j, c, slot, width=wdt)
            if c < 2:
                dstT = cqT if c == 0 else ckvT
                dres = ("act", 0) if c == 0 else ("act", 4)

                def cons(b, bank, c=c, dstT=dstT, dres=dres):
                    ps = self.ps[bank]
                    col = 4 + b
                    self.rms_rstd(ps[:, :], 512, col, rd=[("ps", bank)])
                    xn = self.xn[b % 2]
                    xr = ("xn", b % 2)
                    s.dve(lambda e: e.tensor_scalar(xn[:, 0:512], ps[:, :], self.rstd[:, col:col + 1], None,
                                                    op0=ALU.mult),
                          r=[("ps", bank), ("rstd", col)], w=[xr])
                    bank2 = self.next_bank()
                    for k in range(4):
                        self.transpose_to(bank2, k * 128, xn[:, k * 128:(k + 1) * 128], rd=[xr])
                    ps2 = self.ps[bank2]
                    gc = self.colp[:, cb + 4 * c:cb + 4 * c + 4].unsqueeze(2).to_broadcast([128, 4, 128])
                    s.dve(lambda e: e.tensor_tensor(dstT[:, :, b * 128:(b + 1) * 128],
                                                    ps2[:, :].rearrange("p (a t) -> p a t", t=128), gc, op=ALU.mult),
                          r=[("ps", bank2), "colp"], w=[dres])
                self.proj_tok(self.hT, "hT", 4, 16, slot, 512, cons)
            else:
                def consr(b, bank):
                    ps = self.ps[bank]
                    stg = self.stg[b % 2]
                    sr = ("stg", b % 2)
                    s.act(lambda e: e.copy(stg[:, 0:64], ps[:, 0:64]), r=[("ps", bank)], w=[sr])
                    sv = stg[:, 0:64].rearrange("p (h e) -> p h e", e=64)
                    dv = self.qk_tok[:, b, 0:64].rearrange("p (h e) -> p h e", e=64)
                    self.rope_ops(dv, sv, rp[:, b, 32:64], rp[:, b, 64:96], 1, 32, rd=[sr, rres], wr=[("qk_tok", b)])
                    s.pool(lambda e: e.tensor_copy(self.qk_tok[:, b, 64:128], self.qk_tok[:, b, 0:64]),
                           r=[("qk_tok", b)], w=[("qk_tok", b)])
                self.proj_tok(self.hT, "hT", 4, 16, slot, 64, consr)
                bank = self.next_bank()
                for b in range(4):
                    self.transpose_to(bank, b * 128, self.qk_tok[:, b, 0:128], rd=[("qk_tok", b)])
                ps = self.ps[bank]
                s.act(lambda e, ps=ps: e.copy(self.qkT[:, 0, :], ps[:, :]), r=[("ps", bank)], w=["qkT"])
                s.dma("act", self.kr_loc[:, t * TT:(t + 1) * TT], self.qkT[:, 0, :], r=["qkT"], w=["kr_loc"])
        for c in range(4):
            slot = self.wp_rr % 3
            self.wp_rr += 1
            self.load_panel("mla_wuq", j, c, slot, kc=4)
            for hh in range(4):
                def cons(bank, hh=hh):
                    ps = self.ps[bank]
                    s.act(lambda e: e.copy(self.qkT[:, hh, :], ps[:, :]), r=[("ps", bank)], w=["qkT"])
                self.proj_feat(cqT, ("act", 0), 512, 4, slot, hh * 128, cons)
            s.dma("act", self.q_loc[4 * c:4 * c + 4, :, t * TT:(t + 1) * TT].rearrange("u p t -> p u t"), self.qkT[:, :, :],
                  r=["qkT"], w=[("q_loc", 4 * c + i) for i in range(4)])
        for c in range(2):
            slot = self.wp_rr % 3
            self.wp_rr += 1
            self.load_panel("mla_wuq", j, 4 + c, slot, kc=4)

            def consq(b, bank):
                ps = self.ps[bank]
                stg = self.stg[b % 2]
                sr = ("stg", b % 2)
                s.act(lambda e: e.copy(stg[:, :], ps[:, :]), r=[("ps", bank)], w=[sr])
                sv = stg[:, :].rearrange("p (h e) -> p h e", e=64)
                dv = self.qk_tok[:, b, :].rearrange("p (h e) -> p h e", e=64)
                self.rope_ops(dv, sv, rp[:, b, 32:64], rp[:, b, 64:96], 8, 32, rd=[sr, rres], wr=[("qk_tok", b)])
            self.proj_tok(cqT, ("act", 0), 4, 4, slot, 512, consq)
            for pr in range(4):
                bank = self.next_bank()
                for b in range(4):
                    self.transpose_to(bank, b * 128, self.qk_tok[:, b, pr * 128:(pr + 1) * 128], rd=[("qk_tok", b)])
                ps = self.ps[bank]
                s.act(lambda e, ps=ps, pr=pr: e.copy(self.qkT[:, pr, :], ps[:, :]), r=[("ps", bank)], w=["qkT"])
            s.dma("act", self.qr_loc[4 * c:4 * c + 4, :, t * TT:(t + 1) * TT].rearrange("u p t -> p u t"), self.qkT[:, :, :],
                  r=["qkT"], w=[("qr_loc", 4 * c + i) for i in range(4)])
        for c in range(4):
            slot = self.wp_rr % 3
            self.wp_rr += 1
            self.load_panel("mla_wukv", j, c, slot, kc=4)
            for hh in range(4):
                def cons(bank, hh=hh):
                    ps = self.ps[bank]
                    s.act(lambda e: e.copy(self.qkT[:, hh, :], ps[:, :]), r=[("ps", bank)], w=["qkT"])
                self.proj_feat(ckvT, ("act", 4), 512, 4, slot, hh * 128, cons)
            s.dma("act", self.kt_loc[4 * c:4 * c + 4, :, t * TT:(t + 1) * TT].rearrange("u p t -> p u t"), self.qkT[:, :, :],
                  r=["qkT"], w=[("kt_loc", 4 * c + i) for i in range(4)])
        vl = self.v_loc.rearrange("(h n d) -> h n d", n=NT, d=128)
        for c in range(4):
            slot = self.wp_rr % 3
            self.wp_rr += 1
            self.load_panel("mla_wukv", j, 4 + c, slot, kc=4)

            def consv(b, bank):
                ps = self.ps[bank]
                s.act(lambda e: e.copy(self.vst[:, b, :], ps[:, :]), r=[("ps", bank)], w=[("vst", b)])
            self.proj_tok(ckvT, ("act", 4), 4, 4, slot, 512, consv)
            for hd in range(4):
                s.dma("act", vl[4 * c + hd, t * TT:(t + 1) * TT, :].rearrange("(b p) d -> p b d", p=128),
                      self.vst[:, :, hd * 128:(hd + 1) * 128],
                      r=[("vst", b) for b in range(4)], w=[("v_loc", 4 * c + hd)])

    def exchange(self, L):
        s, NT = self.s, self.NT
        rg = [[0, 1, 2, 3], [4, 5, 6, 7]]
        mla = (L % 2 == 1)

        def ag(src, dst, rd, wr):
            s.cc(lambda e: e.collective_compute("AllGather", ALU.bypass, replica_groups=rg,
                                                ins=[src.opt()], outs=[dst.opt()]), r=rd, w=wr)
        if mla:
            ag(self.kr_loc, self.kr_all, ["kr_loc"], ["kr_all"])
        for u in range(16):
            ag(self.kt_loc[u], self.kt_all[u], [("kt_loc", u)], [("kt_all", u)])
            if not mla and u % 2 == 0:
                h = u // 2
                ch = self.vch_da
                vl = self.v_loc.rearrange("(h n d) -> h n d", n=NT, d=256)
                va = self.v_all.rearrange("(h c r n d) -> h c (r n) d", c=NT // ch, r=4, n=ch, d=256)
                for c2 in range(NT // ch):
                    ag(vl[h, c2 * ch:(c2 + 1) * ch, :], va[h, c2], [("v_loc", h)], [("v_all", h)])
            if mla:
                ch = self.vch_mla
                vl = self.v_loc.rearrange("(h n d) -> h n d", n=NT, d=128)
                va = self.v_all.rearrange("(h c r n d) -> h c (r n) d", c=NT // ch, r=4, n=ch, d=128)
                for c2 in range(NT // ch):
                    ag(vl[u, c2 * ch:(c2 + 1) * ch, :], va[u, c2], [("v_loc", u)], [("v_all", u)])

    def attention(self, L):
        s, NT, NB = self.s, self.NT, self.NB
        mla = (L % 2 == 1)
        j = L // 2
        NQT = NT // 512
        dv = 128 if mla else 256
        V = self.Vml if mla else self.Vda
        sc = 1.0 / math.sqrt(192.0 if mla else 128.0)
        KT, QT, KR, QR, PT = self.KT, self.QT, self.KR, self.QR, self.PT
        s.pool(lambda e: e.memset(V[:, :, :, dv:dv + 1], 1.0), w=["Vones"])
        if mla:
            s.dma("sp", KR[:, :, :], self.kr_all.rearrange("(r p) t -> p r t", p=128), r=["kr_all"], w=["KR"])
        else:
            li = lambda_init(L)
            lamt, lamw, lam = self.lamt, self.lamw, self.lam
            s.dma("sp", lamt[:, :], self.lam_in[j].partition_broadcast(128), w=["lamt"])
            lv = lamt[:, :].rearrange("p (a b d) -> p a b d", a=2, b=2)
            s.dve(lambda e: e.tensor_tensor(lamw[:, :].rearrange("p (a d) -> p a d", a=2), lv[:, :, 0, :], lv[:, :, 1, :],
                                            op=ALU.mult), r=["lamt"], w=["lamw"])
            s.dve(lambda e: e.reduce_sum(lam[:, 0:2], lamw[:, :].rearrange("p (a d) -> p a d", a=2), axis=AX.X),
                  r=["lamw"], w=[("lam", 0)])
            s.act(lambda e: e.activation(lam[:, 2:4], lam[:, 0:2], AF.Exp), r=[("lam", 0)], w=[("lam", 1)])
            s.dve(lambda e: e.tensor_tensor(lam[:, 4:5], lam[:, 2:3], lam[:, 3:4], op=ALU.subtract),
                  r=[("lam", 1)], w=[("lam", 2)])
            s.dve(lambda e: e.tensor_scalar(lam[:, 5:6], lam[:, 4:5], li, -1.0, op0=ALU.add, op1=ALU.mult),
                  r=[("lam", 2)], w=[("lam", 3)])
        kb_global = [0]
        osb_rr = [0]
        for u in range(16):
            hb = (u % 2) * 64
            s.dma("sp", KT[:, :, :], self.kt_all[u].rearrange("(r p) t -> p r t", p=128), r=[("kt_all", u)], w=["KT"])
            s.dma("sp", QT[:, :], self.q_loc[u], r=[("q_loc", u)], w=["QT"])
            if mla:
                if u % 2 == 0:
                    s.dma("sp", QR[:, :], self.qr_loc[u // 2], r=[("qr_loc", u // 2)], w=["QR"])
                ch = self.vch_mla
                va = self.v_all.rearrange("(h c r n d) -> h c r n d", c=NT // ch, r=4, n=ch, d=128)
                mb = ch // 128
                for c2 in range(NT // ch):
                    for r in range(4):
                        s.dma("sp", V[:, r, c2 * mb:(c2 + 1) * mb, 0:128],
                              va[u, c2, r].rearrange("(m p) d -> p m d", p=128), r=[("v_all", u)], w=[("V", r)])
            elif u % 2 == 0:
                ch = self.vch_da
                va = self.v_all.rearrange("(h c r n d) -> h c r n d", c=NT // ch, r=4, n=ch, d=256)
                mb = ch // 128
                for c2 in range(NT // ch):
                    for r in range(4):
                        s.dma("sp", V[:, r, c2 * mb:(c2 + 1) * mb, 0:256],
                              va[u // 2, c2, r].rearrange("(m p) d -> p m d", p=128), r=[("v_all", u // 2)], w=[("V", r)])
            items = []
            for t in range(NQT):
                for r in range(4):
                    for m in range(4 * t + 4):
                        items.append((t, r, m))
            n = len(items)
            LA = 3

            def qk(idx):
                t, r, m = items[idx]
                i0 = max(0, m - 4 * t)
                ncols = 512 - 128 * i0
                q0 = t * 512 + 128 * i0
                kb = kb_global[0] + idx
                bank = kb % 4
                ps = self.ps[bank]
                pt = PT[kb % 4]
                s.pe(lambda e: e.matmul(ps[:, 0:ncols], lhsT=KT[:, r, m * 128:(m + 1) * 128], rhs=QT[:, q0:q0 + ncols],
                                        start=True, stop=(not mla)), r=["KT", "QT"], w=[("ps", bank)])
                if mla:
                    s.pe(lambda e, hb=hb: e.matmul(ps[:, 0:ncols], lhsT=KR[hb:hb + 64, r, m * 128:(m + 1) * 128],
                                                   rhs=QR[hb:hb + 64, q0:q0 + ncols], start=False, stop=True),
                         r=["KR", "QR"], w=[("ps", bank)])
                s.act(lambda e: e.activation(pt[:, 0:ncols], ps[:, 0:ncols], AF.Exp, scale=sc),
                      r=[("ps", bank)], w=[("PT", kb % 4)])
                if m >= 4 * t:
                    s.dve(lambda e: e.tensor_tensor(pt[:, 0:128], pt[:, 0:128], self.masks[:, r, :], op=ALU.mult),
                          r=[("PT", kb % 4), "masks"], w=[("PT", kb % 4)])

            def pv(idx):
                t, r, m = items[idx]
                i0 = max(0, m - 4 * t)
                kb = kb_global[0] + idx
                pt = PT[kb % 4]
                for i in range(i0, 4):
                    first = (r == 0 and m == 0)
                    last = (r == 3 and m == 4 * t + i)
                    ob = 4 + i
                    po = self.ps[ob]
                    s.pe(lambda e, i=i, po=po, first=first, last=last: e.matmul(
                        po[:, 0:dv + 1], lhsT=pt[:, (i - i0) * 128:(i - i0 + 1) * 128],
                        rhs=V[:, r, m, 0:dv + 1], start=first, stop=last),
                         r=[("PT", kb % 4), ("V", r), "Vones"], w=[("ps", ob)])
                if r == 3 and m == 4 * t + 3:
                    finish(t)

            def finish(t):
                ob_i = osb_rr[0] % 2
                osb_rr[0] += 1
                osb = self.osb[ob_i]
                rz = self.rz
                for i in range(4):
                    po = self.ps[4 + i]
                    blk = 4 * t + i
                    s.dve(lambda e, po=po, i=i: e.reciprocal(rz[:, i:i + 1], po[:, dv:dv + 1]),
                          r=[("ps", 4 + i)], w=[("rz", i)])
                    if mla:
                        s.dve(lambda e, po=po, i=i: e.tensor_scalar(osb[:, i, 0:128], po[:, 0:128], rz[:, i:i + 1], None,
                                                                    op0=ALU.mult),
                              r=[("ps", 4 + i), ("rz", i)], w=[("osb", ob_i)])
                    elif u % 2 == 0:
                        s.dve(lambda e, po=po, i=i, blk=blk: e.tensor_scalar(self.O1n[:, blk, :], po[:, 0:256], rz[:, i:i + 1],
                                                                             None, op0=ALU.mult),
                              r=[("ps", 4 + i), ("rz", i)], w=[("O1n", blk)])
                    else:
                        s.dve(lambda e, i=i: e.tensor_tensor(rz[:, 4 + i:5 + i], rz[:, i:i + 1], self.lam[:, 5:6], op=ALU.mult),
                              r=[("rz", i), ("lam", 3)], w=[("rz", 4 + i)])
                        s.dve(lambda e, po=po, i=i, blk=blk: e.scalar_tensor_tensor(
                            osb[:, i, :], po[:, 0:256], rz[:, 4 + i:5 + i], self.O1n[:, blk, :],
                            op0=ALU.mult, op1=ALU.add),
                            r=[("ps", 4 + i), ("rz", 4 + i), ("O1n", blk)], w=[("osb", ob_i)])
                if mla:
                    s.dma("sp", self.ao[t * 512:(t + 1) * 512, u * 128:(u + 1) * 128].rearrange("(b p) d -> p b d", p=128),
                          osb[:, :, 0:128], r=[("osb", ob_i)], w=[("ao", t)])
                elif u % 2 == 1:
                    h = u // 2
                    s.dma("sp", self.ao[t * 512:(t + 1) * 512, h * 256:(h + 1) * 256].rearrange("(b p) d -> p b d", p=128),
                          osb[:, :, :], r=[("osb", ob_i)], w=[("ao", t)])

            for idx in range(n + LA):
                if idx < n:
                    qk(idx)
                if idx - LA >= 0:
                    pv(idx - LA)
            kb_global[0] += n

    def c_phase(self, L, t):
        s, NT = self.s, self.NT
        mla = (L % 2 == 1)
        j = L // 2
        xsrc = self.x_in if L == 0 else self.xs
        xres = "x_in" if L == 0 else ("xs", t)
        xt = self.xt
        s.dma("sp", xt[:, :, :], xsrc[t * TT:(t + 1) * TT, :].rearrange("(b p) d -> p b d", p=128),
              r=[xres], w=[("xt", b) for b in range(4)])
        aot = self.aot
        for b in range(4):
            s.dma("sp", aot[:, b, :], self.ao[t * TT + b * 128:t * TT + (b + 1) * 128, :],
                  r=[("ao", t)], w=[("act", 8 * b + i) for i in range(8)])
        hT = self.hT
        for b in range(4):
            ares = ("act", 8 * b)
            xn = self.xn[b % 2]
            xr = ("xn", b % 2)
            if not mla:
                av = aot[:, b, :].rearrange("p (h e) -> p h e", e=256)
                for h in range(8):
                    s.act(lambda e, h=h, av=av: e.activation(self.junk[:, 0:256], av[:, h, :], AF.Square,
                                                             accum_out=self.ss[:, 8 + h:9 + h]),
                          r=[ares], w=[("ss", 8 + h), "junk"])
                s.act(lambda e: e.activation(self.rstd[:, 8:16], self.ss[:, 8:16], AF.Ln, bias=self.epsb[:, 0:1], scale=1.0 / 256),
                      r=[("ss", 8 + h) for h in range(8)] + ["epsb"], w=[("rstd", 8)])
                s.act(lambda e: e.activation(self.rstd[:, 8:16], self.rstd[:, 8:16], AF.Exp, scale=-0.5),
                      r=[("rstd", 8)], w=[("rstd", 8)])
                rb = self.rstd[:, 8:16].unsqueeze(2).to_broadcast([128, 8, 256])
                s.dve(lambda e, av=av, xn=xn, rb=rb: e.tensor_tensor(xn[:, :].rearrange("p (h e) -> p h e", e=256), av, rb,
                                                                      op=ALU.mult),
                      r=[ares, ("rstd", 8)], w=[xr])
            else:
                s.dve(lambda e, xn=xn, b=b: e.tensor_copy(xn[:, :], aot[:, b, :]), r=[ares], w=[xr])
            for g in range(4):
                bank = self.next_bank()
                for k in range(4):
                    kc = 4 * g + k
                    self.transpose_to(bank, k * 128, xn[:, kc * 128:(kc + 1) * 128], rd=[xr])
                ps = self.ps[bank]
                dstv = hT[:, 4 * g:4 * g + 4, b * 128:(b + 1) * 128]
                psv = ps[:, :].rearrange("p (a t) -> p a t", t=128)
                if not mla:
                    c0 = 14 * 16 + j * 2
                    for k in range(4):
                        kc = 4 * g + k
                        gcol = self.colp[:, c0 + (kc % 2):c0 + (kc % 2) + 1]
                        s.dve(lambda e, k=k, gcol=gcol, dstv=dstv, psv=psv: e.tensor_scalar(
                            dstv[:, k, :], psv[:, k, :], gcol, 1.0 - lambda_init(L), op0=ALU.mult, op1=ALU.mult),
                            r=[("ps", bank), "colp"], w=[("hT", b)])
                else:
                    s.act(lambda e, dstv=dstv, psv=psv: e.copy(dstv, psv), r=[("ps", bank)], w=[("hT", b)])
        wname = "mla_wo" if mla else "da_wo"
        self.linear_residual(wname, j, hT, "hT", 16)
        self.norm_to_hT(xt, 4, (1 * 4 + L) * 16, lambda b: ("xt", b))
        slot = self.wp_rr % 3
        self.wp_rr += 1
        self.load_panel("ca_wq", L, 0, slot)
        qcT = self.qkT
        for h in range(4):
            def cons(bank, h=h):
                ps = self.ps[bank]
                s.act(lambda e: e.copy(qcT[:, h, :], ps[:, :]), r=[("ps", bank)], w=["qkT"])
            self.proj_feat(hT, "hT", 512, 16, slot, h * 128, cons)
        ocT = self.qk_tok
        ptc = self.vst
        scc = 1.0 / math.sqrt(128.0)
        for h in range(4):
            for mb in range(2):
                bank = self.next_bank()
                ps = self.ps[bank]
                s.pe(lambda e, ps=ps, mb=mb, h=h: e.matmul(ps[:, :], lhsT=self.kmT[:, h, mb * 128:(mb + 1) * 128],
                                                            rhs=qcT[:, h, :], start=True, stop=True),
                     r=["kmT", "qkT"], w=[("ps", bank)])
                s.act(lambda e, ps=ps, mb=mb: e.activation(ptc[:, mb, :], ps[:, :], AF.Exp, scale=scc),
                      r=[("ps", bank)], w=[("vst", mb)])
            bo = self.next_bank()
            bz = self.next_bank()
            po, pz = self.ps[bo], self.ps[bz]
            for mb in range(2):
                s.pe(lambda e, mb=mb, h=h, po=po: e.matmul(po[:, :], lhsT=self.vm[:, mb, h * 128:(h + 1) * 128],
                                                           rhs=ptc[:, mb, :], start=(mb == 0), stop=(mb == 1)),
                     r=["vm", ("vst", mb)], w=[("ps", bo)])
            for mb in range(2):
                s.pe(lambda e, mb=mb, pz=pz: e.matmul(pz[:, :], lhsT=self.ones_bf[:, :], rhs=ptc[:, mb, :],
                                                      start=(mb == 0), stop=(mb == 1)),
                     r=["ones_bf", ("vst", mb)], w=[("ps", bz)])
            stg = self.stg[h % 2]
            sr = ("stg", h % 2)
            s.dve(lambda e, stg=stg, pz=pz: e.reciprocal(stg[:, :], pz[:, :]), r=[("ps", bz)], w=[sr])
            s.dve(lambda e, stg=stg, po=po, h=h: e.tensor_tensor(ocT[:, h, :], po[:, :], stg[:, :], op=ALU.mult),
                  r=[("ps", bo), sr], w=[("qk_tok", h)])
        self.linear_residual("ca_wo", L, ocT, [("qk_tok", h) for h in range(4)], 4)
        self.norm_to_hT(xt, 4, (2 * 4 + L) * 16, lambda b: ("xt", b))
        actb = self.actb
        for fp in range(16):
            slot = self.wp_rr % 3
            self.wp_rr += 1
            self.load_panel("mlp_wup", L, fp, slot)
            for n_ in range(4):
                fc = fp * 4 + n_

                def cons(bank, fc=fc):
                    ps = self.ps[bank]
                    stg = self.stg[fc % 2]
                    sr = ("stg", fc % 2)
                    s.act(lambda e: e.activation(stg[:, :], ps[:, :], AF.Relu), r=[("ps", bank)], w=[sr])
                    s.pool(lambda e: e.tensor_tensor(actb[:, fc, :], stg[:, :], stg[:, :], op=ALU.mult),
                           r=[sr], w=[("act", fc)])
                self.proj_feat(hT, "hT", 512, 16, slot, n_ * 128, cons)
        for c in range(4):
            banks = [self.next_bank() for _ in range(4)]
            for g in range(4):
                slot = self.wp_rr % 3
                self.wp_rr += 1
                self.load_panel("mlp_wdown", L, c * 4 + g, slot)
                wp = self.WP[slot]
                for b in range(4):
                    ps = self.ps[banks[b]]
                    for k in range(16):
                        fc = g * 16 + k
                        s.pe(lambda e, ps=ps, fc=fc, b=b, k=k, wp=wp, g=g: e.matmul(
                            ps[:, :], lhsT=actb[:, fc, b * 128:(b + 1) * 128], rhs=wp[:, k, :],
                            start=(g == 0 and k == 0), stop=(g == 3 and k == 15)),
                            r=[("act", fc), ("WP", slot)], w=[("ps", banks[b])])
            for b in range(4):
                ps = self.ps[banks[b]]
                s.dve(lambda e, ps=ps, b=b, c=c: e.tensor_tensor(xt[:, b, c * 512:(c + 1) * 512], xt[:, b, c * 512:(c + 1) * 512],
                                                                 ps[:, :], op=ALU.add),
                      r=[("ps", banks[b]), ("xt", b)], w=[("xt", b)])

    def linear_residual(self, wname, li, lhs, lhs_res, kcn):
        s = self.s
        xt = self.xt
        for c in range(4):
            slot = self.wp_rr % 3
            self.wp_rr += 1
            self.load_panel(wname, li, c, slot, kc=kcn)

            def cons(b, bank, c=c):
                ps = self.ps[bank]
                s.dve(lambda e: e.tensor_tensor(xt[:, b, c * 512:(c + 1) * 512], xt[:, b, c * 512:(c + 1) * 512], ps[:, :],
                                                op=ALU.add),
                      r=[("ps", bank), ("xt", b)], w=[("xt", b)])
            self.proj_tok(lhs, lhs_res, 4, kcn, slot, 512, cons)

    def final_norm(self, t):
        s = self.s
        xt = self.xt
        g = self.hT_f32
        s.dma("sp", g[:, :], self.fin_in.partition_broadcast(128), r=HT4, w=HT4)
        for b in range(4):
            self.rms_rstd(xt[:, b, :], D, b, rd=[("xt", b)], junk_ap=self.xn[b % 2][:, :], junk_res=("xn", b % 2))
            s.dve(lambda e, b=b: e.scalar_tensor_tensor(xt[:, b, :], xt[:, b, :], self.rstd[:, b:b + 1], g[:, :],
                                                        op0=ALU.mult, op1=ALU.mult),
                  r=[("xt", b), ("rstd", b)] + HT4, w=[("xt", b)])
        s.dma("act", self.out[t * TT:(t + 1) * TT, :].rearrange("(b p) d -> p b d", p=128), xt[:, :, :],
              r=[("xt", b) for b in range(4)], w=[("out", t)])

    def cast_layer_c(self, L):
        import os
        j = L // 2
        sel = os.environ.get("CASTS", "wo,ca_wkv,ca_wq,ca_wo,mlp_wup,mlp_wdown").split(",")
        if "wo" in sel:
            self.cast_panels("mla_wo" if L % 2 else "da_wo", j)
        for nme in ("ca_wkv", "ca_wq", "ca_wo", "mlp_wup", "mlp_wdown"):
            if nme in sel:
                self.cast_panels(nme, L)

    def cast_layer_p(self, L):
        j = L // 2
        if L % 2 == 0:
            self.cast_panels("da_wqkv", j)
        else:
            self.cast_panels("mla_wdown", j)
            self.cast_panels("mla_wuq", j)
            self.cast_panels("mla_wukv", j)

    def p_phase(self, L, t):
        if L % 2 == 0:
            self.p_phase_da(L, t)
        else:
            self.p_phase_mla(L, t)

    def build(self):
        nc, s = self.nc, self.s
        self.declare()
        self.wp_rr = 0
        with ExitStack() as st:
            self.alloc(st)
            self.pibias = st.enter_context(nc.sbuf_tensor("pibias", [128, 1], F32))
            s.dve(lambda e: e.memset(self.pibias[:, :], PI), w=["pibias"])
            self.epsb = st.enter_context(nc.sbuf_tensor("epsb", [128, 1], F32))
            s.dve(lambda e: e.memset(self.epsb[:, :], EPS), w=["epsb"])
            self.cast_layer_p(0)
            if self.stage >= 1:
                self.setup()
            for t in range(self.NTT):
                if self.stage < 2:
                    break
                s.dma("sp", self.xt[:, :, :], self.x_in[t * TT:(t + 1) * TT, :].rearrange("(b p) d -> p b d", p=128),
                      r=["x_in"], w=[("xt", b) for b in range(4)])
                self.p_phase(0, t)
            import os
            if self.stage >= 3 and not os.environ.get("NOEX"):
                self.exchange(0)
            for L in range(self.depth):
                if self.stage < 4:
                    break
                self.cast_layer_c(L)
                if L + 1 < self.depth:
                    self.cast_layer_p(L + 1)
                s.barrier()
                if self.stage < 5:
                    break
                self.attention(L)
                s.barrier()
                if self.stage < 6:
                    break
                self.mem_kv(L)
                for t in range(self.NTT):
                    self.c_phase(L, t)
                    if L + 1 < self.depth:
                        s.dma("act", self.xs[t * TT:(t + 1) * TT, :].rearrange("(b p) d -> p b d", p=128), self.xt[:, :, :],
                              r=[("xt", b) for b in range(4)], w=[("xs", t)])
                        self.p_phase(L + 1, t)
                    else:
                        self.final_norm(t)
                if L + 1 < self.depth:
                    self.exchange(L + 1)
            sems = {}
            for e in ("pe", "act", "dve", "pool"):
                sems[("eng", e)] = st.enter_context(nc.semaphore("s_" + e))
            for i in range(N_DMA_SEMS + N_SW_SEMS):
                sems[("dma", i)] = st.enter_context(nc.semaphore("s_dma%d" % i))
            for i in range(N_CC_SEMS):
                sems[("cc", i)] = st.enter_context(nc.semaphore("s_cc%d" % i))
            block = st.enter_context(nc.Block())
            s.finalize(nc, block, sems)
        return nc


def make_core_inputs(inputs, S, depth, b, j):
    NT = S // 4
    NB = NT // 128
    nd = (depth + 1) // 2
    nm = depth // 2
    f = lambda a: np.ascontiguousarray(np.asarray(a), dtype=np.float32)
    x = np.asarray(inputs["x"])[b, :S].reshape(S // 512, 4, 128, D)[:, j].reshape(NT, D)
    pos = np.asarray(inputs["positions"])[b, :S].reshape(S // 512, 4, 128)[:, j]
    colv = []
    for nme in ("attn_norm", "cross_norm", "mlp_norm"):
        a = np.asarray(inputs[nme])
        for L in range(4):
            colv.append(a[min(L, a.shape[0] - 1)].reshape(16, 128).T)
    colv.append(np.asarray(inputs["mem_norm"]).reshape(16, 128).T)
    colv.append(np.asarray(inputs["final_norm"]).reshape(16, 128).T)
    for jj in range(nd):
        colv.append(np.asarray(inputs["da_subln"])[jj].reshape(2, 128).T)
    for jj in range(max(nm, 1)):
        if nm:
            colv.append(np.asarray(inputs["mla_q_norm"])[jj].reshape(4, 128).T)
            colv.append(np.asarray(inputs["mla_kv_norm"])[jj].reshape(4, 128).T)
        else:
            colv.append(np.zeros((128, 8), np.float32))
    colp = np.concatenate(colv, axis=1)
    kk = np.arange(128)[:, None]
    qq = np.arange(128)[None, :]
    masks = np.zeros((128, 4, 128), np.float32)
    for r in range(4):
        if r < j:
            masks[:, r, :] = 1.0
        elif r == j:
            masks[:, r, :] = (kk <= qq).astype(np.float32)
    invf = np.zeros((128, 48), np.float32)
    invf[:, 0:16] = (THETA ** (-np.arange(0, 32, 2, dtype=np.float32) / 32)).astype(np.float32)[None, :]
    invf[:, 16:48] = (THETA ** (-np.arange(0, 64, 2, dtype=np.float32) / 64)).astype(np.float32)[None, :]
    m = {
        "x": f(x),
        "pos": np.ascontiguousarray(pos.T.astype(np.int32)),
        "mem": f(np.asarray(inputs["mem"])[b]),
        "colp": f(colp),
        "ident": np.eye(128, dtype=np.float32),
        "masks": masks,
        "invf": invf,
        "final_norm": f(inputs["final_norm"]),
        "da_lambda": f(np.asarray(inputs["da_lambda"])[:nd].reshape(nd, 512)),
        "da_wqkv": f(np.asarray(inputs["da_wqkv"])[:nd]),
        "da_wo": f(np.asarray(inputs["da_wo"])[:nd]),
        "ca_wq": f(np.asarray(inputs["ca_wq"])[:depth]),
        "ca_wkv": f(np.asarray(inputs["ca_wkv"])[:depth]),
        "ca_wo": f(np.asarray(inputs["ca_wo"])[:depth]),
        "mlp_wup": f(np.asarray(inputs["mlp_wup"])[:depth]),
        "mlp_wdown": f(np.asarray(inputs["mlp_wdown"])[:depth]),
    }
    if nm:
        m.update({
            "mla_wdown": f(np.asarray(inputs["mla_wdown"])[:nm]),
            "mla_wuq": f(np.asarray(inputs["mla_wuq"])[:nm]),
            "mla_wukv": f(np.asarray(inputs["mla_wukv"])[:nm]),
            "mla_wo": f(np.asarray(inputs["mla_wo"])[:nm]),
        })
    return m


def run(inputs, S, depth, trace=False, stage=99):
    bld = Builder(S, depth, stage)
    nc = bld.build()
    in_maps = []
    shared = None
    for c in range(8):
        b, j = c // 4, c % 4
        m = make_core_inputs(inputs, S, depth, b, j)
        if shared is None:
            shared = {k: v for k, v in m.items() if k not in ("x", "pos", "mem", "masks")}
        else:
            for k in shared:
                m[k] = shared[k]
        in_maps.append(m)
    res = run_bass_kernel_spmd(nc, in_maps, core_ids=list(range(8)), trace=trace)
    NT = S // 4
    out = np.zeros((2, S, D), np.float32)
    for c in range(8):
        b, j = c // 4, c % 4
        o = np.asarray(res.results[c]["out"]).reshape(S // 512, 128, D)
        out[b].reshape(S // 512, 4, 128, D)[:, j] = o
    return out, res


def kernel(**inputs):
    S = int(np.asarray(inputs["x"]).shape[1])
    out, _ = run(inputs, S, DEPTH)
    return out
```
